# Optimizing a Trainium2 kernel written in Bass

```python
import jax, jax.numpy as jnp
from jax import lax
import numpy as np

D_MODEL = 1024
BATCH = 32
SEQ = 256
DEPTH = 4
DEC_BATCH = 4
DEC_SEQ = 2048
PAST_LEN = 512

GRID_W = 64
D_MIX = D_MODEL
D_A = D_MIX // 4
RG_BLOCKS = 4
RG_BLOCK_DIM = D_A // RG_BLOCKS
RG_C = 8.0
CONV_W = 4
CONV_PAD = (2, 1)
D_B = D_MIX // 4
HG_HEADS = 4
HG_DIM = D_B // HG_HEADS
HG_CHUNK = 16
D_C = D_MIX - D_A - D_B
NA_DIM = 64
NA_HEADS = D_C // NA_DIM
NA_SCALE = NA_DIM ** -0.5
WIN_R = 8
WIN_C = 16
Q_BLOCK = 128
D_FF = 4 * D_MODEL
SPLIT_SIZES = (D_A, D_A, D_B, D_B, D_B, D_B, D_B, D_C, D_C, D_C)
D_IN = 2 * D_A + 5 * D_B + 3 * D_C
EPS = 1e-6
F32 = jnp.float32

kernel_name = 'hybrid_rglru_hgrn2_natten_flow_step'


def rms_norm(x, g):
    xf = x.astype(F32)
    y = xf * lax.rsqrt(jnp.mean(xf * xf, axis=-1, keepdims=True) + EPS)
    return (y * g.astype(F32)).astype(x.dtype)


def ada_modulation(cond, w_mod, b_mod):
    m = (jax.nn.silu(cond) @ w_mod + b_mod)[..., None, :]
    return jnp.split(m, 6, axis=-1)


def centred_conv(x, w, b):
    y = lax.conv_general_dilated(x, w[:, None, :].astype(x.dtype), window_strides=(1,),
                                 padding=[CONV_PAD], dimension_numbers=('NWC', 'WIO', 'NWC'),
                                 feature_group_count=x.shape[-1])
    return y + b


def linear_scan(a, b, h0):
    def combine(left, right):
        al, bl = left
        ar, br = right
        return al * ar, ar * bl + br
    a_cum, b_cum = lax.associative_scan(combine, (a, b), axis=1)
    return b_cum + a_cum * h0[:, None, :]


def rglru_direction(xc, w_a, b_a, w_x, b_x, lam, h0):
    bsz, t_len, ch = xc.shape
    xb = xc.reshape(bsz, t_len, RG_BLOCKS, RG_BLOCK_DIM)
    r = jax.nn.sigmoid(jnp.einsum('btnd,nde->btne', xb, w_a).reshape(bsz, t_len, ch) + b_a)
    i = jax.nn.sigmoid(jnp.einsum('btnd,nde->btne', xb, w_x).reshape(bsz, t_len, ch) + b_x)
    log_a = -RG_C * r.astype(F32) * jax.nn.softplus(-lam.astype(F32))
    b = jnp.sqrt(-jnp.expm1(2.0 * log_a)) * (i * xc).astype(F32)
    return linear_scan(jnp.exp(log_a), b, h0)


def rglru_bidir(xc, w_a, b_a, w_x, b_x, lam, h0):
    h_f = rglru_direction(xc, w_a[0], b_a[0], w_x[0], b_x[0], lam[0], h0[:, 0])
    h_b = rglru_direction(xc[:, ::-1], w_a[1], b_a[1], w_x[1], b_x[1], lam[1], h0[:, 1])[:, ::-1]
    final = jnp.stack([h_f[:, -1], h_b[:, 0]], axis=1)
    return h_f + h_b, final


def hgrn2_chunkwise(q, k, v, logf, s0):
    bsz, t_len, nh, _ = q.shape
    n_chunks = t_len // HG_CHUNK

    def chunk(t):
        return t.reshape(bsz, n_chunks, HG_CHUNK, nh, t.shape[-1])

    q, k, v, logf = chunk(q), chunk(k), chunk(v), chunk(logf)
    cum = jnp.cumsum(logf, axis=2)
    tot = cum[:, :, -1]
    causal = jnp.tril(jnp.ones((HG_CHUNK, HG_CHUNK), dtype=bool))[None, None, :, :, None, None]
    diff = cum[:, :, :, None] - cum[:, :, None]
    decay = jnp.exp(jnp.where(causal, diff, -jnp.inf))
    scores = jnp.einsum('bnthk,bntshk,bnshk->bnhts', q, decay, k)
    o_intra = jnp.einsum('bnhts,bnshv->bnthv', scores, v)
    ds = jnp.einsum('bnchk,bnchv->bnhkv', k * jnp.exp(tot[:, :, None] - cum), v)

    def step(s, inp):
        dec, d = inp
        return dec[..., None] * s + d, s

    s_fin, s_in = lax.scan(step, s0, (jnp.moveaxis(jnp.exp(tot), 1, 0), jnp.moveaxis(ds, 1, 0)))
    s_in = jnp.moveaxis(s_in, 0, 1)
    o_inter = jnp.einsum('bnchk,bnhkv->bnchv', q * jnp.exp(cum), s_in)
    return (o_intra + o_inter).reshape(bsz, t_len, nh, v.shape[-1]), s_fin


def hgrn2_bidir(q, k_f, k_b, v, logf_f, logf_b, s0):
    o_f, s_f = hgrn2_chunkwise(q, k_f, v, logf_f, s0[:, 0])
    o_b, s_b = hgrn2_chunkwise(q[:, ::-1], k_b[:, ::-1], v[:, ::-1], logf_b[:, ::-1], s0[:, 1])
    return o_f + o_b[:, ::-1], jnp.stack([s_f, s_b], axis=1)


def dense_context_attention(q, k, v):
    bsz, l_len, nh, hd = q.shape
    qb = jnp.moveaxis(q.reshape(bsz, l_len // Q_BLOCK, Q_BLOCK, nh, hd), 1, 0)

    def block(qi):
        s = jnp.einsum('bqhd,bkhd->bhqk', qi, k).astype(F32) * NA_SCALE
        p = jax.nn.softmax(s, axis=-1).astype(v.dtype)
        return jnp.einsum('bhqk,bkhd->bqhd', p, v)

    o = lax.map(block, qb)
    return jnp.moveaxis(o, 0, 1).reshape(bsz, l_len, nh, hd)


def neighbourhood_attention(q, k, v, ctx_k, ctx_v, rpb):
    bsz, t_len, nh, hd = q.shape
    rows = t_len // GRID_W
    kr = min(WIN_R, rows)
    r = np.arange(rows)
    ridx = np.clip(r - kr // 2, 0, rows - kr)[:, None] + np.arange(kr)[None]
    col = np.arange(GRID_W)
    c0 = np.clip(col - WIN_C // 2, 0, GRID_W - WIN_C)
    col_mask = (col[None] >= c0[:, None]) & (col[None] < c0[:, None] + WIN_C)
    dy = ridx - r[:, None] + WIN_R - 1
    dx = np.clip(col[None] - col[:, None], 1 - WIN_C, WIN_C - 1) + WIN_C - 1
    bias = rpb.astype(F32)[:, dy[:, None, :, None], dx[None, :, None, :]]
    qg = q.reshape(bsz, rows, GRID_W, nh, hd)
    kg = k.reshape(bsz, rows, GRID_W, nh, hd)[:, ridx]
    vg = v.reshape(bsz, rows, GRID_W, nh, hd)[:, ridx]
    s_loc = jnp.einsum('brchd,bruwhd->bhrcuw', qg, kg).astype(F32) * NA_SCALE + bias[None]
    s_loc = jnp.where(jnp.asarray(col_mask)[None, None, None, :, None, :], s_loc, -jnp.inf)
    s_ctx = jnp.einsum('brchd,blhd->bhrcl', qg, ctx_k).astype(F32) * NA_SCALE
    n_loc = kr * GRID_W
    s_all = jnp.concatenate([s_loc.reshape(bsz, nh, rows, GRID_W, n_loc), s_ctx], axis=-1)
    probs = jax.nn.softmax(s_all, axis=-1).astype(v.dtype)
    p_loc = probs[..., :n_loc].reshape(bsz, nh, rows, GRID_W, kr, GRID_W)
    o = (jnp.einsum('bhrcuw,bruwhd->brchd', p_loc, vg)
         + jnp.einsum('bhrcl,blhd->brchd', probs[..., n_loc:], ctx_v))
    return o.reshape(bsz, t_len, nh, hd)


def trunk_layer(x, cond, p, rg_h0, hg_s0, ctx_k, ctx_v):
    is_context = ctx_k is None
    bsz, t_len, _ = x.shape
    sh1, sc1, g1, sh2, sc2, g2 = ada_modulation(cond, p['w_mod'], p['b_mod'])
    h = rms_norm(x, p['norm1']) * (1 + sc1) + sh1
    split_points = np.cumsum(SPLIT_SIZES)[:-1].tolist()
    xa, ga, bq, bff, bfb, bi, bg, cq, ck, cv = jnp.split(h @ p['w_in'], split_points, axis=-1)
    if is_context:
        rg_h0 = jnp.zeros((bsz, 2, D_A), F32)
        hg_s0 = jnp.zeros((bsz, 2, HG_HEADS, HG_DIM, HG_DIM), F32)

    xc = centred_conv(xa, p['rg_conv_w'], p['rg_conv_b'])
    ya, rg_fin = rglru_bidir(xc, p['rg_w_a'], p['rg_b_a'], p['rg_w_x'], p['rg_b_x'], p['rg_lambda'],
                             rg_h0.astype(F32))
    out_a = jax.nn.gelu(ga) * ya.astype(x.dtype)

    def heads_b(t):
        return t.reshape(bsz, t_len, HG_HEADS, HG_DIM)

    lb = p['hg_lb']
    f_f = lb[0] + (1 - lb[0]) * jax.nn.sigmoid(bff.astype(F32))
    f_b = lb[1] + (1 - lb[1]) * jax.nn.sigmoid(bfb.astype(F32))
    ob, hg_fin = hgrn2_bidir(heads_b(jax.nn.silu(bq).astype(F32)), heads_b(1 - f_f), heads_b(1 - f_b),
                             heads_b(bi.astype(F32)), heads_b(jnp.log(f_f)), heads_b(jnp.log(f_b)),
                             hg_s0.astype(F32))
    out_b = (rms_norm(ob, p['hg_norm'].reshape(HG_HEADS, HG_DIM)).reshape(bsz, t_len, D_B).astype(x.dtype)
             * jax.nn.silu(bg))

    def heads_c(t):
        return t.reshape(bsz, t_len, NA_HEADS, NA_DIM)

    q, k, v = heads_c(cq), heads_c(ck), heads_c(cv)
    if is_context:
        out_c = dense_context_attention(q, k, v)
    else:
        out_c = neighbourhood_attention(q, k, v, ctx_k, ctx_v, p['na_rpb'])

    mix = jnp.concatenate([out_a, out_b, out_c.reshape(bsz, t_len, D_C)], axis=-1) @ p['w_out']
    x = x + g1 * mix
    h2 = rms_norm(x, p['norm2']) * (1 + sc2) + sh2
    x = x + g2 * (jnp.square(jax.nn.relu(h2 @ p['w1'])) @ p['w2'])
    if is_context:
        return x, (k, v, rg_fin.astype(x.dtype), hg_fin.astype(x.dtype))
    return x, None


def setup_inputs(seed: int = 0) -> dict:
    key = jax.random.key(seed)
    ks = jax.random.split(key, 32)

    def nrm(k, shape, s):
        return jax.random.normal(k, shape, jnp.float32) * s

    x_prompt = nrm(ks[0], (BATCH, SEQ, D_MODEL), 1.0)
    x_sample = nrm(ks[1], (DEC_BATCH, DEC_SEQ, D_MODEL), 1.0)
    cache_k = nrm(ks[2], (DEC_BATCH, DEPTH, PAST_LEN, NA_HEADS, NA_DIM), 1.0)
    cache_v = nrm(ks[3], (DEC_BATCH, DEPTH, PAST_LEN, NA_HEADS, NA_DIM), 1.0)
    state_rglru = nrm(ks[4], (DEC_BATCH, DEPTH, 2, D_A), 1.0)
    state_hgrn = nrm(ks[5], (DEC_BATCH, DEPTH, 2, HG_HEADS, HG_DIM, HG_DIM), 0.5)
    c = nrm(ks[6], (DEC_BATCH, D_MODEL), 1.0)
    c_ctx = nrm(ks[7], (D_MODEL,), 1.0)
    w_mod = nrm(ks[8], (DEPTH, D_MODEL, 6 * D_MODEL), 0.5 * D_MODEL ** -0.5)
    b_mod = nrm(ks[9], (DEPTH, 6 * D_MODEL), 0.01)
    norm1 = 1.0 + nrm(ks[10], (DEPTH, D_MODEL), 0.02)
    norm2 = 1.0 + nrm(ks[11], (DEPTH, D_MODEL), 0.02)
    w_in = nrm(ks[12], (DEPTH, D_MODEL, D_IN), D_MODEL ** -0.5)
    rg_conv_w = nrm(ks[13], (DEPTH, CONV_W, D_A), CONV_W ** -0.5)
    rg_conv_b = nrm(ks[14], (DEPTH, D_A), 0.01)
    rg_w_a = nrm(ks[15], (DEPTH, 2, RG_BLOCKS, RG_BLOCK_DIM, RG_BLOCK_DIM), RG_BLOCK_DIM ** -0.5)
    rg_b_a = nrm(ks[16], (DEPTH, 2, D_A), 0.01)
    rg_w_x = nrm(ks[17], (DEPTH, 2, RG_BLOCKS, RG_BLOCK_DIM, RG_BLOCK_DIM), RG_BLOCK_DIM ** -0.5)
    rg_b_x = nrm(ks[18], (DEPTH, 2, D_A), 0.01)
    a0 = jax.random.uniform(ks[19], (DEPTH, 2, D_A), jnp.float32, minval=0.9, maxval=0.999)
    s = a0 ** (1.0 / RG_C)
    rg_lambda = jnp.log(s) - jnp.log1p(-s)
    hg_lb = nrm(ks[20], (DEPTH, 2, D_B), 1.0)
    hg_norm = 1.0 + nrm(ks[21], (DEPTH, D_B), 0.02)
    na_rpb = nrm(ks[22], (DEPTH, NA_HEADS, 2 * WIN_R - 1, 2 * WIN_C - 1), 0.1)
    w_out = nrm(ks[23], (DEPTH, D_MIX, D_MODEL), D_MIX ** -0.5)
    w1 = nrm(ks[24], (DEPTH, D_MODEL, D_FF), D_MODEL ** -0.5)
    w2 = nrm(ks[25], (DEPTH, D_FF, D_MODEL), D_FF ** -0.5)
    norm_f = 1.0 + nrm(ks[26], (D_MODEL,), 0.02)
    return {'x_prompt': x_prompt, 'x_sample': x_sample, 'cache_k': cache_k, 'cache_v': cache_v,
            'state_rglru': state_rglru, 'state_hgrn': state_hgrn, 'c': c, 'c_ctx': c_ctx,
            'w_mod': w_mod, 'b_mod': b_mod, 'norm1': norm1, 'norm2': norm2, 'w_in': w_in,
            'rg_conv_w': rg_conv_w, 'rg_conv_b': rg_conv_b, 'rg_w_a': rg_w_a, 'rg_b_a': rg_b_a,
            'rg_w_x': rg_w_x, 'rg_b_x': rg_b_x, 'rg_lambda': rg_lambda, 'hg_lb': hg_lb,
            'hg_norm': hg_norm, 'na_rpb': na_rpb, 'w_out': w_out, 'w1': w1, 'w2': w2, 'norm_f': norm_f}


def reference(x_prompt, x_sample, cache_k, cache_v, state_rglru, state_hgrn, c, c_ctx,
              w_mod, b_mod, norm1, norm2, w_in, rg_conv_w, rg_conv_b, rg_w_a, rg_b_a, rg_w_x, rg_b_x,
              rg_lambda, hg_lb, hg_norm, na_rpb, w_out, w1, w2, norm_f):
    lb_w = jax.nn.softmax(hg_lb.astype(F32), axis=0)
    hg_lower = jnp.cumsum(lb_w, axis=0) - lb_w[0]
    xp, xs = x_prompt, x_sample
    new_k, new_v, new_rg, new_hg = [], [], [], []
    for l in range(DEPTH):
        p = {'w_mod': w_mod[l], 'b_mod': b_mod[l], 'norm1': norm1[l], 'norm2': norm2[l],
             'w_in': w_in[l], 'rg_conv_w': rg_conv_w[l], 'rg_conv_b': rg_conv_b[l],
             'rg_w_a': rg_w_a[l], 'rg_b_a': rg_b_a[l], 'rg_w_x': rg_w_x[l], 'rg_b_x': rg_b_x[l],
             'rg_lambda': rg_lambda[l], 'hg_lb': hg_lower[l], 'hg_norm': hg_norm[l],
             'na_rpb': na_rpb[l], 'w_out': w_out[l], 'w1': w1[l], 'w2': w2[l]}
        xp, (k_l, v_l, rg_l, hg_l) = trunk_layer(xp, c_ctx, p, None, None, None, None)
        new_k.append(k_l)
        new_v.append(v_l)
        new_rg.append(rg_l)
        new_hg.append(hg_l)
        xs, _ = trunk_layer(xs, c, p, state_rglru[:, l], state_hgrn[:, l], cache_k[:, l], cache_v[:, l])
    y_prompt = rms_norm(xp, norm_f)
    y_sample = rms_norm(xs, norm_f)
    return (y_prompt, y_sample, jnp.stack(new_k, axis=1), jnp.stack(new_v, axis=1),
            jnp.stack(new_rg, axis=1), jnp.stack(new_hg, axis=1))
```

```python
import numpy as np
import concourse.bass as bass
import concourse.mybir as mybir
from concourse.bass_utils import run_bass_kernel_spmd

F32 = mybir.dt.float32
BF16 = mybir.dt.bfloat16
AF = mybir.ActivationFunctionType
ALU = mybir.AluOpType

NL = 4
T = 2048
NEG = -30000.0


class Buf:
    __slots__ = ("name", "w", "r", "excl")

    def __init__(self, name, excl=False):
        self.name = name
        self.w = None
        self.r = []
        self.excl = excl


class Op:
    __slots__ = ("eng", "fn", "deps", "kind", "sig", "val", "sem", "waits")


class Sched:
    ENG = ("pe", "act", "dve", "pool", "sp")

    def __init__(self, nc, n_dma_sems=24):
        self.nc = nc
        self.ops = []
        self.n_dma_sems = n_dma_sems

    def _add(self, eng, fn, reads, writes, kind):
        op = Op()
        op.eng, op.fn, op.kind = eng, fn, kind
        writes = list(writes) + [b for b in reads if b.excl]
        reads = [b for b in reads if not b.excl]
        deps = []
        for b in reads:
            if b.w is not None:
                deps.append(b.w)
        for b in writes:
            if b.w is not None:
                deps.append(b.w)
            deps.extend(b.r)
        op.deps = deps
        op.sig = False
        op.val = 0
        op.sem = None
        self.ops.append(op)
        for b in reads:
            b.r.append(op)
        for b in writes:
            b.w = op
            b.r = []
        return op

    def op(self, eng, fn, reads=(), writes=()):
        return self._add(eng, fn, reads, writes, "c")

    def dma(self, eng, out, in_, reads=(), writes=()):
        return self._add(eng, lambda e: e.dma_start(out=out, in_=in_), reads, writes, "d")

    def finalize(self, final_bufs):
        nc = self.nc
        self._add("sp", None, [], list(final_bufs), "c")
        needed = set()
        for op in self.ops:
            for d in op.deps:
                if d.kind == "c" and not (d.eng == op.eng and op.eng == "pe"):
                    needed.add(id(d))
        csem = {e: nc.alloc_semaphore("c_" + e) for e in self.ENG}
        dsem = [nc.alloc_semaphore("d_%d" % i) for i in range(self.n_dma_sems)]
        dval = [0] * self.n_dma_sems
        dnext = 0
        dnext_sw = 0
        cnt = {e: 0 for e in self.ENG}
        known = {e: {} for e in self.ENG}
        streams = {e: [] for e in self.ENG}
        for op in self.ops:
            w = {}
            kn = known[op.eng]
            for d in op.deps:
                if d.kind == "c":
                    if d.eng == op.eng and op.eng == "pe":
                        continue
                    key = ("c", d.eng)
                else:
                    key = ("d", d.sem)
                if kn.get(key, 0) >= d.val:
                    continue
                if w.get(key, 0) < d.val:
                    w[key] = d.val
            if op.kind == "d":
                half = self.n_dma_sems // 2
                if op.eng == "pool":
                    s = half + dnext_sw
                    dnext_sw = (dnext_sw + 1) % half
                else:
                    s = dnext
                    dnext = (dnext + 1) % half
                if dval[s] > 0 and kn.get(("d", s), 0) < dval[s]:
                    if w.get(("d", s), 0) < dval[s]:
                        w[("d", s)] = dval[s]
                dval[s] += 16
                op.sem = s
                op.val = dval[s]
            else:
                if id(op) in needed:
                    cnt[op.eng] += 1
                    op.sig = True
                op.val = cnt[op.eng]
            for k, v in w.items():
                kn[k] = v
            op.waits = w
            streams[op.eng].append(op)

        def replay(name, e):
            for op in streams[name]:
                for (kind, k), v in op.waits.items():
                    e.wait_ge(csem[k] if kind == "c" else dsem[k], v)
                if op.fn is None:
                    continue
                ins = op.fn(e)
                if op.kind == "d":
                    ins.then_inc(dsem[op.sem], 16)
                elif op.sig:
                    ins.then_inc(csem[name], 1)

        with nc.Block() as block:
            @block.tensor
            def _(e):
                replay("pe", e)

            @block.scalar
            def _(e):
                replay("act", e)

            @block.vector
            def _(e):
                replay("dve", e)

            @block.gpsimd
            def _(e):
                replay("pool", e)

            @block.sync
            def _(e):
                replay("sp", e)


def _vec_layout():
    off = {}
    n = 0
    for name, sz in (("b_mod", NL * 48), ("norm1", NL * 8), ("norm2", NL * 8), ("norm_f", 8),
                     ("conv_w", NL * 2 * 4), ("conv_b", NL * 2), ("rg_b_a", NL * 4), ("rg_b_x", NL * 4),
                     ("rg_lam", NL * 4), ("hg_lb", 16), ("hg_norm", NL * 2), ("rg_h0", NL * 4),
                     ("cond", 8), ("flags", 4)):
        off[name] = n
        n += sz
    return off, n


VOFF, NVEC = _vec_layout()
C_ID, C_ONES, C_BLK, C_MF, C_MB, C_CM, C_HF, C_HB, NCONST = 0, 128, 256, 384, 448, 512, 2560, 2624, 2688
PIECES = {"A": (0, 512), "B0": (512, 512), "B1": (1024, 512), "BG": (1536, 256),
          "CQ": (1792, 512), "CK": (2304, 512), "CV": (2816, 512)}


def _win_perm():
    xa, ga, bq, bff, bfb, bi, bg, cq, ck, cv = 0, 256, 512, 768, 1024, 1280, 1536, 1792, 2304, 2816
    cols = list(range(0, 512))
    for cc in range(2):
        for base in (bq, bff, bfb, bi):
            cols += list(range(base + cc * 128, base + cc * 128 + 128))
    cols += list(range(bg, bg + 256))
    cols += list(range(cq, cq + 512)) + list(range(ck, ck + 512)) + list(range(cv, cv + 512))
    return np.array(cols)


def _attn_pairs():
    pairs = []
    for qg in range(8):
        lst = []
        for rel in range(-2, 4):
            kb = 2 * qg + rel
            if 0 <= kb < 16:
                lst.append((kb, rel + 2))
        pairs.append(lst)
    return pairs


def _mask_class(qg):
    return 0 if qg == 0 else (2 if qg == 7 else 1)


def build_program(nl_run=NL, nlw=NL, stage=None):
    nc = bass.Bass("TRN2", target_bir_lowering=False)
    S = Sched(nc)

    def din(name, shape):
        return nc.dram_tensor(name, list(shape), F32, kind="ExternalInput").ap()

    def dout(name, shape):
        return nc.dram_tensor(name, list(shape), F32, kind="ExternalOutput").ap()

    xT_d = din("xT", [1024, T])
    vecs_d = din("vecs", [128, NVEC])
    consts_d = din("consts", [128, NCONST])
    rgw_d = din("rgw", [NL, 8, 128, 128])
    hgs0_d = din("hgs0", [NL, 4, 128, 64])
    ctxk_d = din("ctxk", [NL, 512, 512])
    ctxv_d = din("ctxv", [NL, 512, 512])
    biasT_d = din("biasT", [nlw, 8, 128, 1536])
    maskT_d = din("maskT", [128, 3 * 1536])
    wmod_d = din("w_mod", [nlw, 1024, 6144])
    win_d = din("w_in", [nlw, 1024, 3328])
    wout_d = din("w_out", [nlw, 1024, 1024])
    w1_d = din("w1", [nlw, 1024, 4096])
    w2_d = din("w2", [nlw, 4096, 1024])

    yT_d = dout("yT", [1024, T])
    kT_d = dout("kT", [NL, 512, T])
    vo_d = dout("vo", [NL, T, 512])
    rgo_d = dout("rgo", [128, NL * 4 * 8])
    hgo_d = dout("hgo", [NL, 4, 128, 8 * 64])
    mix_d = nc.dram_tensor("mix_scratch", [1024, T], BF16).ap()
    mixB = [Buf("mixd%d" % i) for i in range(16)]
    den_d = nc.dram_tensor("den_scratch", [8, T], F32).ap()
    denB = Buf("den")

    def sb(name, shape, dt=F32):
        return nc.alloc_sbuf_tensor("sb_" + name, list(shape), dt)

    x_sb = sb("x", [128, 8, T])
    xB = [[Buf("x%d_%d" % (j, n)) for n in range(4)] for j in range(8)]
    h_sb = sb("h", [128, 8, T], BF16)
    hB = [Buf("h%d" % n) for n in range(4)]
    SW = 2056
    Sx = [sb("S%d" % i, [128, SW]) for i in range(8)]
    SB = [Buf("S%d" % i) for i in range(8)]
    NSLOT = 3
    wr = [sb("wr%d" % i, [128, 4096], BF16) for i in range(NSLOT)]
    wrB = [Buf("wr%d" % i) for i in range(NSLOT)]
    stg = [sb("stg%d" % i, [128, 512]) for i in range(4)]
    stgB = [Buf("stg%d" % i) for i in range(4)]
    vecs = sb("vecs", [128, NVEC])
    vecsB = Buf("vecs")
    cst = sb("cst", [128, NCONST], BF16)
    cstB = Buf("cst")
    rgw_sb = sb("rgw", [128, 8, 128], BF16)
    rgwB = Buf("rgw")
    der = sb("der", [128, 512])
    derB = Buf("der")
    modv = sb("modv", [128, NL * 48])
    modB = Buf("modv")
    small = sb("small", [128, 256])
    smallB = Buf("small")
    rgst = sb("rgst", [128, NL * 4 * 8])
    rgstB = Buf("rgst")

    banks = [nc.alloc_psum_tensor("bank%d" % i, [128, 512], F32) for i in range(8)]
    bankB = [Buf("bank%d" % i, excl=True) for i in range(8)]
    bstate = {"i": 0, "stg": 0, "slot": 0, "ev": 0}

    def bank():
        i = bstate["i"]
        bstate["i"] = (i + 1) % 7
        return banks[i], bankB[i]

    def staging():
        i = bstate["stg"]
        bstate["stg"] = (i + 1) % 4
        return stg[i], stgB[i]

    def slot():
        i = bstate["slot"]
        bstate["slot"] = (i + 1) % NSLOT
        return wr[i], wrB[i]

    def evac_eng():
        bstate["ev"] ^= 1
        return "act" if bstate["ev"] else "dve"

    def mm(out, lhsT, rhs, start, stop, r, w, **kw):
        S.op("pe", lambda e: e.matmul(out, lhsT, rhs, start=start, stop=stop, **kw), r, w)

    def act(out, in_, func, r, w, bias=None, scale=None):
        kw = {}
        if bias is not None:
            kw["bias"] = bias
        if scale is not None:
            kw["scale"] = scale
        S.op("act", lambda e: e.activation(out=out, in_=in_, func=func, **kw), r, w)

    def tt(eng, out, a, b, op, r, w):
        S.op(eng, lambda e: e.tensor_tensor(out=out, in0=a, in1=b, op=op), r, w)

    def ts(eng, out, a, s1, s2, op0, op1, r, w):
        if op1 is None:
            S.op(eng, lambda e: e.tensor_scalar(out=out, in0=a, scalar1=s1, scalar2=None, op0=op0), r, w)
        else:
            S.op(eng, lambda e: e.tensor_scalar(out=out, in0=a, scalar1=s1, scalar2=s2, op0=op0, op1=op1), r, w)

    def stt(out, a, sc, b, op0, op1, r, w):
        S.op("dve", lambda e: e.scalar_tensor_tensor(out=out, in0=a, scalar=sc, in1=b, op0=op0, op1=op1), r, w)

    def cp(eng, out, in_, r, w):
        if eng == "act":
            S.op("act", lambda e: e.activation(out=out, in_=in_, func=AF.Copy), r, w)
        else:
            S.op(eng, lambda e: e.tensor_copy(out=out, in_=in_), r, w)

    def scan(out, d0, d1, init, r, w):
        S.op("dve", lambda e: e.tensor_tensor_scan(out=out, data0=d0, data1=d1, initial=init,
                                                   op0=ALU.mult, op1=ALU.add), r, w)

    def V(name, i=0):
        o = VOFF[name] + i
        return vecs[:, o:o + 1]

    def wload(src_ap, view_out):
        t, b = slot()
        S.dma("pool", out=view_out(t), in_=src_ap, reads=(), writes=(b,))
        return t, b

    S.dma("sp", out=vecs[:], in_=vecs_d, writes=(vecsB,))
    for c0 in range(0, NCONST, 1344):
        S.dma("pool", out=cst[:, c0:c0 + 1344], in_=consts_d[:, c0:c0 + 1344], writes=(cstB,))
    for j in range(8):
        for n in range(4):
            S.dma("sp", out=x_sb[:, j, n * 512:(n + 1) * 512], in_=xT_d[j * 128:(j + 1) * 128, n * 512:(n + 1) * 512],
                  writes=(xB[j][n],))
    ident = cst[:, C_ID:C_ID + 128]
    onesD = cst[:, C_ONES:C_ONES + 128]
    blk64 = cst[:, C_BLK:C_BLK + 128]
    maskf = cst[:, C_MF:C_MF + 64]
    maskb = cst[:, C_MB:C_MB + 64]
    cmask = cst[:, C_CM:C_CM + T]
    hmask = [cst[:, C_HF:C_HF + 64], cst[:, C_HB:C_HB + 64]]
    isP = V("flags", 0)
    notP = V("flags", 1)
    ctxneg = V("flags", 2)

    D_SCOND = 0
    D_C8N = 16
    D_C8N2 = 32
    D_HGL = 48
    D_OML = 64
    D_NW = 80
    D_GS1 = 112
    D_GS2 = 144
    D_TMP = 200
    scond = sb("scond", [128, 8], BF16)
    scondB = Buf("scond")
    act(scond[:], vecs[:, VOFF["cond"]:VOFF["cond"] + 8], AF.Silu, (vecsB,), (scondB,))
    lam = vecs[:, VOFF["rg_lam"]:VOFF["rg_lam"] + 16]
    act(der[:, D_TMP:D_TMP + 16], lam, AF.Exp, (vecsB,), (derB,), scale=-1.0)
    act(der[:, D_TMP + 16:D_TMP + 32], der[:, D_TMP:D_TMP + 16], AF.Ln, (derB,), (derB,), bias=1.0)
    ts("dve", der[:, D_C8N:D_C8N + 16], der[:, D_TMP + 16:D_TMP + 32], -8.0, None, ALU.mult, None, (derB,), (derB,))
    ts("dve", der[:, D_C8N2:D_C8N2 + 16], der[:, D_TMP + 16:D_TMP + 32], -16.0, None, ALU.mult, None, (derB,), (derB,))
    hraw = vecs[:, VOFF["hg_lb"]:VOFF["hg_lb"] + 16].rearrange("p (g l) -> p g l", l=4)
    S.op("dve", lambda e: e.tensor_reduce(out=der[:, D_TMP + 32:D_TMP + 36], in_=hraw, axis=mybir.AxisListType.X,
                                          op=ALU.max, negate=True), (vecsB,), (derB,))
    for g in range(4):
        act(der[:, D_TMP + 40 + g * 4:D_TMP + 44 + g * 4], vecs[:, VOFF["hg_lb"] + g * 4:VOFF["hg_lb"] + g * 4 + 4],
            AF.Exp, (vecsB, derB), (derB,), bias=der[:, D_TMP + 32 + g:D_TMP + 33 + g])
    ew = der[:, D_TMP + 40:D_TMP + 56].rearrange("p (g l) -> p g l", l=4)
    S.op("dve", lambda e: e.tensor_reduce(out=der[:, D_TMP + 56:D_TMP + 60], in_=ew, axis=mybir.AxisListType.X,
                                          op=ALU.add), (derB,), (derB,))
    S.op("dve", lambda e: e.reciprocal(out=der[:, D_TMP + 60:D_TMP + 64], in_=der[:, D_TMP + 56:D_TMP + 60]), (derB,), (derB,))
    for g in range(4):
        ts("dve", der[:, D_TMP + 40 + g * 4:D_TMP + 44 + g * 4], der[:, D_TMP + 40 + g * 4:D_TMP + 44 + g * 4],
           der[:, D_TMP + 60 + g:D_TMP + 61 + g], None, ALU.mult, None, (derB,), (derB,))
    hgl = der[:, D_HGL:D_HGL + 16].rearrange("p (g l) -> p g l", l=4)
    S.op("dve", lambda e: e.memset(der[:, D_HGL:D_HGL + 16], 0.0), (), (derB,))
    for l in range(1, 4):
        tt("dve", hgl[:, :, l:l + 1], hgl[:, :, l - 1:l], ew[:, :, l:l + 1], ALU.add, (derB,), (derB,))
    ts("dve", der[:, D_OML:D_OML + 16], der[:, D_HGL:D_HGL + 16], -1.0, 1.0, ALU.mult, ALU.add, (derB,), (derB,))
    ts("dve", der[:, D_NW:D_NW + 32], vecs[:, VOFF["conv_w"]:VOFF["conv_w"] + 32], isP, -1.0, ALU.mult, ALU.mult,
       (vecsB,), (derB,))

    mod_slots = {}

    def mod_load(l, g):
        wt, wb = wload(wmod_d[l, :, g * 512:(g + 1) * 512].rearrange("(kc p) n -> p kc n", p=128),
                       lambda t: t[:].rearrange("p (kc n) -> p kc n", kc=8))
        mod_slots[(l, g)] = (wt[:].rearrange("p (kc n) -> p kc n", kc=8), wb)

    def mod_mm(l, g):
        wv, wb = mod_slots.pop((l, g))
        pb, pbB = banks[7], bankB[7]
        for mc in range(4):
            gi = g * 4 + mc
            for kc in range(8):
                mm(pb[:, gi:gi + 1], wv[:, kc, mc * 128:(mc + 1) * 128], scond[:, kc:kc + 1],
                   kc == 0, kc == 7, (wb, scondB), (pbB,))

    def mod_finish(l):
        pb, pbB = banks[7], bankB[7]
        tt("dve", modv[:, l * 48:(l + 1) * 48], pb[:, 0:48], vecs[:, VOFF["b_mod"] + l * 48:VOFF["b_mod"] + (l + 1) * 48],
           ALU.add, (pbB, vecsB), (modB,))
        stt(der[:, D_GS1 + l * 8:D_GS1 + l * 8 + 8], modv[:, l * 48 + 8:l * 48 + 16], 1.0,
            vecs[:, VOFF["norm1"] + l * 8:VOFF["norm1"] + l * 8 + 8], ALU.add, ALU.mult, (modB, vecsB), (derB,))
        stt(der[:, D_GS2 + l * 8:D_GS2 + l * 8 + 8], modv[:, l * 48 + 32:l * 48 + 40], 1.0,
            vecs[:, VOFF["norm2"] + l * 8:VOFF["norm2"] + l * 8 + 8], ALU.add, ALU.mult, (modB, vecsB), (derB,))

    if nl_run > 0:
        mod_load(0, 0)
        mod_load(0, 1)
        for g in range(12):
            mod_mm(0, g)
            if g + 2 < 12:
                mod_load(0, g + 2)
        mod_finish(0)
    bstate["i"] = 0

    def MOD(l, m, j):
        o = l * 48 + m * 8 + j
        return modv[:, o:o + 1]

    S.op("pool", lambda e: e.memset(rgst[:], 0.0), (), (rgstB,))
    S.op("pool", lambda e: e.memset(Sx[0][:, 0:2], 0.0), (), (SB[0],))
    S.op("pool", lambda e: e.memset(Sx[0][:, 2050:SW], 0.0), (), (SB[0],))

    def norm_mod(gs_off, sh_of, l, tiles=(0, 1, 2, 3)):
        sq = Sx[7][:].bitcast(BF16)
        for n in tiles:
            tsl = slice(n * 512, (n + 1) * 512)
            pb, pbB = bank()
            for j in range(8):
                act(sq[:, j * 512:(j + 1) * 512] if False else sq[:, (j % 8) * 512:(j % 8) * 512 + 512],
                    x_sb[:, j, tsl], AF.Square, (xB[j][n],), (SB[7],))
                mm(pb[:], onesD, sq[:, (j % 8) * 512:(j % 8) * 512 + 512], j == 0, j == 7, (SB[7], cstB), (pbB,))
            rs = Sx[6][:, 0:512]
            act(rs, pb[:], AF.Sqrt, (pbB,), (SB[6],), bias=1e-6)
            S.op("dve", lambda e, rs=rs: e.reciprocal(out=rs, in_=rs), (SB[6],), (SB[6],))
            for j in range(8):
                tmp = Sx[6][:, 512 + (j % 2) * 512:1024 + (j % 2) * 512]
                stt(tmp, x_sb[:, j, tsl], der[:, gs_off + l * 8 + j:gs_off + l * 8 + j + 1], rs, ALU.mult, ALU.mult,
                    (xB[j][n], derB, SB[6]), (SB[6],))
                act(h_sb[:, j, tsl], tmp, AF.Identity, (SB[6], modB), (hB[n],), bias=sh_of(j))

    def proj_fm(wv, wb, c0, n):
        pb, pbB = bank()
        for kc in range(8):
            mm(pb[:], wv[:, kc, c0:c0 + 128], h_sb[:, kc, n * 512:(n + 1) * 512], kc == 0, kc == 7, (wb, hB[n]), (pbB,))
        return pb, pbB

    piece_cache = {}

    def prefetch(l, name):
        if (l, name) not in piece_cache:
            piece_cache[(l, name)] = load_piece_raw(l, name)

    def load_piece(l, name):
        if (l, name) in piece_cache:
            return piece_cache.pop((l, name))
        return load_piece_raw(l, name)

    def load_piece_raw(l, name):
        c0, ncol = PIECES[name]
        wt, wb = wload(win_d[l, :, c0:c0 + ncol].rearrange("(kc p) n -> p kc n", p=128),
                       lambda t: t[:, 0:8 * ncol].rearrange("p (kc n) -> p kc n", kc=8))
        return wt[:, 0:8 * ncol].rearrange("p (kc n) -> p kc n", kc=8), wb

    pairs = _attn_pairs()

    class _Stop(Exception):
        pass

    cur = {"l": 0}

    def chk(name):
        if stage == name or stage == "%d:%s" % (cur["l"], name):
            raise _Stop()

    fin_state = {"done": False}

    def final_norm(tiles):
        sq = Sx[7][:].bitcast(BF16)
        for n in tiles:
            tsl = slice(n * 512, (n + 1) * 512)
            pb, pbB = bank()
            for j in range(8):
                act(sq[:, j * 512:(j + 1) * 512], x_sb[:, j, tsl], AF.Square, (xB[j][n],), (SB[7],))
                mm(pb[:], onesD, sq[:, j * 512:(j + 1) * 512], j == 0, j == 7, (SB[7], cstB), (pbB,))
            rs = Sx[6][:, 0:512]
            act(rs, pb[:], AF.Sqrt, (pbB,), (SB[6],), bias=1e-6)
            S.op("dve", lambda e, rs=rs: e.reciprocal(out=rs, in_=rs), (SB[6],), (SB[6],))
            for j in range(8):
                st_, stB_ = staging()
                stt(st_[:], x_sb[:, j, tsl], V("norm_f", j), rs, ALU.mult, ALU.mult, (xB[j][n], vecsB, SB[6]), (stB_,))
                S.dma("sp", out=yT_d[j * 128:(j + 1) * 128, tsl], in_=st_[:], reads=(stB_,))

    def layer_body(l):
        cur["l"] = l
        for _once in (0,):
            if l == 0:
                norm_mod(D_GS1, lambda j: MOD(l, 0, j), l)
            S.dma("pool", out=rgw_sb[:], in_=rgw_d[l].rearrange("m p n -> p m n"), writes=(rgwB,))

            if stage == 'norm1':
                break
            wA, wAb = load_piece(l, "A")
            for cc in range(2):
                xa_pad = Sx[0]
                xc = Sx[1][:, 0:T]
                gga = Sx[7][:].bitcast(BF16)[:, 0:T]
                xcb = Sx[7][:].bitcast(BF16)[:, T:2 * T]
                S.op("pool", lambda e: e.memset(Sx[0][:, 0:2], 0.0), (), (SB[0],))
                S.op("pool", lambda e: e.memset(Sx[0][:, 2050:SW], 0.0), (), (SB[0],))
                for n in range(4):
                    pb, pbB = proj_fm(wA, wAb, cc * 128, n)
                    cp(evac_eng(), xa_pad[:, 2 + n * 512:2 + (n + 1) * 512], pb[:], (pbB,), (SB[0],))
                    pb, pbB = proj_fm(wA, wAb, 256 + cc * 128, n)
                    act(gga[:, n * 512:(n + 1) * 512], pb[:], AF.Gelu, (pbB,), (SB[7],))
                prefetch(l, "B0" if cc == 0 else "BG")
                cw = lambda k: vecs[:, VOFF["conv_w"] + (l * 2 + cc) * 4 + k:VOFF["conv_w"] + (l * 2 + cc) * 4 + k + 1]
                nw = lambda k: der[:, D_NW + (l * 2 + cc) * 4 + k:D_NW + (l * 2 + cc) * 4 + k + 1]
                cb = vecs[:, VOFF["conv_b"] + l * 2 + cc:VOFF["conv_b"] + l * 2 + cc + 1]
                ts("dve", xc, xa_pad[:, 0:T], cw(0), cb, ALU.mult, ALU.add, (SB[0], vecsB), (SB[1],))
                for k in range(1, 4):
                    stt(xc, xa_pad[:, k:k + T], cw(k), xc, ALU.mult, ALU.add, (SB[0], SB[1], vecsB), (SB[1],))
                Xv = lambda o: xa_pad[:, 258 + o:258 + o + 1792:256]
                Cv = lambda o: Sx[1][:, 256 + o:256 + o + 1792:256]
                for (co, xo, k) in ((0, -2, 0), (0, -1, 1), (1, -1, 0), (-1, 0, 3)):
                    stt(Cv(co), Xv(xo), nw(k), Cv(co), ALU.mult, ALU.add, (SB[0], SB[1], derB), (SB[1],))
                chk("A1")
                cp("act", xcb, xc, (SB[1],), (SB[7],))
                chk("A2")
                hdir = [Sx[5][:, 0:T], Sx[6][:, 0:T]]
                for dr in range(2):
                    idx = l * 4 + dr * 2 + cc
                    r_t, i_t, a_t = Sx[2][:, 0:T], Sx[3][:, 0:T], Sx[4][:, 0:T]
                    for n in range(4):
                        tsl = slice(n * 512, (n + 1) * 512)
                        pb, pbB = bank()
                        mm(pb[:], rgw_sb[:, (dr * 2 + cc) * 2 + 0, :], xcb[:, tsl], True, True, (rgwB, SB[7]), (pbB,))
                        act(r_t[:, tsl], pb[:], AF.Sigmoid, (pbB, vecsB), (SB[2],), bias=V("rg_b_a", idx))
                        pb, pbB = bank()
                        mm(pb[:], rgw_sb[:, (dr * 2 + cc) * 2 + 1, :], xcb[:, tsl], True, True, (rgwB, SB[7]), (pbB,))
                        act(i_t[:, tsl], pb[:], AF.Sigmoid, (pbB, vecsB), (SB[3],), bias=V("rg_b_x", idx))
                    act(a_t, r_t, AF.Exp, (SB[2], derB), (SB[4],), scale=der[:, D_C8N + idx:D_C8N + idx + 1])
                    act(r_t, r_t, AF.Exp, (SB[2], derB), (SB[2],), scale=der[:, D_C8N2 + idx:D_C8N2 + idx + 1])
                    act(r_t, r_t, AF.Sqrt, (SB[2],), (SB[2],), scale=-1.0, bias=1.0)
                    chk("A3")
                    tt("dve", i_t, i_t, r_t, ALU.mult, (SB[2], SB[3]), (SB[3],))
                    tt("dve", i_t, i_t, xc, ALU.mult, (SB[3], SB[1]), (SB[3],))
                    if dr == 0:
                        av = Sx[4][:, 256:256 + 1792:256]
                    else:
                        av = Sx[4][:, 255:255 + 1792:256]
                    ts("dve", av, av, notP, None, ALU.mult, None, (SB[4], vecsB), (SB[4],))
                    chk("A4")
                    h0 = V("rg_h0", idx)
                    hb_ = SB[5 + dr]
                    if dr == 0:
                        scan(hdir[0], a_t, i_t, h0, (SB[4], SB[3], vecsB), (hb_,))
                        cp("pool", rgst[:, idx * 8:idx * 8 + 8], Sx[5][:, 255:T:256], (hb_,), (rgstB,))
                    else:
                        scan(hdir[1][:, ::-1], a_t[:, ::-1], i_t[:, ::-1], h0, (SB[4], SB[3], vecsB), (hb_,))
                        cp("pool", rgst[:, idx * 8:idx * 8 + 8], Sx[6][:, 0:T:256], (hb_,), (rgstB,))
                chk("A5")
                tt("dve", hdir[0], hdir[0], hdir[1], ALU.add, (SB[5], SB[6]), (SB[5],))
                oa = Sx[2][:].bitcast(BF16)[:, 0:T]
                tt("dve", oa, hdir[0], gga, ALU.mult, (SB[5], SB[7]), (SB[2],))
                chk("A6")
                S.dma("sp", out=mix_d[cc * 128:(cc + 1) * 128, :], in_=oa, reads=(SB[2],), writes=(mixB[cc * 2], mixB[cc * 2 + 1]))
                chk("A7")

            if stage == 'A':
                break
            wGv, wGb = None, None
            for cc in range(2):
                wBv, wBb = load_piece(l, "B%d" % cc)
                q_t = Sx[0][:, 0:T]
                sg = [Sx[1][:, 0:T], Sx[2][:, 0:T]]
                o_acc = Sx[3][:, 0:T]
                S4b = Sx[4][:].bitcast(BF16)
                AT2 = S4b[:, 0:T].rearrange("p (h j t) -> p h j t", h=2, t=64)
                Qd = S4b[:, T:2 * T]
                DSf = Sx[5][:, 0:T]
                DS = DSf.rearrange("p (v n) -> p v n", n=32)
                S7b = Sx[7][:].bitcast(BF16)
                Vh = S7b[:, 0:T].rearrange("p (b c) -> p b c", c=128)
                Qi, Kt, Kd, KdT = [S7b[:, T + i * 512:T + (i + 1) * 512] for i in range(4)]
                Sbf = Sx[6][:].bitcast(BF16)[:, 0:2112].rearrange("p (m v) -> p m v", v=64)
                fq, lgq, cumq, eq = [Sx[6][:, i * 512:(i + 1) * 512] for i in range(4)]
                for n in range(4):
                    tsl = slice(n * 512, (n + 1) * 512)
                    pb, pbB = proj_fm(wBv, wBb, 0, n)
                    act(q_t[:, tsl], pb[:], AF.Silu, (pbB,), (SB[0],))
                    for dr in range(2):
                        pb, pbB = proj_fm(wBv, wBb, 128 + dr * 128, n)
                        act(sg[dr][:, tsl], pb[:], AF.Sigmoid, (pbB,), (SB[1 + dr],))
                for b4 in range(4):
                    pb, pbB = bank()
                    for bb in range(4):
                        blk = b4 * 4 + bb
                        for kc in range(8):
                            mm(pb[:, bb * 128:(bb + 1) * 128], h_sb[:, kc, blk * 128:(blk + 1) * 128], wBv[:, kc, 384:512],
                               kc == 0, kc == 7, (wBb, hB[blk // 4]), (pbB,))
                    cp(evac_eng(), Vh[:, b4 * 4:(b4 + 1) * 4, :], pb[:].rearrange("p (b c) -> p b c", c=128), (pbB,), (SB[7],))
                prefetch(l, "B1" if cc == 0 else "CQ")
                chk("B1")
                mid_t, tot_t, dec_t = small[:, 0:32], small[:, 32:64], small[:, 64:96]
                for dr in range(2):
                    gi = dr * 2 + cc
                    mk = maskf if dr == 0 else maskb
                    mid_i, tot_i = (31, 63) if dr == 0 else (32, 0)
                    sidx = (lambda n: n) if dr == 0 else (lambda n: 31 - n)
                    oml = der[:, D_OML + gi * 4 + l:D_OML + gi * 4 + l + 1]
                    hgl_ = der[:, D_HGL + gi * 4 + l:D_HGL + gi * 4 + l + 1]
                    for nq in range(4):
                        qs = slice(nq * 512, (nq + 1) * 512)
                        ts("dve", fq, sg[dr][:, qs], oml, hgl_, ALU.mult, ALU.add, (SB[1 + dr], derB), (SB[6],))
                        act(lgq, fq, AF.Ln, (SB[6],), (SB[6],))
                        act(fq, fq, AF.Identity, (SB[6],), (SB[6],), scale=-1.0, bias=1.0)
                        if dr == 0:
                            scan(cumq, cmask[:, 0:512], lgq, 0.0, (cstB, SB[6]), (SB[6],))
                        else:
                            scan(cumq[:, ::-1], cmask[:, 0:512], lgq[:, ::-1], 0.0, (cstB, SB[6]), (SB[6],))
                        c3 = cumq.rearrange("p (n c) -> p n c", c=64)
                        l3 = lgq.rearrange("p (n c) -> p n c", c=64)
                        if dr == 0:
                            ms = slice(nq * 8, nq * 8 + 8)
                            cp("dve", mid_t[:, ms], c3[:, :, mid_i], (SB[6],), (smallB,))
                            cp("dve", tot_t[:, ms], c3[:, :, tot_i], (SB[6],), (smallB,))
                            midv, totv = mid_t[:, ms], tot_t[:, ms]
                        else:
                            lo = 31 - (nq * 8 + 7)
                            cp("dve", mid_t[:, lo:lo + 8], c3[:, ::-1, mid_i], (SB[6],), (smallB,))
                            cp("dve", tot_t[:, lo:lo + 8], c3[:, ::-1, tot_i], (SB[6],), (smallB,))
                            midv, totv = mid_t[:, lo:lo + 8][:, ::-1], tot_t[:, lo:lo + 8][:, ::-1]
                        midb = midv.unsqueeze(2).to_broadcast([128, 8, 64])
                        totb = totv.unsqueeze(2).to_broadcast([128, 8, 64])
                        tt("dve", l3, c3, midb, ALU.subtract, (SB[6], smallB), (SB[6],))
                        act(eq, lgq, AF.Exp, (SB[6],), (SB[6],))
                        tt("dve", Qi, q_t[:, qs], eq, ALU.mult, (SB[0], SB[6]), (SB[7],))
                        act(eq, lgq, AF.Exp, (SB[6],), (SB[6],), scale=-1.0)
                        tt("dve", Kt, fq, eq, ALU.mult, (SB[6],), (SB[7],))
                        tt("dve", l3, c3, totb, ALU.subtract, (SB[6], smallB), (SB[6],))
                        act(eq, lgq, AF.Exp, (SB[6],), (SB[6],), scale=-1.0)
                        tt("dve", Kd, fq, eq, ALU.mult, (SB[6],), (SB[7],))
                        Kt0 = lgq.bitcast(BF16)[:, 0:512]
                        tt("dve", Kt0.rearrange("p (n c) -> p n c", c=64), Kt.rearrange("p (n c) -> p n c", c=64),
                           hmask[dr].unsqueeze(1).to_broadcast([128, 8, 64]), ALU.mult, (SB[7], cstB), (SB[6],))
                        act(eq, cumq, AF.Exp, (SB[6],), (SB[6],))
                        tt("pool", Qd[:, qs], q_t[:, qs], eq, ALU.mult, (SB[0], SB[6]), (SB[4],))
                        chk("B2")
                        pb, pbB = bank()
                        pbb = pb[:].bitcast(BF16)
                        for bb in range(4):
                            S.op("pe", lambda e, o=pbb[:, bb * 128:(bb + 1) * 128], i=Kd[:, bb * 128:(bb + 1) * 128]:
                                 e.transpose(o, i, ident), (SB[7], cstB), (pbB,))
                        cp(evac_eng(), KdT, pbb[:, 0:512], (pbB,), (SB[7],))
                        chk("B2b")
                        pbh = [bank(), bank()]
                        for c8 in range(8):
                            tp, j = (c8 % 2) * 64, c8 // 2
                            cs = slice(c8 * 64, (c8 + 1) * 64)
                            for hh in range(2):
                                ps_ = slice(hh * 64, (hh + 1) * 64)
                                pb, pbB = pbh[hh]
                                lo_k, hi_k = (Kt0, Kt) if dr == 0 else (Kt, Kt0)
                                mm(pb[tp:tp + 64, j * 64:j * 64 + 32], lo_k[ps_, cs], Qi[ps_, c8 * 64:c8 * 64 + 32], True, True,
                                   (SB[7], SB[6]), (pbB,))
                                mm(pb[tp:tp + 64, j * 64 + 32:j * 64 + 64], hi_k[ps_, cs], Qi[ps_, c8 * 64 + 32:c8 * 64 + 64], True, True,
                                   (SB[7], SB[6]), (pbB,))
                        for hh in range(2):
                            pb, pbB = pbh[hh]
                            tt("dve", AT2[:, hh, nq * 4:(nq + 1) * 4, :],
                               pb[:, 0:256].rearrange("p (j t) -> p j t", t=64),
                               mk.unsqueeze(1).to_broadcast([128, 4, 64]), ALU.mult, (pbB, cstB), (SB[4],))
                        chk("B2c")
                        pbp = [bank(), bank()]
                        for c8 in range(8):
                            par, blk = c8 % 2, c8 // 2
                            tp = par * 64
                            gblk = nq * 4 + blk
                            pb, pbB = pbp[par]
                            for hh in range(2):
                                mm(pb[hh * 64:(hh + 1) * 64, blk * 64:(blk + 1) * 64],
                                   KdT[tp:tp + 64, blk * 128 + hh * 64:blk * 128 + hh * 64 + 64],
                                   Vh[tp:tp + 64, gblk, hh * 64:(hh + 1) * 64], True, True, (SB[7],), (pbB,))
                        for par in range(2):
                            pb, pbB = pbp[par]
                            if dr == 0:
                                st0 = nq * 8 + par
                                dsv = DS[:, :, st0:st0 + 7:2]
                            else:
                                lo = 31 - nq * 8 - par - 6
                                dsv = DS[:, :, lo:lo + 7:2][:, :, ::-1]
                            cp(evac_eng(), dsv.rearrange("p v n -> p n v"), pb[:, 0:256].rearrange("p (n v) -> p n v", v=64),
                               (pbB,), (SB[5],))
                    chk("B4")
                    act(dec_t, tot_t, AF.Exp, (smallB,), (smallB,))
                    if dr == 0:
                        dv = small[:, 64 + 4:64 + 32:4]
                    else:
                        dv = small[:, 64 + 4:64 + 32:4]
                    ts("dve", dv, dv, notP, None, ALU.mult, None, (smallB, vecsB), (smallB,))
                    s0, s0B = staging()
                    S.dma("sp", out=s0[:, 0:64], in_=hgs0_d[l, gi], writes=(s0B,))
                    stt(DS[:, :, 0], s0[:, 0:64], small[:, 64:65], DS[:, :, 0], ALU.mult, ALU.add, (s0B, smallB, SB[5]), (SB[5],))
                    S.op("dve", lambda e: e.memset(small[:, 64:65], 0.0), (), (smallB,))
                    decfull = Sx[6][:, 0:T]
                    cp("dve", decfull.rearrange("p (v n) -> p v n", n=32), dec_t.unsqueeze(1).to_broadcast([128, 64, 32]),
                       (smallB,), (SB[6],))
                    scan(DSf, decfull, DSf, 0.0, (SB[6], SB[5]), (SB[5],))
                    cp("pool", Sbf[:, 0, :], s0[:, 0:64], (s0B,), (SB[6],))
                    cp("pool", Sbf[:, 1:33, :], DS.rearrange("p v n -> p n v"), (SB[5],), (SB[6],))
                    ts("pool", Sbf[:, 4:32:4, :], Sbf[:, 4:32:4, :], notP, None, ALU.mult, None, (SB[6], vecsB), (SB[6],))
                    fin = Sx[6][:, 1100:1612]
                    cp("dve", fin.rearrange("p (j v) -> p j v", v=64), DS[:, :, 3:32:4].rearrange("p v j -> p j v"), (SB[5],), (SB[6],))
                    S.dma("sp", out=hgo_d[l, gi], in_=fin, reads=(SB[6],))
                    chk("B5")
                    for nq in range(4):
                        qs = slice(nq * 512, (nq + 1) * 512)
                        pI = [bank(), bank()]
                        pN = [bank(), bank()]
                        for c8 in range(8):
                            n_ = nq * 8 + c8
                            m_ = sidx(n_)
                            par, j = c8 % 2, n_ // 2
                            tp = par * 64
                            cs = slice(n_ * 64, (n_ + 1) * 64)
                            for hh in range(2):
                                ps_ = slice(hh * 64, (hh + 1) * 64)
                                mm(pI[par][0][ps_, c8 * 64:(c8 + 1) * 64], Vh[tp:tp + 64, j, hh * 64:(hh + 1) * 64],
                                   AT2[tp:tp + 64, hh, j, :], True, True, (SB[7], SB[4]), (pI[par][1],))
                                mm(pN[hh][0][ps_, c8 * 64:(c8 + 1) * 64], Sbf[ps_, m_, :], Qd[ps_, cs], True, True,
                                   (SB[6], SB[4]), (pN[hh][1],))
                        oq = o_acc[:, qs]
                        for hh in range(2):
                            ps_ = slice(hh * 64, (hh + 1) * 64)
                            if dr == 0:
                                cp("dve" if hh else "act", oq[ps_, :], pN[hh][0][ps_, :], (pN[hh][1],), (SB[3],))
                            else:
                                tt("dve", oq[ps_, :], oq[ps_, :], pN[hh][0][ps_, :], ALU.add, (SB[3], pN[hh][1]), (SB[3],))
                        for par in range(2):
                            ov = oq.rearrange("p (a b t) -> p a b t", b=2, t=64)[:, :, par, :]
                            iv = pI[par][0][:].rearrange("p (a b t) -> p a b t", b=2, t=64)[:, :, par, :]
                            tt("dve", ov, ov, iv, ALU.add, (SB[3], pI[par][1]), (SB[3],))
                chk("B6")
                if cc == 0:
                    wGv, wGb = load_piece(l, "BG")
                for n in range(4):
                    qs = slice(n * 512, (n + 1) * 512)
                    pbg, pbgB = proj_fm(wGv, wGb, cc * 128, n)
                    sbg = Sx[6][:, 0:512]
                    act(sbg, pbg[:], AF.Silu, (pbgB,), (SB[6],))
                    sqb = S7b[:, T:T + 512]
                    act(sqb, o_acc[:, qs], AF.Square, (SB[3],), (SB[7],))
                    pm, pmB = bank()
                    mm(pm[:], blk64, sqb, True, True, (cstB, SB[7]), (pmB,))
                    rs = Sx[6][:, 512:1024]
                    act(rs, pm[:], AF.Sqrt, (pmB,), (SB[6],), bias=1e-6)
                    S.op("dve", lambda e, rs=rs: e.reciprocal(out=rs, in_=rs), (SB[6],), (SB[6],))
                    t_ = Sx[6][:, 1024:1536]
                    stt(t_, o_acc[:, qs], V("hg_norm", l * 2 + cc), rs, ALU.mult, ALU.mult, (SB[3], SB[6], vecsB), (SB[6],))
                    ob = S7b[:, T + 512 + (n % 2) * 512:T + 1024 + (n % 2) * 512]
                    tt("dve", ob, t_, sbg, ALU.mult, (SB[6],), (SB[7],))
                    S.dma("sp", out=mix_d[256 + cc * 128:384 + cc * 128, qs], in_=ob, reads=(SB[7],),
                          writes=(mixB[4 + cc * 2], mixB[5 + cc * 2]))

            if stage == 'B':
                break
            def qv(c):
                return Sx[c // 2][:].bitcast(BF16)[:, (c % 2) * T:(c % 2 + 1) * T]

            def kv(c):
                return Sx[2 + c // 2][:].bitcast(BF16)[:, (c % 2) * T:(c % 2 + 1) * T]

            def vaug(blk):
                t_ = 4 + blk // 7
                o = (blk % 7) * 528
                return Sx[t_][:].bitcast(BF16)[:, o:o + 528].rearrange("p (h d) -> p h d", d=66), SB[t_]

            S6b = Sx[6][:].bitcast(BF16)
            S7b = Sx[7][:].bitcast(BF16)
            ctxV = S6b[:, 1056:1056 + 2112].rearrange("p (b h d) -> p b h d", b=4, d=66)
            ctxK = S7b[:, 0:T].rearrange("p (c k) -> p c k", k=512)
            for t_ in (4, 5, 6):
                S.op("pool", lambda e, t_=t_: e.memset(Sx[t_][:].bitcast(BF16), 1.0), (), (SB[t_],))
            S.dma("pool", out=ctxK, in_=ctxk_d[l].rearrange("(c p) k -> p c k", p=128), writes=(SB[7],))
            for b in range(4):
                S.dma("pool", out=ctxV[:, b, :, 0:64], in_=ctxv_d[l, b * 128:(b + 1) * 128, :].rearrange("p (h d) -> p h d", d=64),
                      writes=(SB[6],))
            chk("C0a")
            wQ, wQb = load_piece(l, "CQ")
            prefetch(l, "CK")
            for c in range(4):
                for n in range(4):
                    pb, pbB = proj_fm(wQ, wQb, c * 128, n)
                    cp(evac_eng(), qv(c)[:, n * 512:(n + 1) * 512], pb[:], (pbB,), (SB[c // 2],))
            chk("C0b")
            wK, wKb = load_piece(l, "CK")
            prefetch(l, "CV")
            for c in range(4):
                for n in range(4):
                    pb, pbB = proj_fm(wK, wKb, c * 128, n)
                    st_, stB_ = staging()
                    cp("act", st_[:], pb[:], (pbB,), (stB_,))
                    cp("dve", kv(c)[:, n * 512:(n + 1) * 512], pb[:], (pbB,), (SB[2 + c // 2],))
                    S.dma("sp", out=kT_d[l, c * 128:(c + 1) * 128, n * 512:(n + 1) * 512], in_=st_[:], reads=(stB_,))
            chk("C0c")
            wV, wVb = load_piece(l, "CV")
            for blk in range(16):
                pb, pbB = bank()
                for kc in range(8):
                    mm(pb[:], h_sb[:, kc, blk * 128:(blk + 1) * 128], wV[:, kc, 0:512], kc == 0, kc == 7, (wVb, hB[blk // 4]), (pbB,))
                st_, stB_ = staging()
                cp("act", st_[:], pb[:], (pbB,), (stB_,))
                va, vaB = vaug(blk)
                cp("dve", va[:, :, 0:64], pb[:].rearrange("p (h d) -> p h d", d=64), (pbB,), (vaB,))
                S.dma("sp", out=vo_d[l, blk * 128:(blk + 1) * 128, :], in_=st_[:], reads=(stB_,))
            chk("C1")
            amB, ebB, bsB = Buf("amask"), [Buf("ebm0"), Buf("ebm1")], Buf("bstg")
            ptB = [Buf("pt%d" % i) for i in range(8)]
            S.op("pool", lambda e: e.memset(small[:, 200:201], 0.0), (), tuple(hB) + (amB, bsB, smallB) + tuple(ebB) + tuple(ptB))
            amask = h_sb[:, 0:3, :].rearrange("p a b -> p (a b)")[:, 0:4608]
            ebm = [h_sb[:, 3, 0:1536], h_sb[:, 4, 0:1536]]
            bstg = h_sb[:, 5:7, :].rearrange("p a b -> p (a b)").bitcast(F32)[:, 0:1536]
            PT = [h_sb[:, 7, i * 256:(i + 1) * 256] for i in range(8)]
            for c0 in range(0, 4608, 1536):
                S.dma("pool", out=amask[:, c0:c0 + 1536], in_=maskT_d[:, c0:c0 + 1536], writes=(amB,))
            modq = {"mm": 0, "ld": 0, "on": (l + 1 < nl_run)}

            def mod_step(final=False):
                if not modq["on"]:
                    return
                n_do = 12 if final else 2
                for _ in range(n_do):
                    if modq["mm"] < modq["ld"]:
                        mod_mm(l + 1, modq["mm"])
                        modq["mm"] += 1
                    if modq["ld"] < 12 and modq["ld"] - modq["mm"] < 2:
                        mod_load(l + 1, modq["ld"])
                        modq["ld"] += 1
                    if modq["mm"] >= 12:
                        break
                if final:
                    while modq["mm"] < 12:
                        if modq["ld"] <= modq["mm"]:
                            mod_load(l + 1, modq["ld"])
                            modq["ld"] += 1
                        mod_mm(l + 1, modq["mm"])
                        modq["mm"] += 1
                    mod_finish(l + 1)

            LA = 4
            work = []
            for hd in range(8):
                for qg in range(8):
                    items = [("l", kb, ri) for (kb, ri) in pairs[qg]] + [("c", cb, 0) for cb in range(4)]
                    for ii, (kind, kb, ri) in enumerate(items):
                        work.append((hd, qg, kind, kb, ri, ii == 0, ii == len(items) - 1))
            st8 = {"ebi": 0, "cls": -1, "hd": -1, "obank": 0}
            stage1 = []

            def emit_front(w, idx):
                hd, qg, kind, kb, ri, first, last = w
                c, hh = hd // 2, hd % 2
                ps_ = slice(hh * 64, (hh + 1) * 64)
                qcols = slice(qg * 256, (qg + 1) * 256)
                if hd != st8["hd"]:
                    mod_step()
                    S.dma("sp", out=bstg, in_=biasT_d[l, hd], writes=(bsB,))
                    act(bstg, bstg, AF.Exp, (bsB,), (bsB,))
                    st8["hd"] = hd
                    st8["cls"] = -1
                cls = _mask_class(qg)
                if cls != st8["cls"]:
                    st8["ebi"] ^= 1
                    tt("pool", ebm[st8["ebi"]], bstg, amask[:, cls * 1536:(cls + 1) * 1536], ALU.mult, (bsB, amB), (ebB[st8["ebi"]],))
                    st8["cls"] = cls
                if first:
                    st8["obank"] = (st8["obank"] + 1) % 3
                ob_, obB = banks[st8["obank"]], bankB[st8["obank"]]
                sp_, spB = banks[3 + idx % 4], bankB[3 + idx % 4]
                pt, ptb = PT[idx % 8], ptB[idx % 8]
                if kind == "l":
                    mm(sp_[:, 0:256], kv(c)[ps_, kb * 128:(kb + 1) * 128], qv(c)[ps_, qcols], True, True,
                       (SB[2 + c // 2], SB[c // 2]), (spB,))
                    act(pt, sp_[:, 0:256], AF.Exp, (spB,), (ptb,), scale=0.125)
                    e_ = st8["ebi"]
                    tt("dve", pt, pt, ebm[e_][:, ri * 256:(ri + 1) * 256], ALU.mult, (ptb, ebB[e_]), (ptb,))
                    va, vaB = vaug(kb)
                    lhs = va[:, hd, 0:65]
                else:
                    mm(sp_[:, 0:256], ctxK[ps_, c, kb * 128:(kb + 1) * 128], qv(c)[ps_, qcols], True, True,
                       (SB[7], SB[c // 2]), (spB,))
                    act(pt, sp_[:, 0:256], AF.Exp, (spB, vecsB), (ptb,), scale=0.125, bias=ctxneg)
                    vaB = SB[6]
                    lhs = ctxV[:, kb, hd, 0:65]
                stage1.append((pt, ptb, lhs, vaB, ob_, obB))

            fin_cnt = [0]

            def emit_back(w, idx):
                hd, qg, kind, kb, ri, first, last = w
                pt, ptb, lhs, vaB, ob_, obB = stage1[idx]
                qcols = slice(qg * 256, (qg + 1) * 256)
                mm(ob_[0:65, 0:256], lhs, pt, first, last, (vaB, ptb), (obB,))
                if last:
                    st_, stB_ = staging()
                    fi = fin_cnt[0] % 2
                    fin_cnt[0] += 1
                    oo = S7b[:, T + fi * 256:T + (fi + 1) * 256]
                    cp("dve", st_[64:65, 0:256], ob_[64:65, 0:256], (obB,), (stB_,))
                    cp("act", oo[0:64, :], ob_[0:64, 0:256], (obB,), (SB[7],))
                    S.dma("sp", out=den_d[hd:hd + 1, qcols], in_=st_[64:65, 0:256], reads=(stB_,), writes=(denB,))
                    S.dma("sp", out=mix_d[512 + hd * 64:576 + hd * 64, qcols], in_=oo[0:64, :], reads=(SB[7],), writes=(mixB[8 + hd],))

            G = 4
            nw = len(work)
            for base in range(0, nw + G, G):
                for idx in range(base, min(base + G, nw)):
                    emit_front(work[idx], idx)
                for idx in range(max(base - G, 0), min(base, nw)):
                    emit_back(work[idx], idx)
            mod_step(final=True)
            S.op("pool", lambda e: e.memset(small[:, 200:201], 0.0), (), (amB, bsB, smallB) + tuple(ebB) + tuple(ptB) + tuple(hB))

            if stage == 'C':
                break
            wO = []
            for o in range(2):
                wt, wb = wload(wout_d[l, :, o * 512:(o + 1) * 512].rearrange("(kc p) n -> p kc n", p=128),
                               lambda t: t[:].rearrange("p (kc n) -> p kc n", kc=8))
                wO.append((wt[:].rearrange("p (kc n) -> p kc n", kc=8), wb))
            def wout_prep(n):
                mt = Sx[n % 3][:].bitcast(BF16)[:, 0:4096].rearrange("p (kc t) -> p kc t", kc=8)
                S.dma("sp", out=mt, in_=mix_d[:, n * 512:(n + 1) * 512].rearrange("(kc p) t -> p kc t", p=128),
                      reads=tuple(mixB), writes=(SB[n % 3],))
                dn = Sx[3 + n % 3][:, 0:T].rearrange("p (c t) -> p c t", c=4)
                for hd in range(8):
                    S.dma("sp", out=dn[(hd % 2) * 64:(hd % 2 + 1) * 64, hd // 2, :],
                          in_=den_d[hd:hd + 1, n * 512:(n + 1) * 512].to_broadcast([64, 512]), reads=(denB,), writes=(SB[3 + n % 3],))
                S.op("dve", lambda e, dn=dn: e.reciprocal(out=dn, in_=dn), (SB[3 + n % 3],), (SB[3 + n % 3],))
                tt("dve", mt[:, 4:8, :], mt[:, 4:8, :], dn, ALU.mult, (SB[n % 3], SB[3 + n % 3]), (SB[n % 3],))
                return mt

            mts = {0: wout_prep(0), 1: wout_prep(1)}
            for n in range(4):
                if n + 2 < 4:
                    mts[n + 2] = wout_prep(n + 2)
                mt = mts[n]
                for oc in range(8):
                    wv, wb = wO[oc // 4]
                    pb, pbB = bank()
                    for kc in range(8):
                        mm(pb[:], wv[:, kc, (oc % 4) * 128:(oc % 4 + 1) * 128], mt[:, kc, :], kc == 0, kc == 7, (wb, SB[n % 3]), (pbB,))
                    stt(x_sb[:, oc, n * 512:(n + 1) * 512], pb[:], MOD(l, 2, oc), x_sb[:, oc, n * 512:(n + 1) * 512], ALU.mult, ALU.add,
                        (pbB, modB, xB[oc][n]), (xB[oc][n],))
                if stage != "wout":
                    norm_mod(D_GS2, lambda j: MOD(l, 3, j), l, tiles=(n,))

            if stage == 'wout':
                break
            def hid(c):
                return Sx[c // 2][:].bitcast(BF16)[:, (c % 2) * T:(c % 2 + 1) * T]

            for g in range(4):
                for half in range(2):
                    c0 = g * 1024 + half * 512
                    wt, wb = wload(w1_d[l, :, c0:c0 + 512].rearrange("(kc p) n -> p kc n", p=128),
                                   lambda t: t[:].rearrange("p (kc n) -> p kc n", kc=8))
                    wv = wt[:].rearrange("p (kc n) -> p kc n", kc=8)
                    for mc in range(4):
                        c = half * 4 + mc
                        for n in range(4):
                            pb, pbB = proj_fm(wv, wb, mc * 128, n)
                            st_, stB_ = staging()
                            act(st_[:], pb[:], AF.Relu, (pbB,), (stB_,))
                            act(hid(c)[:, n * 512:(n + 1) * 512], st_[:], AF.Square, (stB_,), (SB[c // 2],))
                w2v = []
                for half in range(2):
                    wt, wb = wload(w2_d[l, g * 1024:(g + 1) * 1024, half * 512:(half + 1) * 512].rearrange("(kc p) n -> p kc n", p=128),
                                   lambda t: t[:].rearrange("p (kc n) -> p kc n", kc=8))
                    w2v.append((wt[:].rearrange("p (kc n) -> p kc n", kc=8), wb))
                for n in range(4):
                    for oc in range(8):
                        wv, wb = w2v[oc // 4]
                        oc4 = oc % 4
                        pb, pbB = bank()
                        for kc in range(8):
                            mm(pb[:], wv[:, kc, oc4 * 128:(oc4 + 1) * 128], hid(kc)[:, n * 512:(n + 1) * 512], kc == 0, kc == 7,
                               (wb, SB[kc // 2]), (pbB,))
                        stt(x_sb[:, oc, n * 512:(n + 1) * 512], pb[:], MOD(l, 5, oc), x_sb[:, oc, n * 512:(n + 1) * 512],
                            ALU.mult, ALU.add, (pbB, modB, xB[oc][n]), (xB[oc][n],))
                    if g == 3 and l + 1 == nl_run and nl_run == NL and stage is None:
                        final_norm((n,))
                        fin_state["done"] = True
                    if g == 3 and l + 1 < nl_run and stage is None:
                        norm_mod(D_GS1, lambda j, l1=l + 1: MOD(l1, 0, j), l + 1, tiles=(n,))
            if stage is not None and l + 1 < nl_run:
                norm_mod(D_GS1, lambda j, l1=l + 1: MOD(l1, 0, j), l + 1)

    for l in range(nl_run):
        try:
            layer_body(l)
        except _Stop:
            break

    if not fin_state["done"]:
        final_norm((0, 1, 2, 3))
    S.dma("sp", out=rgo_d, in_=rgst[:], reads=(rgstB,))
    allb = [b for row in xB for b in row] + hB + SB + wrB + stgB + [vecsB, cstB, rgwB, derB, modB, smallB, rgstB, denB] + mixB
    S.finalize(allb)
    return nc


def _pvec(a):
    a = np.asarray(a, np.float32)
    lead = a.shape[:-1]
    n = a.shape[-1] // 128
    a = a.reshape(lead + (n, 128))
    return np.moveaxis(a, -1, 0)


def _consts():
    c = np.zeros((128, NCONST), np.float32)
    c[:, C_ID:C_ID + 128] = np.eye(128)
    c[:, C_ONES:C_ONES + 128] = 1.0 / 1024.0
    for hh in range(2):
        c[hh * 64:(hh + 1) * 64, C_BLK + hh * 64:C_BLK + (hh + 1) * 64] = 1.0 / 64.0
    s = np.arange(64)[:, None]
    t = np.arange(64)[None, :]
    mf = (t >= s).astype(np.float32)
    mb = (t <= s).astype(np.float32)
    c[:, C_MF:C_MF + 64] = np.concatenate([mf, mf], 0)
    c[:, C_MB:C_MB + 64] = np.concatenate([mb, mb], 0)
    cm = np.ones(T, np.float32)
    cm[::64] = 0.0
    c[:, C_CM:C_CM + T] = cm[None, :]
    c[:, C_HF:C_HF + 32] = 1.0
    c[:, C_HB + 32:C_HB + 64] = 1.0
    return c


def _mask_tables(is_prompt):
    m = np.zeros((128, 3, 6, 256), np.float32)
    kp = np.arange(128)
    qq = np.arange(256)
    if is_prompt:
        for cls in range(3):
            m[:, cls, 2, :] = 1.0
            m[:, cls, 3, :] = 1.0
        return m.reshape(128, 3 * 1536)
    kr_l, kc = kp // 64, kp % 64
    qr_l, qc = qq // 64, qq % 64
    c0 = np.clip(qc - 8, 0, 48)
    colok = (kc[:, None] >= c0[None, :]) & (kc[:, None] < c0[None, :] + 16)
    for cls, qg in ((0, 0), (1, 3), (2, 7)):
        for ri in range(6):
            kb = 2 * qg + ri - 2
            if kb < 0 or kb > 15:
                continue
            kr = 2 * kb + kr_l
            qr = 4 * qg + qr_l
            st = np.clip(qr - 4, 0, 24)
            rowok = (kr[:, None] >= st[None, :]) & (kr[:, None] < st[None, :] + 8)
            m[:, cls, ri, :] = (rowok & colok).astype(np.float32)
    return m.reshape(128, 3 * 1536)


def _bias_tables(rpb):
    kp = np.arange(128)
    qq = np.arange(256)
    kr_l, kc = kp // 64, kp % 64
    qr_l, qc = qq // 64, qq % 64
    out = np.zeros((NL, 8, 128, 6, 256), np.float32)
    dx = kc[:, None] - qc[None, :]
    okx = np.abs(dx) <= 15
    dxi = np.clip(dx, -15, 15) + 15
    for ri in range(6):
        dy = (2 * (ri - 2) + kr_l)[:, None] - qr_l[None, :]
        oky = np.abs(dy) <= 7
        dyi = np.clip(dy, -7, 7) + 7
        g = rpb[:, :, dyi, dxi]
        out[:, :, :, ri, :] = g * (okx & oky)[None, None].astype(np.float32) if False else np.where((okx & oky)[None, None], g, 0.0)
    return out.reshape(NL, 8, 128, 1536)


_PROG = {}


def make_in_maps(x_prompt, x_sample, cache_k, cache_v, state_rglru, state_hgrn, c, c_ctx,
           w_mod, b_mod, norm1, norm2, w_in, rg_conv_w, rg_conv_b, rg_w_a, rg_b_a, rg_w_x, rg_b_x,
           rg_lambda, hg_lb, hg_norm, na_rpb, w_out, w1, w2, norm_f):
    f32 = lambda a: np.ascontiguousarray(np.asarray(a, np.float32))
    x_prompt, x_sample = f32(x_prompt), f32(x_sample)
    perm = _win_perm()
    w_in_p = f32(np.asarray(w_in)[:, :, perm])
    w_mod, w_out, w1, w2 = f32(w_mod), f32(w_out), f32(w1), f32(w2)
    consts = _consts()
    rgw = np.zeros((NL, 2, 2, 2, 128, 128), np.float32)
    for ax, wsrc in enumerate((np.asarray(rg_w_a), np.asarray(rg_w_x))):
        for cc in range(2):
            for hh in range(2):
                rgw[:, :, cc, ax, hh * 64:(hh + 1) * 64, hh * 64:(hh + 1) * 64] = wsrc[:, :, cc * 2 + hh]
    rgw = rgw.reshape(NL, 8, 128, 128)
    biasT_s = _bias_tables(np.asarray(na_rpb, np.float32))
    biasT_p = np.zeros_like(biasT_s)
    mask_s, mask_p = _mask_tables(False), _mask_tables(True)

    def vec_common():
        v = np.zeros((128, NVEC), np.float32)

        def put(name, arr):
            arr = np.asarray(arr, np.float32).reshape(128, -1)
            v[:, VOFF[name]:VOFF[name] + arr.shape[1]] = arr
        put("b_mod", _pvec(np.asarray(b_mod).reshape(NL, 48 * 128)).reshape(128, NL * 48))
        put("norm1", _pvec(norm1))
        put("norm2", _pvec(norm2))
        put("norm_f", _pvec(norm_f))
        cw = _pvec(np.asarray(rg_conv_w))
        put("conv_w", np.transpose(cw, (0, 1, 3, 2)))
        put("conv_b", _pvec(rg_conv_b))
        put("rg_b_a", _pvec(rg_b_a))
        put("rg_b_x", _pvec(rg_b_x))
        put("rg_lam", _pvec(rg_lambda))
        hl = _pvec(hg_lb)
        put("hg_lb", np.transpose(hl, (0, 2, 3, 1)))
        put("hg_norm", _pvec(hg_norm))
        return v, put

    in_maps = []
    for core in range(8):
        v, put = vec_common()
        if core < 4:
            xs = x_prompt[core * 8:(core + 1) * 8].reshape(T, 1024)
            put("cond", _pvec(c_ctx))
            v[:, VOFF["flags"]:VOFF["flags"] + 4] = np.array([1.0, 0.0, NEG, 0.0], np.float32)[None, :]
            hgs0 = np.zeros((NL, 4, 128, 64), np.float32)
            ctxk = np.zeros((NL, 512, 512), np.float32)
            ctxv = np.zeros((NL, 512, 512), np.float32)
            bT, mT = biasT_p, mask_p
        else:
            b = core - 4
            xs = x_sample[b]
            put("cond", _pvec(np.asarray(c)[b]))
            v[:, VOFF["flags"]:VOFF["flags"] + 4] = np.array([0.0, 1.0, 0.0, 0.0], np.float32)[None, :]
            put("rg_h0", _pvec(np.asarray(state_rglru)[b]))
            sh = np.asarray(state_hgrn, np.float32)[b]
            hgs0 = sh.reshape(NL, 2, 2, 2, 64, 64).reshape(NL, 4, 128, 64)
            ctxk = np.ascontiguousarray(np.transpose(np.asarray(cache_k, np.float32)[b], (0, 2, 3, 1)).reshape(NL, 512, 512))
            ctxv = np.asarray(cache_v, np.float32)[b].reshape(NL, 512, 512)
            bT, mT = biasT_s, mask_s
        in_maps.append({
            "xT": np.ascontiguousarray(xs.T), "vecs": v, "consts": consts, "rgw": rgw, "hgs0": f32(hgs0),
            "ctxk": f32(ctxk), "ctxv": f32(ctxv), "biasT": bT, "maskT": mT,
            "w_mod": w_mod, "w_in": w_in_p, "w_out": w_out, "w1": w1, "w2": w2,
        })
    return in_maps


def kernel(**inputs):
    in_maps = make_in_maps(**inputs)
    if "nc" not in _PROG:
        _PROG["nc"] = build_program()
    res = run_bass_kernel_spmd(_PROG["nc"], in_maps, core_ids=list(range(8)))
    return assemble(res.results)


def assemble(R):
    y_prompt = np.concatenate([R[i]["yT"].T.reshape(8, 256, 1024) for i in range(4)], 0)
    y_sample = np.stack([R[4 + i]["yT"].T for i in range(4)], 0)
    nk = np.concatenate([np.transpose(R[i]["kT"].reshape(NL, 8, 64, 8, 256), (3, 0, 4, 1, 2)) for i in range(4)], 0)
    nv = np.concatenate([np.transpose(R[i]["vo"].reshape(NL, 8, 256, 8, 64), (1, 0, 2, 3, 4)) for i in range(4)], 0)
    rgs = []
    hgs = []
    for i in range(4):
        r = R[i]["rgo"].reshape(128, NL, 2, 2, 8)
        rgs.append(np.transpose(r, (4, 1, 2, 3, 0)).reshape(8, NL, 2, 256))
        hg = R[i]["hgo"].reshape(NL, 2, 2, 2, 64, 8, 64)
        hg = np.transpose(hg, (5, 0, 1, 2, 3, 4, 6)).reshape(8, NL, 2, 4, 64, 64).copy()
        hg[:, :, 1] = hg[::-1, :, 1]
        hgs.append(hg)
    return (y_prompt.astype(np.float32), y_sample.astype(np.float32), np.ascontiguousarray(nk, np.float32),
            np.ascontiguousarray(nv, np.float32), np.concatenate(rgs, 0).astype(np.float32),
            np.concatenate(hgs, 0).astype(np.float32))
```

```python
import numpy as np
import concourse.bass as bass
import concourse.mybir as mybir
from concourse.bass_utils import run_bass_kernel_spmd

F32 = mybir.dt.float32
BF16 = mybir.dt.bfloat16
AF = mybir.ActivationFunctionType
ALU = mybir.AluOpType

NL = 4
T = 2048
NEG = -30000.0


class Buf:
    __slots__ = ("name", "w", "r", "excl")

    def __init__(self, name, excl=False):
        self.name = name
        self.w = None
        self.r = []
        self.excl = excl


class Op:
    __slots__ = ("eng", "fn", "deps", "kind", "sig", "val", "sem", "waits")


class Sched:
    ENG = ("pe", "act", "dve", "pool", "sp")

    def __init__(self, nc, n_dma_sems=24):
        self.nc = nc
        self.ops = []
        self.n_dma_sems = n_dma_sems

    def _add(self, eng, fn, reads, writes, kind):
        op = Op()
        op.eng, op.fn, op.kind = eng, fn, kind
        writes = list(writes) + [b for b in reads if b.excl]
        reads = [b for b in reads if not b.excl]
        deps = []
        for b in reads:
            if b.w is not None:
                deps.append(b.w)
        for b in writes:
            if b.w is not None:
                deps.append(b.w)
            deps.extend(b.r)
        op.deps = deps
        op.sig = False
        op.val = 0
        op.sem = None
        self.ops.append(op)
        for b in reads:
            b.r.append(op)
        for b in writes:
            b.w = op
            b.r = []
        return op

    def op(self, eng, fn, reads=(), writes=()):
        return self._add(eng, fn, reads, writes, "c")

    def dma(self, eng, out, in_, reads=(), writes=()):
        return self._add(eng, lambda e: e.dma_start(out=out, in_=in_), reads, writes, "d")

    def finalize(self, final_bufs):
        nc = self.nc
        self._add("sp", None, [], list(final_bufs), "c")
        needed = set()
        for op in self.ops:
            for d in op.deps:
                if d.kind == "c" and not (d.eng == op.eng and op.eng == "pe"):
                    needed.add(id(d))
        csem = {e: nc.alloc_semaphore("c_" + e) for e in self.ENG}
        dsem = [nc.alloc_semaphore("d_%d" % i) for i in range(self.n_dma_sems)]
        dval = [0] * self.n_dma_sems
        dnext = 0
        dnext_sw = 0
        cnt = {e: 0 for e in self.ENG}
        known = {e: {} for e in self.ENG}
        streams = {e: [] for e in self.ENG}
        for op in self.ops:
            w = {}
            kn = known[op.eng]
            for d in op.deps:
                if d.kind == "c":
                    if d.eng == op.eng and op.eng == "pe":
                        continue
                    key = ("c", d.eng)
                else:
                    key = ("d", d.sem)
                if kn.get(key, 0) >= d.val:
                    continue
                if w.get(key, 0) < d.val:
                    w[key] = d.val
            if op.kind == "d":
                half = self.n_dma_sems // 2
                if op.eng == "pool":
                    s = half + dnext_sw
                    dnext_sw = (dnext_sw + 1) % half
                else:
                    s = dnext
                    dnext = (dnext + 1) % half
                if dval[s] > 0 and kn.get(("d", s), 0) < dval[s]:
                    if w.get(("d", s), 0) < dval[s]:
                        w[("d", s)] = dval[s]
                dval[s] += 16
                op.sem = s
                op.val = dval[s]
            else:
                if id(op) in needed:
                    cnt[op.eng] += 1
                    op.sig = True
                op.val = cnt[op.eng]
            for k, v in w.items():
                kn[k] = v
            op.waits = w
            streams[op.eng].append(op)

        def replay(name, e):
            for op in streams[name]:
                for (kind, k), v in op.waits.items():
                    e.wait_ge(csem[k] if kind == "c" else dsem[k], v)
                if op.fn is None:
                    continue
                ins = op.fn(e)
                if op.kind == "d":
                    ins.then_inc(dsem[op.sem], 16)
                elif op.sig:
                    ins.then_inc(csem[name], 1)

        with nc.Block() as block:
            @block.tensor
            def _(e):
                replay("pe", e)

            @block.scalar
            def _(e):
                replay("act", e)

            @block.vector
            def _(e):
                replay("dve", e)

            @block.gpsimd
            def _(e):
                replay("pool", e)

            @block.sync
            def _(e):
                replay("sp", e)


def _vec_layout():
    off = {}
    n = 0
    for name, sz in (("b_mod", NL * 48), ("norm1", NL * 8), ("norm2", NL * 8), ("norm_f", 8),
                     ("conv_w", NL * 2 * 4), ("conv_b", NL * 2), ("rg_b_a", NL * 4), ("rg_b_x", NL * 4),
                     ("rg_lam", NL * 4), ("hg_lb", 16), ("hg_norm", NL * 2), ("rg_h0", NL * 4),
                     ("cond", 8), ("flags", 4)):
        off[name] = n
        n += sz
    return off, n


VOFF, NVEC = _vec_layout()
C_ID, C_ONES, C_BLK, C_MF, C_MB, C_CM, C_HF, C_HB, NCONST = 0, 128, 256, 384, 448, 512, 2560, 2624, 2688
PIECES = {"A": (0, 512), "B0": (512, 512), "B1": (1024, 512), "BG": (1536, 256),
          "CQ": (1792, 512), "CK": (2304, 512), "CV": (2816, 512)}


def _win_perm():
    xa, ga, bq, bff, bfb, bi, bg, cq, ck, cv = 0, 256, 512, 768, 1024, 1280, 1536, 1792, 2304, 2816
    cols = list(range(0, 512))
    for cc in range(2):
        for base in (bq, bff, bfb, bi):
            cols += list(range(base + cc * 128, base + cc * 128 + 128))
    cols += list(range(bg, bg + 256))
    cols += list(range(cq, cq + 512)) + list(range(ck, ck + 512)) + list(range(cv, cv + 512))
    return np.array(cols)


def _attn_pairs():
    pairs = []
    for qg in range(8):
        lst = []
        for rel in range(-2, 4):
            kb = 2 * qg + rel
            if 0 <= kb < 16:
                lst.append((kb, rel + 2))
        pairs.append(lst)
    return pairs


def _mask_class(qg):
    return 0 if qg == 0 else (2 if qg == 7 else 1)


def build_program(nl_run=NL, nlw=NL, stage=None):
    nc = bass.Bass("TRN2", target_bir_lowering=False)
    S = Sched(nc)

    def din(name, shape):
        return nc.dram_tensor(name, list(shape), F32, kind="ExternalInput").ap()

    def dout(name, shape):
        return nc.dram_tensor(name, list(shape), F32, kind="ExternalOutput").ap()

    xT_d = din("xT", [1024, T])
    vecs_d = din("vecs", [128, NVEC])
    consts_d = din("consts", [128, NCONST])
    rgw_d = din("rgw", [NL, 8, 128, 128])
    hgs0_d = din("hgs0", [NL, 4, 128, 64])
    ctxk_d = din("ctxk", [NL, 512, 512])
    ctxv_d = din("ctxv", [NL, 512, 512])
    biasT_d = din("biasT", [nlw, 8, 128, 1536])
    maskT_d = din("maskT", [128, 3 * 1536])
    wmod_d = din("w_mod", [nlw, 1024, 6144])
    win_d = din("w_in", [nlw, 1024, 3328])
    wout_d = din("w_out", [nlw, 1024, 1024])
    w1_d = din("w1", [nlw, 1024, 4096])
    w2_d = din("w2", [nlw, 4096, 1024])

    yT_d = dout("yT", [1024, T])
    kT_d = dout("kT", [NL, 512, T])
    vo_d = dout("vo", [NL, T, 512])
    rgo_d = dout("rgo", [128, NL * 4 * 8])
    hgo_d = dout("hgo", [NL, 4, 128, 8 * 64])
    mix_d = nc.dram_tensor("mix_scratch", [1024, T], BF16).ap()
    mixB = [Buf("mixd%d" % i) for i in range(16)]
    den_d = nc.dram_tensor("den_scratch", [8, T], F32).ap()
    denB = Buf("den")

    def sb(name, shape, dt=F32):
        return nc.alloc_sbuf_tensor("sb_" + name, list(shape), dt)

    x_sb = sb("x", [128, 8, T])
    xB = [[Buf("x%d_%d" % (j, n)) for n in range(4)] for j in range(8)]
    h_sb = sb("h", [128, 8, T], BF16)
    hB = [Buf("h%d" % n) for n in range(4)]
    SW = 2056
    Sx = [sb("S%d" % i, [128, SW]) for i in range(8)]
    SB = [Buf("S%d" % i) for i in range(8)]
    NSLOT = 3
    wr = [sb("wr%d" % i, [128, 4096], BF16) for i in range(NSLOT)]
    wrB = [Buf("wr%d" % i) for i in range(NSLOT)]
    stg = [sb("stg%d" % i, [128, 512]) for i in range(4)]
    stgB = [Buf("stg%d" % i) for i in range(4)]
    vecs = sb("vecs", [128, NVEC])
    vecsB = Buf("vecs")
    cst = sb("cst", [128, NCONST], BF16)
    cstB = Buf("cst")
    rgw_sb = sb("rgw", [128, 8, 128], BF16)
    rgwB = Buf("rgw")
    der = sb("der", [128, 512])
    derB = Buf("der")
    modv = sb("modv", [128, NL * 48])
    modB = Buf("modv")
    small = sb("small", [128, 256])
    smallB = Buf("small")
    rgst = sb("rgst", [128, NL * 4 * 8])
    rgstB = Buf("rgst")

    banks = [nc.alloc_psum_tensor("bank%d" % i, [128, 512], F32) for i in range(8)]
    bankB = [Buf("bank%d" % i, excl=True) for i in range(8)]
    bstate = {"i": 0, "stg": 0, "slot": 0, "ev": 0}

    def bank():
        i = bstate["i"]
        bstate["i"] = (i + 1) % 7
        return banks[i], bankB[i]

    def staging():
        i = bstate["stg"]
        bstate["stg"] = (i + 1) % 4
        return stg[i], stgB[i]

    def slot():
        i = bstate["slot"]
        bstate["slot"] = (i + 1) % NSLOT
        return wr[i], wrB[i]

    def evac_eng():
        bstate["ev"] ^= 1
        return "act" if bstate["ev"] else "dve"

    def mm(out, lhsT, rhs, start, stop, r, w, **kw):
        S.op("pe", lambda e: e.matmul(out, lhsT, rhs, start=start, stop=stop, **kw), r, w)

    def act(out, in_, func, r, w, bias=None, scale=None):
        kw = {}
        if bias is not None:
            kw["bias"] = bias
        if scale is not None:
            kw["scale"] = scale
        S.op("act", lambda e: e.activation(out=out, in_=in_, func=func, **kw), r, w)

    def tt(eng, out, a, b, op, r, w):
        S.op(eng, lambda e: e.tensor_tensor(out=out, in0=a, in1=b, op=op), r, w)

    def ts(eng, out, a, s1, s2, op0, op1, r, w):
        if op1 is None:
            S.op(eng, lambda e: e.tensor_scalar(out=out, in0=a, scalar1=s1, scalar2=None, op0=op0), r, w)
        else:
            S.op(eng, lambda e: e.tensor_scalar(out=out, in0=a, scalar1=s1, scalar2=s2, op0=op0, op1=op1), r, w)

    def stt(out, a, sc, b, op0, op1, r, w):
        S.op("dve", lambda e: e.scalar_tensor_tensor(out=out, in0=a, scalar=sc, in1=b, op0=op0, op1=op1), r, w)

    def cp(eng, out, in_, r, w):
        if eng == "act":
            S.op("act", lambda e: e.activation(out=out, in_=in_, func=AF.Copy), r, w)
        else:
            S.op(eng, lambda e: e.tensor_copy(out=out, in_=in_), r, w)

    def scan(out, d0, d1, init, r, w):
        S.op("dve", lambda e: e.tensor_tensor_scan(out=out, data0=d0, data1=d1, initial=init,
                                                   op0=ALU.mult, op1=ALU.add), r, w)

    def V(name, i=0):
        o = VOFF[name] + i
        return vecs[:, o:o + 1]

    def wload(src_ap, view_out):
        t, b = slot()
        S.dma("pool", out=view_out(t), in_=src_ap, reads=(), writes=(b,))
        return t, b

    S.dma("sp", out=vecs[:], in_=vecs_d, writes=(vecsB,))
    for c0 in range(0, NCONST, 1344):
        S.dma("pool", out=cst[:, c0:c0 + 1344], in_=consts_d[:, c0:c0 + 1344], writes=(cstB,))
    for j in range(8):
        for n in range(4):
            S.dma("sp", out=x_sb[:, j, n * 512:(n + 1) * 512], in_=xT_d[j * 128:(j + 1) * 128, n * 512:(n + 1) * 512],
                  writes=(xB[j][n],))
    ident = cst[:, C_ID:C_ID + 128]
    onesD = cst[:, C_ONES:C_ONES + 128]
    blk64 = cst[:, C_BLK:C_BLK + 128]
    maskf = cst[:, C_MF:C_MF + 64]
    maskb = cst[:, C_MB:C_MB + 64]
    cmask = cst[:, C_CM:C_CM + T]
    hmask = [cst[:, C_HF:C_HF + 64], cst[:, C_HB:C_HB + 64]]
    isP = V("flags", 0)
    notP = V("flags", 1)
    ctxneg = V("flags", 2)

    D_SCOND = 0
    D_C8N = 16
    D_C8N2 = 32
    D_HGL = 48
    D_OML = 64
    D_NW = 80
    D_GS1 = 112
    D_GS2 = 144
    D_TMP = 200
    scond = sb("scond", [128, 8], BF16)
    scondB = Buf("scond")
    act(scond[:], vecs[:, VOFF["cond"]:VOFF["cond"] + 8], AF.Silu, (vecsB,), (scondB,))
    lam = vecs[:, VOFF["rg_lam"]:VOFF["rg_lam"] + 16]
    act(der[:, D_TMP:D_TMP + 16], lam, AF.Exp, (vecsB,), (derB,), scale=-1.0)
    act(der[:, D_TMP + 16:D_TMP + 32], der[:, D_TMP:D_TMP + 16], AF.Ln, (derB,), (derB,), bias=1.0)
    ts("dve", der[:, D_C8N:D_C8N + 16], der[:, D_TMP + 16:D_TMP + 32], -8.0, None, ALU.mult, None, (derB,), (derB,))
    ts("dve", der[:, D_C8N2:D_C8N2 + 16], der[:, D_TMP + 16:D_TMP + 32], -16.0, None, ALU.mult, None, (derB,), (derB,))
    hraw = vecs[:, VOFF["hg_lb"]:VOFF["hg_lb"] + 16].rearrange("p (g l) -> p g l", l=4)
    S.op("dve", lambda e: e.tensor_reduce(out=der[:, D_TMP + 32:D_TMP + 36], in_=hraw, axis=mybir.AxisListType.X,
                                          op=ALU.max, negate=True), (vecsB,), (derB,))
    for g in range(4):
        act(der[:, D_TMP + 40 + g * 4:D_TMP + 44 + g * 4], vecs[:, VOFF["hg_lb"] + g * 4:VOFF["hg_lb"] + g * 4 + 4],
            AF.Exp, (vecsB, derB), (derB,), bias=der[:, D_TMP + 32 + g:D_TMP + 33 + g])
    ew = der[:, D_TMP + 40:D_TMP + 56].rearrange("p (g l) -> p g l", l=4)
    S.op("dve", lambda e: e.tensor_reduce(out=der[:, D_TMP + 56:D_TMP + 60], in_=ew, axis=mybir.AxisListType.X,
                                          op=ALU.add), (derB,), (derB,))
    S.op("dve", lambda e: e.reciprocal(out=der[:, D_TMP + 60:D_TMP + 64], in_=der[:, D_TMP + 56:D_TMP + 60]), (derB,), (derB,))
    for g in range(4):
        ts("dve", der[:, D_TMP + 40 + g * 4:D_TMP + 44 + g * 4], der[:, D_TMP + 40 + g * 4:D_TMP + 44 + g * 4],
           der[:, D_TMP + 60 + g:D_TMP + 61 + g], None, ALU.mult, None, (derB,), (derB,))
    hgl = der[:, D_HGL:D_HGL + 16].rearrange("p (g l) -> p g l", l=4)
    S.op("dve", lambda e: e.memset(der[:, D_HGL:D_HGL + 16], 0.0), (), (derB,))
    for l in range(1, 4):
        tt("dve", hgl[:, :, l:l + 1], hgl[:, :, l - 1:l], ew[:, :, l:l + 1], ALU.add, (derB,), (derB,))
    ts("dve", der[:, D_OML:D_OML + 16], der[:, D_HGL:D_HGL + 16], -1.0, 1.0, ALU.mult, ALU.add, (derB,), (derB,))
    ts("dve", der[:, D_NW:D_NW + 32], vecs[:, VOFF["conv_w"]:VOFF["conv_w"] + 32], isP, -1.0, ALU.mult, ALU.mult,
       (vecsB,), (derB,))

    mod_slots = {}

    def mod_load(l, g):
        wt, wb = wload(wmod_d[l, :, g * 512:(g + 1) * 512].rearrange("(kc p) n -> p kc n", p=128),
                       lambda t: t[:].rearrange("p (kc n) -> p kc n", kc=8))
        mod_slots[(l, g)] = (wt[:].rearrange("p (kc n) -> p kc n", kc=8), wb)

    def mod_mm(l, g):
        wv, wb = mod_slots.pop((l, g))
        pb, pbB = banks[7], bankB[7]
        for mc in range(4):
            gi = g * 4 + mc
            for kc in range(8):
                mm(pb[:, gi:gi + 1], wv[:, kc, mc * 128:(mc + 1) * 128], scond[:, kc:kc + 1],
                   kc == 0, kc == 7, (wb, scondB), (pbB,))

    def mod_finish(l):
        pb, pbB = banks[7], bankB[7]
        tt("dve", modv[:, l * 48:(l + 1) * 48], pb[:, 0:48], vecs[:, VOFF["b_mod"] + l * 48:VOFF["b_mod"] + (l + 1) * 48],
           ALU.add, (pbB, vecsB), (modB,))
        stt(der[:, D_GS1 + l * 8:D_GS1 + l * 8 + 8], modv[:, l * 48 + 8:l * 48 + 16], 1.0,
            vecs[:, VOFF["norm1"] + l * 8:VOFF["norm1"] + l * 8 + 8], ALU.add, ALU.mult, (modB, vecsB), (derB,))
        stt(der[:, D_GS2 + l * 8:D_GS2 + l * 8 + 8], modv[:, l * 48 + 32:l * 48 + 40], 1.0,
            vecs[:, VOFF["norm2"] + l * 8:VOFF["norm2"] + l * 8 + 8], ALU.add, ALU.mult, (modB, vecsB), (derB,))

    if nl_run > 0:
        mod_load(0, 0)
        mod_load(0, 1)
        for g in range(12):
            mod_mm(0, g)
            if g + 2 < 12:
                mod_load(0, g + 2)
        mod_finish(0)
    bstate["i"] = 0

    def MOD(l, m, j):
        o = l * 48 + m * 8 + j
        return modv[:, o:o + 1]

    S.op("pool", lambda e: e.memset(rgst[:], 0.0), (), (rgstB,))
    S.op("pool", lambda e: e.memset(Sx[0][:, 0:2], 0.0), (), (SB[0],))
    S.op("pool", lambda e: e.memset(Sx[0][:, 2050:SW], 0.0), (), (SB[0],))

    def norm_mod(gs_off, sh_of, l, tiles=(0, 1, 2, 3)):
        sq = Sx[7][:].bitcast(BF16)
        for n in tiles:
            tsl = slice(n * 512, (n + 1) * 512)
            pb, pbB = bank()
            for j in range(8):
                act(sq[:, j * 512:(j + 1) * 512] if False else sq[:, (j % 8) * 512:(j % 8) * 512 + 512],
                    x_sb[:, j, tsl], AF.Square, (xB[j][n],), (SB[7],))
                mm(pb[:], onesD, sq[:, (j % 8) * 512:(j % 8) * 512 + 512], j == 0, j == 7, (SB[7], cstB), (pbB,))
            rs = Sx[6][:, 0:512]
            act(rs, pb[:], AF.Sqrt, (pbB,), (SB[6],), bias=1e-6)
            S.op("dve", lambda e, rs=rs: e.reciprocal(out=rs, in_=rs), (SB[6],), (SB[6],))
            for j in range(8):
                tmp = Sx[6][:, 512 + (j % 2) * 512:1024 + (j % 2) * 512]
                stt(tmp, x_sb[:, j, tsl], der[:, gs_off + l * 8 + j:gs_off + l * 8 + j + 1], rs, ALU.mult, ALU.mult,
                    (xB[j][n], derB, SB[6]), (SB[6],))
                act(h_sb[:, j, tsl], tmp, AF.Identity, (SB[6], modB), (hB[n],), bias=sh_of(j))

    def proj_fm(wv, wb, c0, n):
        pb, pbB = bank()
        for kc in range(8):
            mm(pb[:], wv[:, kc, c0:c0 + 128], h_sb[:, kc, n * 512:(n + 1) * 512], kc == 0, kc == 7, (wb, hB[n]), (pbB,))
        return pb, pbB

    piece_cache = {}

    def prefetch(l, name):
        if (l, name) not in piece_cache:
            piece_cache[(l, name)] = load_piece_raw(l, name)

    def load_piece(l, name):
        if (l, name) in piece_cache:
            return piece_cache.pop((l, name))
        return load_piece_raw(l, name)

    def load_piece_raw(l, name):
        c0, ncol = PIECES[name]
        wt, wb = wload(win_d[l, :, c0:c0 + ncol].rearrange("(kc p) n -> p kc n", p=128),
                       lambda t: t[:, 0:8 * ncol].rearrange("p (kc n) -> p kc n", kc=8))
        return wt[:, 0:8 * ncol].rearrange("p (kc n) -> p kc n", kc=8), wb

    pairs = _attn_pairs()

    class _Stop(Exception):
        pass

    cur = {"l": 0}

    def chk(name):
        if stage == name or stage == "%d:%s" % (cur["l"], name):
            raise _Stop()

    def layer_body(l):
        cur["l"] = l
        for _once in (0,):
            if l == 0:
                norm_mod(D_GS1, lambda j: MOD(l, 0, j), l)
            S.dma("pool", out=rgw_sb[:], in_=rgw_d[l].rearrange("m p n -> p m n"), writes=(rgwB,))

            if stage == 'norm1':
                break
            wA, wAb = load_piece(l, "A")
            for cc in range(2):
                xa_pad = Sx[0]
                xc = Sx[1][:, 0:T]
                gga = Sx[7][:].bitcast(BF16)[:, 0:T]
                xcb = Sx[7][:].bitcast(BF16)[:, T:2 * T]
                S.op("pool", lambda e: e.memset(Sx[0][:, 0:2], 0.0), (), (SB[0],))
                S.op("pool", lambda e: e.memset(Sx[0][:, 2050:SW], 0.0), (), (SB[0],))
                for n in range(4):
                    pb, pbB = proj_fm(wA, wAb, cc * 128, n)
                    cp(evac_eng(), xa_pad[:, 2 + n * 512:2 + (n + 1) * 512], pb[:], (pbB,), (SB[0],))
                    pb, pbB = proj_fm(wA, wAb, 256 + cc * 128, n)
                    act(gga[:, n * 512:(n + 1) * 512], pb[:], AF.Gelu, (pbB,), (SB[7],))
                prefetch(l, "B0" if cc == 0 else "BG")
                cw = lambda k: vecs[:, VOFF["conv_w"] + (l * 2 + cc) * 4 + k:VOFF["conv_w"] + (l * 2 + cc) * 4 + k + 1]
                nw = lambda k: der[:, D_NW + (l * 2 + cc) * 4 + k:D_NW + (l * 2 + cc) * 4 + k + 1]
                cb = vecs[:, VOFF["conv_b"] + l * 2 + cc:VOFF["conv_b"] + l * 2 + cc + 1]
                ts("dve", xc, xa_pad[:, 0:T], cw(0), cb, ALU.mult, ALU.add, (SB[0], vecsB), (SB[1],))
                for k in range(1, 4):
                    stt(xc, xa_pad[:, k:k + T], cw(k), xc, ALU.mult, ALU.add, (SB[0], SB[1], vecsB), (SB[1],))
                Xv = lambda o: xa_pad[:, 258 + o:258 + o + 1792:256]
                Cv = lambda o: Sx[1][:, 256 + o:256 + o + 1792:256]
                for (co, xo, k) in ((0, -2, 0), (0, -1, 1), (1, -1, 0), (-1, 0, 3)):
                    stt(Cv(co), Xv(xo), nw(k), Cv(co), ALU.mult, ALU.add, (SB[0], SB[1], derB), (SB[1],))
                chk("A1")
                cp("act", xcb, xc, (SB[1],), (SB[7],))
                chk("A2")
                hdir = [Sx[5][:, 0:T], Sx[6][:, 0:T]]
                for dr in range(2):
                    idx = l * 4 + dr * 2 + cc
                    r_t, i_t, a_t = Sx[2][:, 0:T], Sx[3][:, 0:T], Sx[4][:, 0:T]
                    for n in range(4):
                        tsl = slice(n * 512, (n + 1) * 512)
                        pb, pbB = bank()
                        mm(pb[:], rgw_sb[:, (dr * 2 + cc) * 2 + 0, :], xcb[:, tsl], True, True, (rgwB, SB[7]), (pbB,))
                        act(r_t[:, tsl], pb[:], AF.Sigmoid, (pbB, vecsB), (SB[2],), bias=V("rg_b_a", idx))
                        pb, pbB = bank()
                        mm(pb[:], rgw_sb[:, (dr * 2 + cc) * 2 + 1, :], xcb[:, tsl], True, True, (rgwB, SB[7]), (pbB,))
                        act(i_t[:, tsl], pb[:], AF.Sigmoid, (pbB, vecsB), (SB[3],), bias=V("rg_b_x", idx))
                    act(a_t, r_t, AF.Exp, (SB[2], derB), (SB[4],), scale=der[:, D_C8N + idx:D_C8N + idx + 1])
                    act(r_t, r_t, AF.Exp, (SB[2], derB), (SB[2],), scale=der[:, D_C8N2 + idx:D_C8N2 + idx + 1])
                    act(r_t, r_t, AF.Sqrt, (SB[2],), (SB[2],), scale=-1.0, bias=1.0)
                    chk("A3")
                    tt("dve", i_t, i_t, r_t, ALU.mult, (SB[2], SB[3]), (SB[3],))
                    tt("dve", i_t, i_t, xc, ALU.mult, (SB[3], SB[1]), (SB[3],))
                    if dr == 0:
                        av = Sx[4][:, 256:256 + 1792:256]
                    else:
                        av = Sx[4][:, 255:255 + 1792:256]
                    ts("dve", av, av, notP, None, ALU.mult, None, (SB[4], vecsB), (SB[4],))
                    chk("A4")
                    h0 = V("rg_h0", idx)
                    hb_ = SB[5 + dr]
                    if dr == 0:
                        scan(hdir[0], a_t, i_t, h0, (SB[4], SB[3], vecsB), (hb_,))
                        cp("pool", rgst[:, idx * 8:idx * 8 + 8], Sx[5][:, 255:T:256], (hb_,), (rgstB,))
                    else:
                        scan(hdir[1][:, ::-1], a_t[:, ::-1], i_t[:, ::-1], h0, (SB[4], SB[3], vecsB), (hb_,))
                        cp("pool", rgst[:, idx * 8:idx * 8 + 8], Sx[6][:, 0:T:256], (hb_,), (rgstB,))
                chk("A5")
                tt("dve", hdir[0], hdir[0], hdir[1], ALU.add, (SB[5], SB[6]), (SB[5],))
                oa = Sx[2][:].bitcast(BF16)[:, 0:T]
                tt("dve", oa, hdir[0], gga, ALU.mult, (SB[5], SB[7]), (SB[2],))
                chk("A6")
                S.dma("sp", out=mix_d[cc * 128:(cc + 1) * 128, :], in_=oa, reads=(SB[2],), writes=(mixB[cc * 2], mixB[cc * 2 + 1]))
                chk("A7")

            if stage == 'A':
                break
            wGv, wGb = None, None
            for cc in range(2):
                wBv, wBb = load_piece(l, "B%d" % cc)
                q_t = Sx[0][:, 0:T]
                sg = [Sx[1][:, 0:T], Sx[2][:, 0:T]]
                o_acc = Sx[3][:, 0:T]
                S4b = Sx[4][:].bitcast(BF16)
                AT2 = S4b[:, 0:T].rearrange("p (h j t) -> p h j t", h=2, t=64)
                Qd = S4b[:, T:2 * T]
                DSf = Sx[5][:, 0:T]
                DS = DSf.rearrange("p (v n) -> p v n", n=32)
                S7b = Sx[7][:].bitcast(BF16)
                Vh = S7b[:, 0:T].rearrange("p (b c) -> p b c", c=128)
                Qi, Kt, Kd, KdT = [S7b[:, T + i * 512:T + (i + 1) * 512] for i in range(4)]
                Sbf = Sx[6][:].bitcast(BF16)[:, 0:2112].rearrange("p (m v) -> p m v", v=64)
                fq, lgq, cumq, eq = [Sx[6][:, i * 512:(i + 1) * 512] for i in range(4)]
                for n in range(4):
                    tsl = slice(n * 512, (n + 1) * 512)
                    pb, pbB = proj_fm(wBv, wBb, 0, n)
                    act(q_t[:, tsl], pb[:], AF.Silu, (pbB,), (SB[0],))
                    for dr in range(2):
                        pb, pbB = proj_fm(wBv, wBb, 128 + dr * 128, n)
                        act(sg[dr][:, tsl], pb[:], AF.Sigmoid, (pbB,), (SB[1 + dr],))
                for b4 in range(4):
                    pb, pbB = bank()
                    for bb in range(4):
                        blk = b4 * 4 + bb
                        for kc in range(8):
                            mm(pb[:, bb * 128:(bb + 1) * 128], h_sb[:, kc, blk * 128:(blk + 1) * 128], wBv[:, kc, 384:512],
                               kc == 0, kc == 7, (wBb, hB[blk // 4]), (pbB,))
                    cp(evac_eng(), Vh[:, b4 * 4:(b4 + 1) * 4, :], pb[:].rearrange("p (b c) -> p b c", c=128), (pbB,), (SB[7],))
                prefetch(l, "B1" if cc == 0 else "CQ")
                chk("B1")
                mid_t, tot_t, dec_t = small[:, 0:32], small[:, 32:64], small[:, 64:96]
                for dr in range(2):
                    gi = dr * 2 + cc
                    mk = maskf if dr == 0 else maskb
                    mid_i, tot_i = (31, 63) if dr == 0 else (32, 0)
                    sidx = (lambda n: n) if dr == 0 else (lambda n: 31 - n)
                    oml = der[:, D_OML + gi * 4 + l:D_OML + gi * 4 + l + 1]
                    hgl_ = der[:, D_HGL + gi * 4 + l:D_HGL + gi * 4 + l + 1]
                    for nq in range(4):
                        qs = slice(nq * 512, (nq + 1) * 512)
                        ts("dve", fq, sg[dr][:, qs], oml, hgl_, ALU.mult, ALU.add, (SB[1 + dr], derB), (SB[6],))
                        act(lgq, fq, AF.Ln, (SB[6],), (SB[6],))
                        act(fq, fq, AF.Identity, (SB[6],), (SB[6],), scale=-1.0, bias=1.0)
                        if dr == 0:
                            scan(cumq, cmask[:, 0:512], lgq, 0.0, (cstB, SB[6]), (SB[6],))
                        else:
                            scan(cumq[:, ::-1], cmask[:, 0:512], lgq[:, ::-1], 0.0, (cstB, SB[6]), (SB[6],))
                        c3 = cumq.rearrange("p (n c) -> p n c", c=64)
                        l3 = lgq.rearrange("p (n c) -> p n c", c=64)
                        if dr == 0:
                            ms = slice(nq * 8, nq * 8 + 8)
                            cp("dve", mid_t[:, ms], c3[:, :, mid_i], (SB[6],), (smallB,))
                            cp("dve", tot_t[:, ms], c3[:, :, tot_i], (SB[6],), (smallB,))
                            midv, totv = mid_t[:, ms], tot_t[:, ms]
                        else:
                            lo = 31 - (nq * 8 + 7)
                            cp("dve", mid_t[:, lo:lo + 8], c3[:, ::-1, mid_i], (SB[6],), (smallB,))
                            cp("dve", tot_t[:, lo:lo + 8], c3[:, ::-1, tot_i], (SB[6],), (smallB,))
                            midv, totv = mid_t[:, lo:lo + 8][:, ::-1], tot_t[:, lo:lo + 8][:, ::-1]
                        midb = midv.unsqueeze(2).to_broadcast([128, 8, 64])
                        totb = totv.unsqueeze(2).to_broadcast([128, 8, 64])
                        tt("dve", l3, c3, midb, ALU.subtract, (SB[6], smallB), (SB[6],))
                        act(eq, lgq, AF.Exp, (SB[6],), (SB[6],))
                        tt("dve", Qi, q_t[:, qs], eq, ALU.mult, (SB[0], SB[6]), (SB[7],))
                        act(eq, lgq, AF.Exp, (SB[6],), (SB[6],), scale=-1.0)
                        tt("dve", Kt, fq, eq, ALU.mult, (SB[6],), (SB[7],))
                        tt("dve", l3, c3, totb, ALU.subtract, (SB[6], smallB), (SB[6],))
                        act(eq, lgq, AF.Exp, (SB[6],), (SB[6],), scale=-1.0)
                        tt("dve", Kd, fq, eq, ALU.mult, (SB[6],), (SB[7],))
                        Kt0 = lgq.bitcast(BF16)[:, 0:512]
                        tt("dve", Kt0.rearrange("p (n c) -> p n c", c=64), Kt.rearrange("p (n c) -> p n c", c=64),
                           hmask[dr].unsqueeze(1).to_broadcast([128, 8, 64]), ALU.mult, (SB[7], cstB), (SB[6],))
                        act(eq, cumq, AF.Exp, (SB[6],), (SB[6],))
                        tt("pool", Qd[:, qs], q_t[:, qs], eq, ALU.mult, (SB[0], SB[6]), (SB[4],))
                        chk("B2")
                        pb, pbB = bank()
                        pbb = pb[:].bitcast(BF16)
                        for bb in range(4):
                            S.op("pe", lambda e, o=pbb[:, bb * 128:(bb + 1) * 128], i=Kd[:, bb * 128:(bb + 1) * 128]:
                                 e.transpose(o, i, ident), (SB[7], cstB), (pbB,))
                        cp(evac_eng(), KdT, pbb[:, 0:512], (pbB,), (SB[7],))
                        chk("B2b")
                        pbh = [bank(), bank()]
                        for c8 in range(8):
                            tp, j = (c8 % 2) * 64, c8 // 2
                            cs = slice(c8 * 64, (c8 + 1) * 64)
                            for hh in range(2):
                                ps_ = slice(hh * 64, (hh + 1) * 64)
                                pb, pbB = pbh[hh]
                                lo_k, hi_k = (Kt0, Kt) if dr == 0 else (Kt, Kt0)
                                mm(pb[tp:tp + 64, j * 64:j * 64 + 32], lo_k[ps_, cs], Qi[ps_, c8 * 64:c8 * 64 + 32], True, True,
                                   (SB[7], SB[6]), (pbB,))
                                mm(pb[tp:tp + 64, j * 64 + 32:j * 64 + 64], hi_k[ps_, cs], Qi[ps_, c8 * 64 + 32:c8 * 64 + 64], True, True,
                                   (SB[7], SB[6]), (pbB,))
                        for hh in range(2):
                            pb, pbB = pbh[hh]
                            tt("dve", AT2[:, hh, nq * 4:(nq + 1) * 4, :],
                               pb[:, 0:256].rearrange("p (j t) -> p j t", t=64),
                               mk.unsqueeze(1).to_broadcast([128, 4, 64]), ALU.mult, (pbB, cstB), (SB[4],))
                        chk("B2c")
                        pbp = [bank(), bank()]
                        for c8 in range(8):
                            par, blk = c8 % 2, c8 // 2
                            tp = par * 64
                            gblk = nq * 4 + blk
                            pb, pbB = pbp[par]
                            for hh in range(2):
                                mm(pb[hh * 64:(hh + 1) * 64, blk * 64:(blk + 1) * 64],
                                   KdT[tp:tp + 64, blk * 128 + hh * 64:blk * 128 + hh * 64 + 64],
                                   Vh[tp:tp + 64, gblk, hh * 64:(hh + 1) * 64], True, True, (SB[7],), (pbB,))
                        for par in range(2):
                            pb, pbB = pbp[par]
                            if dr == 0:
                                st0 = nq * 8 + par
                                dsv = DS[:, :, st0:st0 + 7:2]
                            else:
                                lo = 31 - nq * 8 - par - 6
                                dsv = DS[:, :, lo:lo + 7:2][:, :, ::-1]
                            cp(evac_eng(), dsv.rearrange("p v n -> p n v"), pb[:, 0:256].rearrange("p (n v) -> p n v", v=64),
                               (pbB,), (SB[5],))
                    chk("B4")
                    act(dec_t, tot_t, AF.Exp, (smallB,), (smallB,))
                    if dr == 0:
                        dv = small[:, 64 + 4:64 + 32:4]
                    else:
                        dv = small[:, 64 + 4:64 + 32:4]
                    ts("dve", dv, dv, notP, None, ALU.mult, None, (smallB, vecsB), (smallB,))
                    s0, s0B = staging()
                    S.dma("sp", out=s0[:, 0:64], in_=hgs0_d[l, gi], writes=(s0B,))
                    stt(DS[:, :, 0], s0[:, 0:64], small[:, 64:65], DS[:, :, 0], ALU.mult, ALU.add, (s0B, smallB, SB[5]), (SB[5],))
                    S.op("dve", lambda e: e.memset(small[:, 64:65], 0.0), (), (smallB,))
                    decfull = Sx[6][:, 0:T]
                    cp("dve", decfull.rearrange("p (v n) -> p v n", n=32), dec_t.unsqueeze(1).to_broadcast([128, 64, 32]),
                       (smallB,), (SB[6],))
                    scan(DSf, decfull, DSf, 0.0, (SB[6], SB[5]), (SB[5],))
                    cp("pool", Sbf[:, 0, :], s0[:, 0:64], (s0B,), (SB[6],))
                    cp("act", Sbf[:, 1:33, :], DS.rearrange("p v n -> p n v"), (SB[5],), (SB[6],))
                    ts("pool", Sbf[:, 4:32:4, :], Sbf[:, 4:32:4, :], notP, None, ALU.mult, None, (SB[6], vecsB), (SB[6],))
                    fin = Sx[6][:, 1100:1612]
                    cp("dve", fin.rearrange("p (j v) -> p j v", v=64), DS[:, :, 3:32:4].rearrange("p v j -> p j v"), (SB[5],), (SB[6],))
                    S.dma("sp", out=hgo_d[l, gi], in_=fin, reads=(SB[6],))
                    chk("B5")
                    for nq in range(4):
                        qs = slice(nq * 512, (nq + 1) * 512)
                        pI = [bank(), bank()]
                        pN = [bank(), bank()]
                        for c8 in range(8):
                            n_ = nq * 8 + c8
                            m_ = sidx(n_)
                            par, j = c8 % 2, n_ // 2
                            tp = par * 64
                            cs = slice(n_ * 64, (n_ + 1) * 64)
                            for hh in range(2):
                                ps_ = slice(hh * 64, (hh + 1) * 64)
                                mm(pI[par][0][ps_, c8 * 64:(c8 + 1) * 64], Vh[tp:tp + 64, j, hh * 64:(hh + 1) * 64],
                                   AT2[tp:tp + 64, hh, j, :], True, True, (SB[7], SB[4]), (pI[par][1],))
                                mm(pN[hh][0][ps_, c8 * 64:(c8 + 1) * 64], Sbf[ps_, m_, :], Qd[ps_, cs], True, True,
                                   (SB[6], SB[4]), (pN[hh][1],))
                        oq = o_acc[:, qs]
                        for hh in range(2):
                            ps_ = slice(hh * 64, (hh + 1) * 64)
                            if dr == 0:
                                cp("dve" if hh else "act", oq[ps_, :], pN[hh][0][ps_, :], (pN[hh][1],), (SB[3],))
                            else:
                                tt("dve", oq[ps_, :], oq[ps_, :], pN[hh][0][ps_, :], ALU.add, (SB[3], pN[hh][1]), (SB[3],))
                        for par in range(2):
                            ov = oq.rearrange("p (a b t) -> p a b t", b=2, t=64)[:, :, par, :]
                            iv = pI[par][0][:].rearrange("p (a b t) -> p a b t", b=2, t=64)[:, :, par, :]
                            tt("dve", ov, ov, iv, ALU.add, (SB[3], pI[par][1]), (SB[3],))
                chk("B6")
                if cc == 0:
                    wGv, wGb = load_piece(l, "BG")
                for n in range(4):
                    qs = slice(n * 512, (n + 1) * 512)
                    pbg, pbgB = proj_fm(wGv, wGb, cc * 128, n)
                    sbg = Sx[6][:, 0:512]
                    act(sbg, pbg[:], AF.Silu, (pbgB,), (SB[6],))
                    sqb = S7b[:, T:T + 512]
                    act(sqb, o_acc[:, qs], AF.Square, (SB[3],), (SB[7],))
                    pm, pmB = bank()
                    mm(pm[:], blk64, sqb, True, True, (cstB, SB[7]), (pmB,))
                    rs = Sx[6][:, 512:1024]
                    act(rs, pm[:], AF.Sqrt, (pmB,), (SB[6],), bias=1e-6)
                    S.op("dve", lambda e, rs=rs: e.reciprocal(out=rs, in_=rs), (SB[6],), (SB[6],))
                    t_ = Sx[6][:, 1024:1536]
                    stt(t_, o_acc[:, qs], V("hg_norm", l * 2 + cc), rs, ALU.mult, ALU.mult, (SB[3], SB[6], vecsB), (SB[6],))
                    ob = S7b[:, T + 512 + (n % 2) * 512:T + 1024 + (n % 2) * 512]
                    tt("dve", ob, t_, sbg, ALU.mult, (SB[6],), (SB[7],))
                    S.dma("sp", out=mix_d[256 + cc * 128:384 + cc * 128, qs], in_=ob, reads=(SB[7],),
                          writes=(mixB[4 + cc * 2], mixB[5 + cc * 2]))

            if stage == 'B':
                break
            def qv(c):
                return Sx[c // 2][:].bitcast(BF16)[:, (c % 2) * T:(c % 2 + 1) * T]

            def kv(c):
                return Sx[2 + c // 2][:].bitcast(BF16)[:, (c % 2) * T:(c % 2 + 1) * T]

            def vaug(blk):
                t_ = 4 + blk // 7
                o = (blk % 7) * 528
                return Sx[t_][:].bitcast(BF16)[:, o:o + 528].rearrange("p (h d) -> p h d", d=66), SB[t_]

            S6b = Sx[6][:].bitcast(BF16)
            S7b = Sx[7][:].bitcast(BF16)
            ctxV = S6b[:, 1056:1056 + 2112].rearrange("p (b h d) -> p b h d", b=4, d=66)
            ctxK = S7b[:, 0:T].rearrange("p (c k) -> p c k", k=512)
            for t_ in (4, 5, 6):
                S.op("pool", lambda e, t_=t_: e.memset(Sx[t_][:].bitcast(BF16), 1.0), (), (SB[t_],))
            S.dma("pool", out=ctxK, in_=ctxk_d[l].rearrange("(c p) k -> p c k", p=128), writes=(SB[7],))
            for b in range(4):
                S.dma("pool", out=ctxV[:, b, :, 0:64], in_=ctxv_d[l, b * 128:(b + 1) * 128, :].rearrange("p (h d) -> p h d", d=64),
                      writes=(SB[6],))
            chk("C0a")
            wQ, wQb = load_piece(l, "CQ")
            prefetch(l, "CK")
            for c in range(4):
                for n in range(4):
                    pb, pbB = proj_fm(wQ, wQb, c * 128, n)
                    cp(evac_eng(), qv(c)[:, n * 512:(n + 1) * 512], pb[:], (pbB,), (SB[c // 2],))
            chk("C0b")
            wK, wKb = load_piece(l, "CK")
            prefetch(l, "CV")
            for c in range(4):
                for n in range(4):
                    pb, pbB = proj_fm(wK, wKb, c * 128, n)
                    st_, stB_ = staging()
                    cp("act", st_[:], pb[:], (pbB,), (stB_,))
                    cp("dve", kv(c)[:, n * 512:(n + 1) * 512], pb[:], (pbB,), (SB[2 + c // 2],))
                    S.dma("sp", out=kT_d[l, c * 128:(c + 1) * 128, n * 512:(n + 1) * 512], in_=st_[:], reads=(stB_,))
            chk("C0c")
            wV, wVb = load_piece(l, "CV")
            for blk in range(16):
                pb, pbB = bank()
                for kc in range(8):
                    mm(pb[:], h_sb[:, kc, blk * 128:(blk + 1) * 128], wV[:, kc, 0:512], kc == 0, kc == 7, (wVb, hB[blk // 4]), (pbB,))
                st_, stB_ = staging()
                cp("act", st_[:], pb[:], (pbB,), (stB_,))
                va, vaB = vaug(blk)
                cp("dve", va[:, :, 0:64], pb[:].rearrange("p (h d) -> p h d", d=64), (pbB,), (vaB,))
                S.dma("sp", out=vo_d[l, blk * 128:(blk + 1) * 128, :], in_=st_[:], reads=(stB_,))
            chk("C1")
            amB, ebB, bsB = Buf("amask"), [Buf("ebm0"), Buf("ebm1")], Buf("bstg")
            ptB = [Buf("pt%d" % i) for i in range(8)]
            S.op("pool", lambda e: e.memset(small[:, 200:201], 0.0), (), tuple(hB) + (amB, bsB, smallB) + tuple(ebB) + tuple(ptB))
            amask = h_sb[:, 0:3, :].rearrange("p a b -> p (a b)")[:, 0:4608]
            ebm = [h_sb[:, 3, 0:1536], h_sb[:, 4, 0:1536]]
            bstg = h_sb[:, 5:7, :].rearrange("p a b -> p (a b)").bitcast(F32)[:, 0:1536]
            PT = [h_sb[:, 7, i * 256:(i + 1) * 256] for i in range(8)]
            for c0 in range(0, 4608, 1536):
                S.dma("pool", out=amask[:, c0:c0 + 1536], in_=maskT_d[:, c0:c0 + 1536], writes=(amB,))
            modq = {"mm": 0, "ld": 0, "on": (l + 1 < nl_run)}

            def mod_step(final=False):
                if not modq["on"]:
                    return
                n_do = 12 if final else 2
                for _ in range(n_do):
                    if modq["mm"] < modq["ld"]:
                        mod_mm(l + 1, modq["mm"])
                        modq["mm"] += 1
                    if modq["ld"] < 12 and modq["ld"] - modq["mm"] < 2:
                        mod_load(l + 1, modq["ld"])
                        modq["ld"] += 1
                    if modq["mm"] >= 12:
                        break
                if final:
                    while modq["mm"] < 12:
                        if modq["ld"] <= modq["mm"]:
                            mod_load(l + 1, modq["ld"])
                            modq["ld"] += 1
                        mod_mm(l + 1, modq["mm"])
                        modq["mm"] += 1
                    mod_finish(l + 1)

            LA = 4
            work = []
            for hd in range(8):
                for qg in range(8):
                    items = [("l", kb, ri) for (kb, ri) in pairs[qg]] + [("c", cb, 0) for cb in range(4)]
                    for ii, (kind, kb, ri) in enumerate(items):
                        work.append((hd, qg, kind, kb, ri, ii == 0, ii == len(items) - 1))
            st8 = {"ebi": 0, "cls": -1, "hd": -1, "obank": 0}
            stage1 = []

            def emit_front(w, idx):
                hd, qg, kind, kb, ri, first, last = w
                c, hh = hd // 2, hd % 2
                ps_ = slice(hh * 64, (hh + 1) * 64)
                qcols = slice(qg * 256, (qg + 1) * 256)
                if hd != st8["hd"]:
                    mod_step()
                    S.dma("sp", out=bstg, in_=biasT_d[l, hd], writes=(bsB,))
                    act(bstg, bstg, AF.Exp, (bsB,), (bsB,))
                    st8["hd"] = hd
                    st8["cls"] = -1
                cls = _mask_class(qg)
                if cls != st8["cls"]:
                    st8["ebi"] ^= 1
                    tt("dve", ebm[st8["ebi"]], bstg, amask[:, cls * 1536:(cls + 1) * 1536], ALU.mult, (bsB, amB), (ebB[st8["ebi"]],))
                    st8["cls"] = cls
                if first:
                    st8["obank"] = (st8["obank"] + 1) % 3
                ob_, obB = banks[st8["obank"]], bankB[st8["obank"]]
                sp_, spB = banks[3 + idx % 4], bankB[3 + idx % 4]
                pt, ptb = PT[idx % 8], ptB[idx % 8]
                if kind == "l":
                    mm(sp_[:, 0:256], kv(c)[ps_, kb * 128:(kb + 1) * 128], qv(c)[ps_, qcols], True, True,
                       (SB[2 + c // 2], SB[c // 2]), (spB,))
                    act(pt, sp_[:, 0:256], AF.Exp, (spB,), (ptb,), scale=0.125)
                    e_ = st8["ebi"]
                    tt("dve", pt, pt, ebm[e_][:, ri * 256:(ri + 1) * 256], ALU.mult, (ptb, ebB[e_]), (ptb,))
                    va, vaB = vaug(kb)
                    lhs = va[:, hd, 0:65]
                else:
                    mm(sp_[:, 0:256], ctxK[ps_, c, kb * 128:(kb + 1) * 128], qv(c)[ps_, qcols], True, True,
                       (SB[7], SB[c // 2]), (spB,))
                    act(pt, sp_[:, 0:256], AF.Exp, (spB, vecsB), (ptb,), scale=0.125, bias=ctxneg)
                    vaB = SB[6]
                    lhs = ctxV[:, kb, hd, 0:65]
                stage1.append((pt, ptb, lhs, vaB, ob_, obB))

            fin_cnt = [0]

            def emit_back(w, idx):
                hd, qg, kind, kb, ri, first, last = w
                pt, ptb, lhs, vaB, ob_, obB = stage1[idx]
                qcols = slice(qg * 256, (qg + 1) * 256)
                mm(ob_[0:65, 0:256], lhs, pt, first, last, (vaB, ptb), (obB,))
                if last:
                    st_, stB_ = staging()
                    fi = fin_cnt[0] % 2
                    fin_cnt[0] += 1
                    oo = S7b[:, T + fi * 256:T + (fi + 1) * 256]
                    cp("dve", st_[64:65, 0:256], ob_[64:65, 0:256], (obB,), (stB_,))
                    cp("act", oo[0:64, :], ob_[0:64, 0:256], (obB,), (SB[7],))
                    S.dma("sp", out=den_d[hd:hd + 1, qcols], in_=st_[64:65, 0:256], reads=(stB_,), writes=(denB,))
                    S.dma("sp", out=mix_d[512 + hd * 64:576 + hd * 64, qcols], in_=oo[0:64, :], reads=(SB[7],), writes=(mixB[8 + hd],))

            G = 4
            nw = len(work)
            for base in range(0, nw + G, G):
                for idx in range(base, min(base + G, nw)):
                    emit_front(work[idx], idx)
                for idx in range(max(base - G, 0), min(base, nw)):
                    emit_back(work[idx], idx)
            mod_step(final=True)
            S.op("pool", lambda e: e.memset(small[:, 200:201], 0.0), (), (amB, bsB, smallB) + tuple(ebB) + tuple(ptB) + tuple(hB))

            if stage == 'C':
                break
            wO = []
            for o in range(2):
                wt, wb = wload(wout_d[l, :, o * 512:(o + 1) * 512].rearrange("(kc p) n -> p kc n", p=128),
                               lambda t: t[:].rearrange("p (kc n) -> p kc n", kc=8))
                wO.append((wt[:].rearrange("p (kc n) -> p kc n", kc=8), wb))
            def wout_prep(n):
                mt = Sx[n % 3][:].bitcast(BF16)[:, 0:4096].rearrange("p (kc t) -> p kc t", kc=8)
                S.dma("sp", out=mt, in_=mix_d[:, n * 512:(n + 1) * 512].rearrange("(kc p) t -> p kc t", p=128),
                      reads=tuple(mixB), writes=(SB[n % 3],))
                dn = Sx[3 + n % 3][:, 0:T].rearrange("p (c t) -> p c t", c=4)
                for hd in range(8):
                    S.dma("sp", out=dn[(hd % 2) * 64:(hd % 2 + 1) * 64, hd // 2, :],
                          in_=den_d[hd:hd + 1, n * 512:(n + 1) * 512].to_broadcast([64, 512]), reads=(denB,), writes=(SB[3 + n % 3],))
                S.op("dve", lambda e, dn=dn: e.reciprocal(out=dn, in_=dn), (SB[3 + n % 3],), (SB[3 + n % 3],))
                tt("dve", mt[:, 4:8, :], mt[:, 4:8, :], dn, ALU.mult, (SB[n % 3], SB[3 + n % 3]), (SB[n % 3],))
                return mt

            mts = {0: wout_prep(0), 1: wout_prep(1)}
            for n in range(4):
                if n + 2 < 4:
                    mts[n + 2] = wout_prep(n + 2)
                mt = mts[n]
                for oc in range(8):
                    wv, wb = wO[oc // 4]
                    pb, pbB = bank()
                    for kc in range(8):
                        mm(pb[:], wv[:, kc, (oc % 4) * 128:(oc % 4 + 1) * 128], mt[:, kc, :], kc == 0, kc == 7, (wb, SB[n % 3]), (pbB,))
                    stt(x_sb[:, oc, n * 512:(n + 1) * 512], pb[:], MOD(l, 2, oc), x_sb[:, oc, n * 512:(n + 1) * 512], ALU.mult, ALU.add,
                        (pbB, modB, xB[oc][n]), (xB[oc][n],))
                if stage != "wout":
                    norm_mod(D_GS2, lambda j: MOD(l, 3, j), l, tiles=(n,))

            if stage == 'wout':
                break
            def hid(c):
                return Sx[c // 2][:].bitcast(BF16)[:, (c % 2) * T:(c % 2 + 1) * T]

            for g in range(4):
                for half in range(2):
                    c0 = g * 1024 + half * 512
                    wt, wb = wload(w1_d[l, :, c0:c0 + 512].rearrange("(kc p) n -> p kc n", p=128),
                                   lambda t: t[:].rearrange("p (kc n) -> p kc n", kc=8))
                    wv = wt[:].rearrange("p (kc n) -> p kc n", kc=8)
                    for mc in range(4):
                        c = half * 4 + mc
                        for n in range(4):
                            pb, pbB = proj_fm(wv, wb, mc * 128, n)
                            st_, stB_ = staging()
                            act(st_[:], pb[:], AF.Relu, (pbB,), (stB_,))
                            act(hid(c)[:, n * 512:(n + 1) * 512], st_[:], AF.Square, (stB_,), (SB[c // 2],))
                w2v = []
                for half in range(2):
                    wt, wb = wload(w2_d[l, g * 1024:(g + 1) * 1024, half * 512:(half + 1) * 512].rearrange("(kc p) n -> p kc n", p=128),
                                   lambda t: t[:].rearrange("p (kc n) -> p kc n", kc=8))
                    w2v.append((wt[:].rearrange("p (kc n) -> p kc n", kc=8), wb))
                for n in range(4):
                    for oc in range(8):
                        wv, wb = w2v[oc // 4]
                        oc4 = oc % 4
                        pb, pbB = bank()
                        for kc in range(8):
                            mm(pb[:], wv[:, kc, oc4 * 128:(oc4 + 1) * 128], hid(kc)[:, n * 512:(n + 1) * 512], kc == 0, kc == 7,
                               (wb, SB[kc // 2]), (pbB,))
                        stt(x_sb[:, oc, n * 512:(n + 1) * 512], pb[:], MOD(l, 5, oc), x_sb[:, oc, n * 512:(n + 1) * 512],
                            ALU.mult, ALU.add, (pbB, modB, xB[oc][n]), (xB[oc][n],))
                    if g == 3 and l + 1 < nl_run and stage is None:
                        norm_mod(D_GS1, lambda j, l1=l + 1: MOD(l1, 0, j), l + 1, tiles=(n,))
            if stage is not None and l + 1 < nl_run:
                norm_mod(D_GS1, lambda j, l1=l + 1: MOD(l1, 0, j), l + 1)

    for l in range(nl_run):
        try:
            layer_body(l)
        except _Stop:
            break

    sq = Sx[7][:].bitcast(BF16)
    for n in range(4):
        tsl = slice(n * 512, (n + 1) * 512)
        pb, pbB = bank()
        for j in range(8):
            act(sq[:, j * 512:(j + 1) * 512], x_sb[:, j, tsl], AF.Square, (xB[j][n],), (SB[7],))
            mm(pb[:], onesD, sq[:, j * 512:(j + 1) * 512], j == 0, j == 7, (SB[7], cstB), (pbB,))
        rs = Sx[6][:, 0:512]
        act(rs, pb[:], AF.Sqrt, (pbB,), (SB[6],), bias=1e-6)
        S.op("dve", lambda e, rs=rs: e.reciprocal(out=rs, in_=rs), (SB[6],), (SB[6],))
        for j in range(8):
            st_, stB_ = staging()
            stt(st_[:], x_sb[:, j, tsl], V("norm_f", j), rs, ALU.mult, ALU.mult, (xB[j][n], vecsB, SB[6]), (stB_,))
            S.dma("sp", out=yT_d[j * 128:(j + 1) * 128, tsl], in_=st_[:], reads=(stB_,))
    S.dma("sp", out=rgo_d, in_=rgst[:], reads=(rgstB,))
    allb = [b for row in xB for b in row] + hB + SB + wrB + stgB + [vecsB, cstB, rgwB, derB, modB, smallB, rgstB, denB] + mixB
    S.finalize(allb)
    return nc


def _pvec(a):
    a = np.asarray(a, np.float32)
    lead = a.shape[:-1]
    n = a.shape[-1] // 128
    a = a.reshape(lead + (n, 128))
    return np.moveaxis(a, -1, 0)


def _consts():
    c = np.zeros((128, NCONST), np.float32)
    c[:, C_ID:C_ID + 128] = np.eye(128)
    c[:, C_ONES:C_ONES + 128] = 1.0 / 1024.0
    for hh in range(2):
        c[hh * 64:(hh + 1) * 64, C_BLK + hh * 64:C_BLK + (hh + 1) * 64] = 1.0 / 64.0
    s = np.arange(64)[:, None]
    t = np.arange(64)[None, :]
    mf = (t >= s).astype(np.float32)
    mb = (t <= s).astype(np.float32)
    c[:, C_MF:C_MF + 64] = np.concatenate([mf, mf], 0)
    c[:, C_MB:C_MB + 64] = np.concatenate([mb, mb], 0)
    cm = np.ones(T, np.float32)
    cm[::64] = 0.0
    c[:, C_CM:C_CM + T] = cm[None, :]
    c[:, C_HF:C_HF + 32] = 1.0
    c[:, C_HB + 32:C_HB + 64] = 1.0
    return c


def _mask_tables(is_prompt):
    m = np.zeros((128, 3, 6, 256), np.float32)
    kp = np.arange(128)
    qq = np.arange(256)
    if is_prompt:
        for cls in range(3):
            m[:, cls, 2, :] = 1.0
            m[:, cls, 3, :] = 1.0
        return m.reshape(128, 3 * 1536)
    kr_l, kc = kp // 64, kp % 64
    qr_l, qc = qq // 64, qq % 64
    c0 = np.clip(qc - 8, 0, 48)
    colok = (kc[:, None] >= c0[None, :]) & (kc[:, None] < c0[None, :] + 16)
    for cls, qg in ((0, 0), (1, 3), (2, 7)):
        for ri in range(6):
            kb = 2 * qg + ri - 2
            if kb < 0 or kb > 15:
                continue
            kr = 2 * kb + kr_l
            qr = 4 * qg + qr_l
            st = np.clip(qr - 4, 0, 24)
            rowok = (kr[:, None] >= st[None, :]) & (kr[:, None] < st[None, :] + 8)
            m[:, cls, ri, :] = (rowok & colok).astype(np.float32)
    return m.reshape(128, 3 * 1536)


def _bias_tables(rpb):
    kp = np.arange(128)
    qq = np.arange(256)
    kr_l, kc = kp // 64, kp % 64
    qr_l, qc = qq // 64, qq % 64
    out = np.zeros((NL, 8, 128, 6, 256), np.float32)
    dx = kc[:, None] - qc[None, :]
    okx = np.abs(dx) <= 15
    dxi = np.clip(dx, -15, 15) + 15
    for ri in range(6):
        dy = (2 * (ri - 2) + kr_l)[:, None] - qr_l[None, :]
        oky = np.abs(dy) <= 7
        dyi = np.clip(dy, -7, 7) + 7
        g = rpb[:, :, dyi, dxi]
        out[:, :, :, ri, :] = g * (okx & oky)[None, None].astype(np.float32) if False else np.where((okx & oky)[None, None], g, 0.0)
    return out.reshape(NL, 8, 128, 1536)


_PROG = {}


def make_in_maps(x_prompt, x_sample, cache_k, cache_v, state_rglru, state_hgrn, c, c_ctx,
           w_mod, b_mod, norm1, norm2, w_in, rg_conv_w, rg_conv_b, rg_w_a, rg_b_a, rg_w_x, rg_b_x,
           rg_lambda, hg_lb, hg_norm, na_rpb, w_out, w1, w2, norm_f):
    f32 = lambda a: np.ascontiguousarray(np.asarray(a, np.float32))
    x_prompt, x_sample = f32(x_prompt), f32(x_sample)
    perm = _win_perm()
    w_in_p = f32(np.asarray(w_in)[:, :, perm])
    w_mod, w_out, w1, w2 = f32(w_mod), f32(w_out), f32(w1), f32(w2)
    consts = _consts()
    rgw = np.zeros((NL, 2, 2, 2, 128, 128), np.float32)
    for ax, wsrc in enumerate((np.asarray(rg_w_a), np.asarray(rg_w_x))):
        for cc in range(2):
            for hh in range(2):
                rgw[:, :, cc, ax, hh * 64:(hh + 1) * 64, hh * 64:(hh + 1) * 64] = wsrc[:, :, cc * 2 + hh]
    rgw = rgw.reshape(NL, 8, 128, 128)
    biasT_s = _bias_tables(np.asarray(na_rpb, np.float32))
    biasT_p = np.zeros_like(biasT_s)
    mask_s, mask_p = _mask_tables(False), _mask_tables(True)

    def vec_common():
        v = np.zeros((128, NVEC), np.float32)

        def put(name, arr):
            arr = np.asarray(arr, np.float32).reshape(128, -1)
            v[:, VOFF[name]:VOFF[name] + arr.shape[1]] = arr
        put("b_mod", _pvec(np.asarray(b_mod).reshape(NL, 48 * 128)).reshape(128, NL * 48))
        put("norm1", _pvec(norm1))
        put("norm2", _pvec(norm2))
        put("norm_f", _pvec(norm_f))
        cw = _pvec(np.asarray(rg_conv_w))
        put("conv_w", np.transpose(cw, (0, 1, 3, 2)))
        put("conv_b", _pvec(rg_conv_b))
        put("rg_b_a", _pvec(rg_b_a))
        put("rg_b_x", _pvec(rg_b_x))
        put("rg_lam", _pvec(rg_lambda))
        hl = _pvec(hg_lb)
        put("hg_lb", np.transpose(hl, (0, 2, 3, 1)))
        put("hg_norm", _pvec(hg_norm))
        return v, put

    in_maps = []
    for core in range(8):
        v, put = vec_common()
        if core < 4:
            xs = x_prompt[core * 8:(core + 1) * 8].reshape(T, 1024)
            put("cond", _pvec(c_ctx))
            v[:, VOFF["flags"]:VOFF["flags"] + 4] = np.array([1.0, 0.0, NEG, 0.0], np.float32)[None, :]
            hgs0 = np.zeros((NL, 4, 128, 64), np.float32)
            ctxk = np.zeros((NL, 512, 512), np.float32)
            ctxv = np.zeros((NL, 512, 512), np.float32)
            bT, mT = biasT_p, mask_p
        else:
            b = core - 4
            xs = x_sample[b]
            put("cond", _pvec(np.asarray(c)[b]))
            v[:, VOFF["flags"]:VOFF["flags"] + 4] = np.array([0.0, 1.0, 0.0, 0.0], np.float32)[None, :]
            put("rg_h0", _pvec(np.asarray(state_rglru)[b]))
            sh = np.asarray(state_hgrn, np.float32)[b]
            hgs0 = sh.reshape(NL, 2, 2, 2, 64, 64).reshape(NL, 4, 128, 64)
            ctxk = np.ascontiguousarray(np.transpose(np.asarray(cache_k, np.float32)[b], (0, 2, 3, 1)).reshape(NL, 512, 512))
            ctxv = np.asarray(cache_v, np.float32)[b].reshape(NL, 512, 512)
            bT, mT = biasT_s, mask_s
        in_maps.append({
            "xT": np.ascontiguousarray(xs.T), "vecs": v, "consts": consts, "rgw": rgw, "hgs0": f32(hgs0),
            "ctxk": f32(ctxk), "ctxv": f32(ctxv), "biasT": bT, "maskT": mT,
            "w_mod": w_mod, "w_in": w_in_p, "w_out": w_out, "w1": w1, "w2": w2,
        })
    return in_maps


def kernel(**inputs):
    in_maps = make_in_maps(**inputs)
    if "nc" not in _PROG:
        _PROG["nc"] = build_program()
    res = run_bass_kernel_spmd(_PROG["nc"], in_maps, core_ids=list(range(8)))
    return assemble(res.results)


def assemble(R):
    y_prompt = np.concatenate([R[i]["yT"].T.reshape(8, 256, 1024) for i in range(4)], 0)
    y_sample = np.stack([R[4 + i]["yT"].T for i in range(4)], 0)
    nk = np.concatenate([np.transpose(R[i]["kT"].reshape(NL, 8, 64, 8, 256), (3, 0, 4, 1, 2)) for i in range(4)], 0)
    nv = np.concatenate([np.transpose(R[i]["vo"].reshape(NL, 8, 256, 8, 64), (1, 0, 2, 3, 4)) for i in range(4)], 0)
    rgs = []
    hgs = []
    for i in range(4):
        r = R[i]["rgo"].reshape(128, NL, 2, 2, 8)
        rgs.append(np.transpose(r, (4, 1, 2, 3, 0)).reshape(8, NL, 2, 256))
        hg = R[i]["hgo"].reshape(NL, 2, 2, 2, 64, 8, 64)
        hg = np.transpose(hg, (5, 0, 1, 2, 3, 4, 6)).reshape(8, NL, 2, 4, 64, 64).copy()
        hg[:, :, 1] = hg[::-1, :, 1]
        hgs.append(hg)
    return (y_prompt.astype(np.float32), y_sample.astype(np.float32), np.ascontiguousarray(nk, np.float32),
            np.ascontiguousarray(nv, np.float32), np.concatenate(rgs, 0).astype(np.float32),
            np.concatenate(hgs, 0).astype(np.float32))
```

```python
import numpy as np
import concourse.bass as bass
import concourse.mybir as mybir
from concourse.bass_utils import run_bass_kernel_spmd

F32 = mybir.dt.float32
BF16 = mybir.dt.bfloat16
AF = mybir.ActivationFunctionType
ALU = mybir.AluOpType

NL = 4
T = 2048
NEG = -30000.0


class Buf:
    __slots__ = ("name", "w", "r", "excl")

    def __init__(self, name, excl=False):
        self.name = name
        self.w = None
        self.r = []
        self.excl = excl


class Op:
    __slots__ = ("eng", "fn", "deps", "kind", "sig", "val", "sem", "waits")


class Sched:
    ENG = ("pe", "act", "dve", "pool", "sp")

    def __init__(self, nc, n_dma_sems=24):
        self.nc = nc
        self.ops = []
        self.n_dma_sems = n_dma_sems

    def _add(self, eng, fn, reads, writes, kind):
        op = Op()
        op.eng, op.fn, op.kind = eng, fn, kind
        writes = list(writes) + [b for b in reads if b.excl]
        reads = [b for b in reads if not b.excl]
        deps = []
        for b in reads:
            if b.w is not None:
                deps.append(b.w)
        for b in writes:
            if b.w is not None:
                deps.append(b.w)
            deps.extend(b.r)
        op.deps = deps
        op.sig = False
        op.val = 0
        op.sem = None
        self.ops.append(op)
        for b in reads:
            b.r.append(op)
        for b in writes:
            b.w = op
            b.r = []
        return op

    def op(self, eng, fn, reads=(), writes=()):
        return self._add(eng, fn, reads, writes, "c")

    def dma(self, eng, out, in_, reads=(), writes=()):
        return self._add(eng, lambda e: e.dma_start(out=out, in_=in_), reads, writes, "d")

    def finalize(self, final_bufs):
        nc = self.nc
        self._add("sp", None, [], list(final_bufs), "c")
        needed = set()
        for op in self.ops:
            for d in op.deps:
                if d.kind == "c" and not (d.eng == op.eng and op.eng == "pe"):
                    needed.add(id(d))
        csem = {e: nc.alloc_semaphore("c_" + e) for e in self.ENG}
        dsem = [nc.alloc_semaphore("d_%d" % i) for i in range(self.n_dma_sems)]
        dval = [0] * self.n_dma_sems
        dnext = 0
        dnext_sw = 0
        cnt = {e: 0 for e in self.ENG}
        known = {e: {} for e in self.ENG}
        streams = {e: [] for e in self.ENG}
        for op in self.ops:
            w = {}
            kn = known[op.eng]
            for d in op.deps:
                if d.kind == "c":
                    if d.eng == op.eng and op.eng == "pe":
                        continue
                    key = ("c", d.eng)
                else:
                    key = ("d", d.sem)
                if kn.get(key, 0) >= d.val:
                    continue
                if w.get(key, 0) < d.val:
                    w[key] = d.val
            if op.kind == "d":
                half = self.n_dma_sems // 2
                if op.eng == "pool":
                    s = half + dnext_sw
                    dnext_sw = (dnext_sw + 1) % half
                else:
                    s = dnext
                    dnext = (dnext + 1) % half
                if dval[s] > 0 and kn.get(("d", s), 0) < dval[s]:
                    if w.get(("d", s), 0) < dval[s]:
                        w[("d", s)] = dval[s]
                dval[s] += 16
                op.sem = s
                op.val = dval[s]
            else:
                if id(op) in needed:
                    cnt[op.eng] += 1
                    op.sig = True
                op.val = cnt[op.eng]
            for k, v in w.items():
                kn[k] = v
            op.waits = w
            streams[op.eng].append(op)

        def replay(name, e):
            for op in streams[name]:
                for (kind, k), v in op.waits.items():
                    e.wait_ge(csem[k] if kind == "c" else dsem[k], v)
                if op.fn is None:
                    continue
                ins = op.fn(e)
                if op.kind == "d":
                    ins.then_inc(dsem[op.sem], 16)
                elif op.sig:
                    ins.then_inc(csem[name], 1)

        with nc.Block() as block:
            @block.tensor
            def _(e):
                replay("pe", e)

            @block.scalar
            def _(e):
                replay("act", e)

            @block.vector
            def _(e):
                replay("dve", e)

            @block.gpsimd
            def _(e):
                replay("pool", e)

            @block.sync
            def _(e):
                replay("sp", e)


def _vec_layout():
    off = {}
    n = 0
    for name, sz in (("b_mod", NL * 48), ("norm1", NL * 8), ("norm2", NL * 8), ("norm_f", 8),
                     ("conv_w", NL * 2 * 4), ("conv_b", NL * 2), ("rg_b_a", NL * 4), ("rg_b_x", NL * 4),
                     ("rg_lam", NL * 4), ("hg_lb", 16), ("hg_norm", NL * 2), ("rg_h0", NL * 4),
                     ("cond", 8), ("flags", 4)):
        off[name] = n
        n += sz
    return off, n


VOFF, NVEC = _vec_layout()
C_ID, C_ONES, C_BLK, C_MF, C_MB, C_CM, C_HF, C_HB, NCONST = 0, 128, 256, 384, 448, 512, 2560, 2624, 2688
PIECES = {"A": (0, 512), "B0": (512, 512), "B1": (1024, 512), "BG": (1536, 256),
          "CQ": (1792, 512), "CK": (2304, 512), "CV": (2816, 512)}


def _win_perm():
    xa, ga, bq, bff, bfb, bi, bg, cq, ck, cv = 0, 256, 512, 768, 1024, 1280, 1536, 1792, 2304, 2816
    cols = list(range(0, 512))
    for cc in range(2):
        for base in (bq, bff, bfb, bi):
            cols += list(range(base + cc * 128, base + cc * 128 + 128))
    cols += list(range(bg, bg + 256))
    cols += list(range(cq, cq + 512)) + list(range(ck, ck + 512)) + list(range(cv, cv + 512))
    return np.array(cols)


def _attn_pairs():
    pairs = []
    for qg in range(8):
        lst = []
        for rel in range(-2, 4):
            kb = 2 * qg + rel
            if 0 <= kb < 16:
                lst.append((kb, rel + 2))
        pairs.append(lst)
    return pairs


def _mask_class(qg):
    return 0 if qg == 0 else (2 if qg == 7 else 1)


def build_program(nl_run=NL, nlw=NL, stage=None):
    nc = bass.Bass("TRN2", target_bir_lowering=False)
    S = Sched(nc)

    def din(name, shape):
        return nc.dram_tensor(name, list(shape), F32, kind="ExternalInput").ap()

    def dout(name, shape):
        return nc.dram_tensor(name, list(shape), F32, kind="ExternalOutput").ap()

    xT_d = din("xT", [1024, T])
    vecs_d = din("vecs", [128, NVEC])
    consts_d = din("consts", [128, NCONST])
    rgw_d = din("rgw", [NL, 8, 128, 128])
    hgs0_d = din("hgs0", [NL, 4, 128, 64])
    ctxk_d = din("ctxk", [NL, 512, 512])
    ctxv_d = din("ctxv", [NL, 512, 512])
    biasT_d = din("biasT", [nlw, 8, 128, 1536])
    maskT_d = din("maskT", [128, 3 * 1536])
    wmod_d = din("w_mod", [nlw, 1024, 6144])
    win_d = din("w_in", [nlw, 1024, 3328])
    wout_d = din("w_out", [nlw, 1024, 1024])
    w1_d = din("w1", [nlw, 1024, 4096])
    w2_d = din("w2", [nlw, 4096, 1024])

    yT_d = dout("yT", [1024, T])
    kT_d = dout("kT", [NL, 512, T])
    vo_d = dout("vo", [NL, T, 512])
    rgo_d = dout("rgo", [128, NL * 4 * 8])
    hgo_d = dout("hgo", [NL, 4, 128, 8 * 64])
    mix_d = nc.dram_tensor("mix_scratch", [1024, T], BF16).ap()
    mixB = [Buf("mixd%d" % i) for i in range(16)]
    den_d = nc.dram_tensor("den_scratch", [8, T], F32).ap()
    denB = Buf("den")

    def sb(name, shape, dt=F32):
        return nc.alloc_sbuf_tensor("sb_" + name, list(shape), dt)

    x_sb = sb("x", [128, 8, T])
    xB = [[Buf("x%d_%d" % (j, n)) for n in range(4)] for j in range(8)]
    h_sb = sb("h", [128, 8, T], BF16)
    hB = [Buf("h%d" % n) for n in range(4)]
    SW = 2056
    Sx = [sb("S%d" % i, [128, SW]) for i in range(8)]
    SB = [Buf("S%d" % i) for i in range(8)]
    NSLOT = 3
    wr = [sb("wr%d" % i, [128, 4096], BF16) for i in range(NSLOT)]
    wrB = [Buf("wr%d" % i) for i in range(NSLOT)]
    stg = [sb("stg%d" % i, [128, 512]) for i in range(4)]
    stgB = [Buf("stg%d" % i) for i in range(4)]
    vecs = sb("vecs", [128, NVEC])
    vecsB = Buf("vecs")
    cst = sb("cst", [128, NCONST], BF16)
    cstB = Buf("cst")
    rgw_sb = sb("rgw", [128, 8, 128], BF16)
    rgwB = Buf("rgw")
    der = sb("der", [128, 512])
    derB = Buf("der")
    modv = sb("modv", [128, NL * 48])
    modB = Buf("modv")
    small = sb("small", [128, 256])
    smallB = Buf("small")
    rgst = sb("rgst", [128, NL * 4 * 8])
    rgstB = Buf("rgst")

    banks = [nc.alloc_psum_tensor("bank%d" % i, [128, 512], F32) for i in range(8)]
    bankB = [Buf("bank%d" % i, excl=True) for i in range(8)]
    bstate = {"i": 0, "stg": 0, "slot": 0, "ev": 0}

    def bank():
        i = bstate["i"]
        bstate["i"] = (i + 1) % 7
        return banks[i], bankB[i]

    def staging():
        i = bstate["stg"]
        bstate["stg"] = (i + 1) % 4
        return stg[i], stgB[i]

    def slot():
        i = bstate["slot"]
        bstate["slot"] = (i + 1) % NSLOT
        return wr[i], wrB[i]

    def evac_eng():
        bstate["ev"] ^= 1
        return "act" if bstate["ev"] else "dve"

    def mm(out, lhsT, rhs, start, stop, r, w, **kw):
        S.op("pe", lambda e: e.matmul(out, lhsT, rhs, start=start, stop=stop, **kw), r, w)

    def act(out, in_, func, r, w, bias=None, scale=None):
        kw = {}
        if bias is not None:
            kw["bias"] = bias
        if scale is not None:
            kw["scale"] = scale
        S.op("act", lambda e: e.activation(out=out, in_=in_, func=func, **kw), r, w)

    def tt(eng, out, a, b, op, r, w):
        S.op(eng, lambda e: e.tensor_tensor(out=out, in0=a, in1=b, op=op), r, w)

    def ts(eng, out, a, s1, s2, op0, op1, r, w):
        if op1 is None:
            S.op(eng, lambda e: e.tensor_scalar(out=out, in0=a, scalar1=s1, scalar2=None, op0=op0), r, w)
        else:
            S.op(eng, lambda e: e.tensor_scalar(out=out, in0=a, scalar1=s1, scalar2=s2, op0=op0, op1=op1), r, w)

    def stt(out, a, sc, b, op0, op1, r, w):
        S.op("dve", lambda e: e.scalar_tensor_tensor(out=out, in0=a, scalar=sc, in1=b, op0=op0, op1=op1), r, w)

    def cp(eng, out, in_, r, w):
        if eng == "act":
            S.op("act", lambda e: e.activation(out=out, in_=in_, func=AF.Copy), r, w)
        else:
            S.op(eng, lambda e: e.tensor_copy(out=out, in_=in_), r, w)

    def scan(out, d0, d1, init, r, w):
        S.op("dve", lambda e: e.tensor_tensor_scan(out=out, data0=d0, data1=d1, initial=init,
                                                   op0=ALU.mult, op1=ALU.add), r, w)

    def V(name, i=0):
        o = VOFF[name] + i
        return vecs[:, o:o + 1]

    def wload(src_ap, view_out):
        t, b = slot()
        S.dma("pool", out=view_out(t), in_=src_ap, reads=(), writes=(b,))
        return t, b

    S.dma("sp", out=vecs[:], in_=vecs_d, writes=(vecsB,))
    for c0 in range(0, NCONST, 1344):
        S.dma("pool", out=cst[:, c0:c0 + 1344], in_=consts_d[:, c0:c0 + 1344], writes=(cstB,))
    for j in range(8):
        for n in range(4):
            S.dma("sp", out=x_sb[:, j, n * 512:(n + 1) * 512], in_=xT_d[j * 128:(j + 1) * 128, n * 512:(n + 1) * 512],
                  writes=(xB[j][n],))
    ident = cst[:, C_ID:C_ID + 128]
    onesD = cst[:, C_ONES:C_ONES + 128]
    blk64 = cst[:, C_BLK:C_BLK + 128]
    maskf = cst[:, C_MF:C_MF + 64]
    maskb = cst[:, C_MB:C_MB + 64]
    cmask = cst[:, C_CM:C_CM + T]
    hmask = [cst[:, C_HF:C_HF + 64], cst[:, C_HB:C_HB + 64]]
    isP = V("flags", 0)
    notP = V("flags", 1)
    ctxneg = V("flags", 2)

    D_SCOND = 0
    D_C8N = 16
    D_C8N2 = 32
    D_HGL = 48
    D_OML = 64
    D_NW = 80
    D_GS1 = 112
    D_GS2 = 144
    D_TMP = 200
    scond = sb("scond", [128, 8], BF16)
    scondB = Buf("scond")
    act(scond[:], vecs[:, VOFF["cond"]:VOFF["cond"] + 8], AF.Silu, (vecsB,), (scondB,))
    lam = vecs[:, VOFF["rg_lam"]:VOFF["rg_lam"] + 16]
    act(der[:, D_TMP:D_TMP + 16], lam, AF.Exp, (vecsB,), (derB,), scale=-1.0)
    act(der[:, D_TMP + 16:D_TMP + 32], der[:, D_TMP:D_TMP + 16], AF.Ln, (derB,), (derB,), bias=1.0)
    ts("dve", der[:, D_C8N:D_C8N + 16], der[:, D_TMP + 16:D_TMP + 32], -8.0, None, ALU.mult, None, (derB,), (derB,))
    ts("dve", der[:, D_C8N2:D_C8N2 + 16], der[:, D_TMP + 16:D_TMP + 32], -16.0, None, ALU.mult, None, (derB,), (derB,))
    hraw = vecs[:, VOFF["hg_lb"]:VOFF["hg_lb"] + 16].rearrange("p (g l) -> p g l", l=4)
    S.op("dve", lambda e: e.tensor_reduce(out=der[:, D_TMP + 32:D_TMP + 36], in_=hraw, axis=mybir.AxisListType.X,
                                          op=ALU.max, negate=True), (vecsB,), (derB,))
    for g in range(4):
        act(der[:, D_TMP + 40 + g * 4:D_TMP + 44 + g * 4], vecs[:, VOFF["hg_lb"] + g * 4:VOFF["hg_lb"] + g * 4 + 4],
            AF.Exp, (vecsB, derB), (derB,), bias=der[:, D_TMP + 32 + g:D_TMP + 33 + g])
    ew = der[:, D_TMP + 40:D_TMP + 56].rearrange("p (g l) -> p g l", l=4)
    S.op("dve", lambda e: e.tensor_reduce(out=der[:, D_TMP + 56:D_TMP + 60], in_=ew, axis=mybir.AxisListType.X,
                                          op=ALU.add), (derB,), (derB,))
    S.op("dve", lambda e: e.reciprocal(out=der[:, D_TMP + 60:D_TMP + 64], in_=der[:, D_TMP + 56:D_TMP + 60]), (derB,), (derB,))
    for g in range(4):
        ts("dve", der[:, D_TMP + 40 + g * 4:D_TMP + 44 + g * 4], der[:, D_TMP + 40 + g * 4:D_TMP + 44 + g * 4],
           der[:, D_TMP + 60 + g:D_TMP + 61 + g], None, ALU.mult, None, (derB,), (derB,))
    hgl = der[:, D_HGL:D_HGL + 16].rearrange("p (g l) -> p g l", l=4)
    S.op("dve", lambda e: e.memset(der[:, D_HGL:D_HGL + 16], 0.0), (), (derB,))
    for l in range(1, 4):
        tt("dve", hgl[:, :, l:l + 1], hgl[:, :, l - 1:l], ew[:, :, l:l + 1], ALU.add, (derB,), (derB,))
    ts("dve", der[:, D_OML:D_OML + 16], der[:, D_HGL:D_HGL + 16], -1.0, 1.0, ALU.mult, ALU.add, (derB,), (derB,))
    ts("dve", der[:, D_NW:D_NW + 32], vecs[:, VOFF["conv_w"]:VOFF["conv_w"] + 32], isP, -1.0, ALU.mult, ALU.mult,
       (vecsB,), (derB,))

    mod_slots = {}

    def mod_load(l, g):
        wt, wb = wload(wmod_d[l, :, g * 512:(g + 1) * 512].rearrange("(kc p) n -> p kc n", p=128),
                       lambda t: t[:].rearrange("p (kc n) -> p kc n", kc=8))
        mod_slots[(l, g)] = (wt[:].rearrange("p (kc n) -> p kc n", kc=8), wb)

    def mod_mm(l, g):
        wv, wb = mod_slots.pop((l, g))
        pb, pbB = banks[7], bankB[7]
        for mc in range(4):
            gi = g * 4 + mc
            for kc in range(8):
                mm(pb[:, gi:gi + 1], wv[:, kc, mc * 128:(mc + 1) * 128], scond[:, kc:kc + 1],
                   kc == 0, kc == 7, (wb, scondB), (pbB,))

    def mod_finish(l):
        pb, pbB = banks[7], bankB[7]
        tt("dve", modv[:, l * 48:(l + 1) * 48], pb[:, 0:48], vecs[:, VOFF["b_mod"] + l * 48:VOFF["b_mod"] + (l + 1) * 48],
           ALU.add, (pbB, vecsB), (modB,))
        stt(der[:, D_GS1 + l * 8:D_GS1 + l * 8 + 8], modv[:, l * 48 + 8:l * 48 + 16], 1.0,
            vecs[:, VOFF["norm1"] + l * 8:VOFF["norm1"] + l * 8 + 8], ALU.add, ALU.mult, (modB, vecsB), (derB,))
        stt(der[:, D_GS2 + l * 8:D_GS2 + l * 8 + 8], modv[:, l * 48 + 32:l * 48 + 40], 1.0,
            vecs[:, VOFF["norm2"] + l * 8:VOFF["norm2"] + l * 8 + 8], ALU.add, ALU.mult, (modB, vecsB), (derB,))

    if nl_run > 0:
        mod_load(0, 0)
        mod_load(0, 1)
        for g in range(12):
            mod_mm(0, g)
            if g + 2 < 12:
                mod_load(0, g + 2)
        mod_finish(0)
    bstate["i"] = 0

    def MOD(l, m, j):
        o = l * 48 + m * 8 + j
        return modv[:, o:o + 1]

    S.op("pool", lambda e: e.memset(rgst[:], 0.0), (), (rgstB,))
    S.op("pool", lambda e: e.memset(Sx[0][:, 0:2], 0.0), (), (SB[0],))
    S.op("pool", lambda e: e.memset(Sx[0][:, 2050:SW], 0.0), (), (SB[0],))

    def norm_mod(gs_off, sh_of, l, tiles=(0, 1, 2, 3)):
        sq = Sx[7][:].bitcast(BF16)
        for n in tiles:
            tsl = slice(n * 512, (n + 1) * 512)
            pb, pbB = bank()
            for j in range(8):
                act(sq[:, j * 512:(j + 1) * 512] if False else sq[:, (j % 8) * 512:(j % 8) * 512 + 512],
                    x_sb[:, j, tsl], AF.Square, (xB[j][n],), (SB[7],))
                mm(pb[:], onesD, sq[:, (j % 8) * 512:(j % 8) * 512 + 512], j == 0, j == 7, (SB[7], cstB), (pbB,))
            rs = Sx[6][:, 0:512]
            act(rs, pb[:], AF.Sqrt, (pbB,), (SB[6],), bias=1e-6)
            S.op("dve", lambda e, rs=rs: e.reciprocal(out=rs, in_=rs), (SB[6],), (SB[6],))
            for j in range(8):
                tmp = Sx[6][:, 512 + (j % 2) * 512:1024 + (j % 2) * 512]
                stt(tmp, x_sb[:, j, tsl], der[:, gs_off + l * 8 + j:gs_off + l * 8 + j + 1], rs, ALU.mult, ALU.mult,
                    (xB[j][n], derB, SB[6]), (SB[6],))
                act(h_sb[:, j, tsl], tmp, AF.Identity, (SB[6], modB), (hB[n],), bias=sh_of(j))

    def proj_fm(wv, wb, c0, n):
        pb, pbB = bank()
        for kc in range(8):
            mm(pb[:], wv[:, kc, c0:c0 + 128], h_sb[:, kc, n * 512:(n + 1) * 512], kc == 0, kc == 7, (wb, hB[n]), (pbB,))
        return pb, pbB

    piece_cache = {}

    def prefetch(l, name):
        if (l, name) not in piece_cache:
            piece_cache[(l, name)] = load_piece_raw(l, name)

    def load_piece(l, name):
        if (l, name) in piece_cache:
            return piece_cache.pop((l, name))
        return load_piece_raw(l, name)

    def load_piece_raw(l, name):
        c0, ncol = PIECES[name]
        wt, wb = wload(win_d[l, :, c0:c0 + ncol].rearrange("(kc p) n -> p kc n", p=128),
                       lambda t: t[:, 0:8 * ncol].rearrange("p (kc n) -> p kc n", kc=8))
        return wt[:, 0:8 * ncol].rearrange("p (kc n) -> p kc n", kc=8), wb

    pairs = _attn_pairs()

    class _Stop(Exception):
        pass

    cur = {"l": 0}

    def chk(name):
        if stage == name or stage == "%d:%s" % (cur["l"], name):
            raise _Stop()

    def layer_body(l):
        cur["l"] = l
        for _once in (0,):
            if l == 0:
                norm_mod(D_GS1, lambda j: MOD(l, 0, j), l)
            S.dma("pool", out=rgw_sb[:], in_=rgw_d[l].rearrange("m p n -> p m n"), writes=(rgwB,))

            if stage == 'norm1':
                break
            wA, wAb = load_piece(l, "A")
            for cc in range(2):
                xa_pad = Sx[0]
                xc = Sx[1][:, 0:T]
                gga = Sx[7][:].bitcast(BF16)[:, 0:T]
                xcb = Sx[7][:].bitcast(BF16)[:, T:2 * T]
                S.op("pool", lambda e: e.memset(Sx[0][:, 0:2], 0.0), (), (SB[0],))
                S.op("pool", lambda e: e.memset(Sx[0][:, 2050:SW], 0.0), (), (SB[0],))
                for n in range(4):
                    pb, pbB = proj_fm(wA, wAb, cc * 128, n)
                    cp(evac_eng(), xa_pad[:, 2 + n * 512:2 + (n + 1) * 512], pb[:], (pbB,), (SB[0],))
                    pb, pbB = proj_fm(wA, wAb, 256 + cc * 128, n)
                    act(gga[:, n * 512:(n + 1) * 512], pb[:], AF.Gelu, (pbB,), (SB[7],))
                prefetch(l, "B0" if cc == 0 else "BG")
                cw = lambda k: vecs[:, VOFF["conv_w"] + (l * 2 + cc) * 4 + k:VOFF["conv_w"] + (l * 2 + cc) * 4 + k + 1]
                nw = lambda k: der[:, D_NW + (l * 2 + cc) * 4 + k:D_NW + (l * 2 + cc) * 4 + k + 1]
                cb = vecs[:, VOFF["conv_b"] + l * 2 + cc:VOFF["conv_b"] + l * 2 + cc + 1]
                ts("dve", xc, xa_pad[:, 0:T], cw(0), cb, ALU.mult, ALU.add, (SB[0], vecsB), (SB[1],))
                for k in range(1, 4):
                    stt(xc, xa_pad[:, k:k + T], cw(k), xc, ALU.mult, ALU.add, (SB[0], SB[1], vecsB), (SB[1],))
                Xv = lambda o: xa_pad[:, 258 + o:258 + o + 1792:256]
                Cv = lambda o: Sx[1][:, 256 + o:256 + o + 1792:256]
                for (co, xo, k) in ((0, -2, 0), (0, -1, 1), (1, -1, 0), (-1, 0, 3)):
                    stt(Cv(co), Xv(xo), nw(k), Cv(co), ALU.mult, ALU.add, (SB[0], SB[1], derB), (SB[1],))
                chk("A1")
                cp("act", xcb, xc, (SB[1],), (SB[7],))
                chk("A2")
                hdir = [Sx[5][:, 0:T], Sx[6][:, 0:T]]
                for dr in range(2):
                    idx = l * 4 + dr * 2 + cc
                    r_t, i_t, a_t = Sx[2][:, 0:T], Sx[3][:, 0:T], Sx[4][:, 0:T]
                    for n in range(4):
                        tsl = slice(n * 512, (n + 1) * 512)
                        pb, pbB = bank()
                        mm(pb[:], rgw_sb[:, (dr * 2 + cc) * 2 + 0, :], xcb[:, tsl], True, True, (rgwB, SB[7]), (pbB,))
                        act(r_t[:, tsl], pb[:], AF.Sigmoid, (pbB, vecsB), (SB[2],), bias=V("rg_b_a", idx))
                        pb, pbB = bank()
                        mm(pb[:], rgw_sb[:, (dr * 2 + cc) * 2 + 1, :], xcb[:, tsl], True, True, (rgwB, SB[7]), (pbB,))
                        act(i_t[:, tsl], pb[:], AF.Sigmoid, (pbB, vecsB), (SB[3],), bias=V("rg_b_x", idx))
                    act(a_t, r_t, AF.Exp, (SB[2], derB), (SB[4],), scale=der[:, D_C8N + idx:D_C8N + idx + 1])
                    act(r_t, r_t, AF.Exp, (SB[2], derB), (SB[2],), scale=der[:, D_C8N2 + idx:D_C8N2 + idx + 1])
                    act(r_t, r_t, AF.Sqrt, (SB[2],), (SB[2],), scale=-1.0, bias=1.0)
                    chk("A3")
                    tt("dve", i_t, i_t, r_t, ALU.mult, (SB[2], SB[3]), (SB[3],))
                    tt("dve", i_t, i_t, xc, ALU.mult, (SB[3], SB[1]), (SB[3],))
                    if dr == 0:
                        av = Sx[4][:, 256:256 + 1792:256]
                    else:
                        av = Sx[4][:, 255:255 + 1792:256]
                    ts("dve", av, av, notP, None, ALU.mult, None, (SB[4], vecsB), (SB[4],))
                    chk("A4")
                    h0 = V("rg_h0", idx)
                    hb_ = SB[5 + dr]
                    if dr == 0:
                        scan(hdir[0], a_t, i_t, h0, (SB[4], SB[3], vecsB), (hb_,))
                        cp("pool", rgst[:, idx * 8:idx * 8 + 8], Sx[5][:, 255:T:256], (hb_,), (rgstB,))
                    else:
                        scan(hdir[1][:, ::-1], a_t[:, ::-1], i_t[:, ::-1], h0, (SB[4], SB[3], vecsB), (hb_,))
                        cp("pool", rgst[:, idx * 8:idx * 8 + 8], Sx[6][:, 0:T:256], (hb_,), (rgstB,))
                chk("A5")
                tt("dve", hdir[0], hdir[0], hdir[1], ALU.add, (SB[5], SB[6]), (SB[5],))
                oa = Sx[2][:].bitcast(BF16)[:, 0:T]
                tt("dve", oa, hdir[0], gga, ALU.mult, (SB[5], SB[7]), (SB[2],))
                chk("A6")
                S.dma("sp", out=mix_d[cc * 128:(cc + 1) * 128, :], in_=oa, reads=(SB[2],), writes=(mixB[cc * 2], mixB[cc * 2 + 1]))
                chk("A7")

            if stage == 'A':
                break
            wGv, wGb = None, None
            for cc in range(2):
                wBv, wBb = load_piece(l, "B%d" % cc)
                q_t = Sx[0][:, 0:T]
                sg = [Sx[1][:, 0:T], Sx[2][:, 0:T]]
                o_acc = Sx[3][:, 0:T]
                S4b = Sx[4][:].bitcast(BF16)
                AT2 = S4b[:, 0:T].rearrange("p (h j t) -> p h j t", h=2, t=64)
                Qd = S4b[:, T:2 * T]
                DSf = Sx[5][:, 0:T]
                DS = DSf.rearrange("p (v n) -> p v n", n=32)
                S7b = Sx[7][:].bitcast(BF16)
                Vh = S7b[:, 0:T].rearrange("p (b c) -> p b c", c=128)
                Qi, Kt, Kd, KdT = [S7b[:, T + i * 512:T + (i + 1) * 512] for i in range(4)]
                Sbf = Sx[6][:].bitcast(BF16)[:, 0:2112].rearrange("p (m v) -> p m v", v=64)
                fq, lgq, cumq, eq = [Sx[6][:, i * 512:(i + 1) * 512] for i in range(4)]
                for n in range(4):
                    tsl = slice(n * 512, (n + 1) * 512)
                    pb, pbB = proj_fm(wBv, wBb, 0, n)
                    act(q_t[:, tsl], pb[:], AF.Silu, (pbB,), (SB[0],))
                    for dr in range(2):
                        pb, pbB = proj_fm(wBv, wBb, 128 + dr * 128, n)
                        act(sg[dr][:, tsl], pb[:], AF.Sigmoid, (pbB,), (SB[1 + dr],))
                for b4 in range(4):
                    pb, pbB = bank()
                    for bb in range(4):
                        blk = b4 * 4 + bb
                        for kc in range(8):
                            mm(pb[:, bb * 128:(bb + 1) * 128], h_sb[:, kc, blk * 128:(blk + 1) * 128], wBv[:, kc, 384:512],
                               kc == 0, kc == 7, (wBb, hB[blk // 4]), (pbB,))
                    cp(evac_eng(), Vh[:, b4 * 4:(b4 + 1) * 4, :], pb[:].rearrange("p (b c) -> p b c", c=128), (pbB,), (SB[7],))
                prefetch(l, "B1" if cc == 0 else "CQ")
                chk("B1")
                mid_t, tot_t, dec_t = small[:, 0:32], small[:, 32:64], small[:, 64:96]
                for dr in range(2):
                    gi = dr * 2 + cc
                    mk = maskf if dr == 0 else maskb
                    mid_i, tot_i = (31, 63) if dr == 0 else (32, 0)
                    sidx = (lambda n: n) if dr == 0 else (lambda n: 31 - n)
                    oml = der[:, D_OML + gi * 4 + l:D_OML + gi * 4 + l + 1]
                    hgl_ = der[:, D_HGL + gi * 4 + l:D_HGL + gi * 4 + l + 1]
                    for nq in range(4):
                        qs = slice(nq * 512, (nq + 1) * 512)
                        ts("dve", fq, sg[dr][:, qs], oml, hgl_, ALU.mult, ALU.add, (SB[1 + dr], derB), (SB[6],))
                        act(lgq, fq, AF.Ln, (SB[6],), (SB[6],))
                        act(fq, fq, AF.Identity, (SB[6],), (SB[6],), scale=-1.0, bias=1.0)
                        if dr == 0:
                            scan(cumq, cmask[:, 0:512], lgq, 0.0, (cstB, SB[6]), (SB[6],))
                        else:
                            scan(cumq[:, ::-1], cmask[:, 0:512], lgq[:, ::-1], 0.0, (cstB, SB[6]), (SB[6],))
                        c3 = cumq.rearrange("p (n c) -> p n c", c=64)
                        l3 = lgq.rearrange("p (n c) -> p n c", c=64)
                        if dr == 0:
                            ms = slice(nq * 8, nq * 8 + 8)
                            cp("dve", mid_t[:, ms], c3[:, :, mid_i], (SB[6],), (smallB,))
                            cp("dve", tot_t[:, ms], c3[:, :, tot_i], (SB[6],), (smallB,))
                            midv, totv = mid_t[:, ms], tot_t[:, ms]
                        else:
                            lo = 31 - (nq * 8 + 7)
                            cp("dve", mid_t[:, lo:lo + 8], c3[:, ::-1, mid_i], (SB[6],), (smallB,))
                            cp("dve", tot_t[:, lo:lo + 8], c3[:, ::-1, tot_i], (SB[6],), (smallB,))
                            midv, totv = mid_t[:, lo:lo + 8][:, ::-1], tot_t[:, lo:lo + 8][:, ::-1]
                        midb = midv.unsqueeze(2).to_broadcast([128, 8, 64])
                        totb = totv.unsqueeze(2).to_broadcast([128, 8, 64])
                        tt("dve", l3, c3, midb, ALU.subtract, (SB[6], smallB), (SB[6],))
                        act(eq, lgq, AF.Exp, (SB[6],), (SB[6],))
                        tt("dve", Qi, q_t[:, qs], eq, ALU.mult, (SB[0], SB[6]), (SB[7],))
                        act(eq, lgq, AF.Exp, (SB[6],), (SB[6],), scale=-1.0)
                        tt("dve", Kt, fq, eq, ALU.mult, (SB[6],), (SB[7],))
                        tt("dve", l3, c3, totb, ALU.subtract, (SB[6], smallB), (SB[6],))
                        act(eq, lgq, AF.Exp, (SB[6],), (SB[6],), scale=-1.0)
                        tt("dve", Kd, fq, eq, ALU.mult, (SB[6],), (SB[7],))
                        Kt0 = lgq.bitcast(BF16)[:, 0:512]
                        tt("dve", Kt0.rearrange("p (n c) -> p n c", c=64), Kt.rearrange("p (n c) -> p n c", c=64),
                           hmask[dr].unsqueeze(1).to_broadcast([128, 8, 64]), ALU.mult, (SB[7], cstB), (SB[6],))
                        act(eq, cumq, AF.Exp, (SB[6],), (SB[6],))
                        tt("dve", Qd[:, qs], q_t[:, qs], eq, ALU.mult, (SB[0], SB[6]), (SB[4],))
                        chk("B2")
                        pb, pbB = bank()
                        pbb = pb[:].bitcast(BF16)
                        for bb in range(4):
                            S.op("pe", lambda e, o=pbb[:, bb * 128:(bb + 1) * 128], i=Kd[:, bb * 128:(bb + 1) * 128]:
                                 e.transpose(o, i, ident), (SB[7], cstB), (pbB,))
                        cp(evac_eng(), KdT, pbb[:, 0:512], (pbB,), (SB[7],))
                        chk("B2b")
                        pbh = [bank(), bank()]
                        for c8 in range(8):
                            tp, j = (c8 % 2) * 64, c8 // 2
                            cs = slice(c8 * 64, (c8 + 1) * 64)
                            for hh in range(2):
                                ps_ = slice(hh * 64, (hh + 1) * 64)
                                pb, pbB = pbh[hh]
                                lo_k, hi_k = (Kt0, Kt) if dr == 0 else (Kt, Kt0)
                                mm(pb[tp:tp + 64, j * 64:j * 64 + 32], lo_k[ps_, cs], Qi[ps_, c8 * 64:c8 * 64 + 32], True, True,
                                   (SB[7], SB[6]), (pbB,))
                                mm(pb[tp:tp + 64, j * 64 + 32:j * 64 + 64], hi_k[ps_, cs], Qi[ps_, c8 * 64 + 32:c8 * 64 + 64], True, True,
                                   (SB[7], SB[6]), (pbB,))
                        for hh in range(2):
                            pb, pbB = pbh[hh]
                            tt("dve", AT2[:, hh, nq * 4:(nq + 1) * 4, :],
                               pb[:, 0:256].rearrange("p (j t) -> p j t", t=64),
                               mk.unsqueeze(1).to_broadcast([128, 4, 64]), ALU.mult, (pbB, cstB), (SB[4],))
                        chk("B2c")
                        pbp = [bank(), bank()]
                        for c8 in range(8):
                            par, blk = c8 % 2, c8 // 2
                            tp = par * 64
                            gblk = nq * 4 + blk
                            pb, pbB = pbp[par]
                            for hh in range(2):
                                mm(pb[hh * 64:(hh + 1) * 64, blk * 64:(blk + 1) * 64],
                                   KdT[tp:tp + 64, blk * 128 + hh * 64:blk * 128 + hh * 64 + 64],
                                   Vh[tp:tp + 64, gblk, hh * 64:(hh + 1) * 64], True, True, (SB[7],), (pbB,))
                        for par in range(2):
                            pb, pbB = pbp[par]
                            if dr == 0:
                                st0 = nq * 8 + par
                                dsv = DS[:, :, st0:st0 + 7:2]
                            else:
                                lo = 31 - nq * 8 - par - 6
                                dsv = DS[:, :, lo:lo + 7:2][:, :, ::-1]
                            cp(evac_eng(), dsv.rearrange("p v n -> p n v"), pb[:, 0:256].rearrange("p (n v) -> p n v", v=64),
                               (pbB,), (SB[5],))
                    chk("B4")
                    act(dec_t, tot_t, AF.Exp, (smallB,), (smallB,))
                    if dr == 0:
                        dv = small[:, 64 + 4:64 + 32:4]
                    else:
                        dv = small[:, 64 + 4:64 + 32:4]
                    ts("dve", dv, dv, notP, None, ALU.mult, None, (smallB, vecsB), (smallB,))
                    s0, s0B = staging()
                    S.dma("sp", out=s0[:, 0:64], in_=hgs0_d[l, gi], writes=(s0B,))
                    stt(DS[:, :, 0], s0[:, 0:64], small[:, 64:65], DS[:, :, 0], ALU.mult, ALU.add, (s0B, smallB, SB[5]), (SB[5],))
                    S.op("dve", lambda e: e.memset(small[:, 64:65], 0.0), (), (smallB,))
                    decfull = Sx[6][:, 0:T]
                    cp("dve", decfull.rearrange("p (v n) -> p v n", n=32), dec_t.unsqueeze(1).to_broadcast([128, 64, 32]),
                       (smallB,), (SB[6],))
                    scan(DSf, decfull, DSf, 0.0, (SB[6], SB[5]), (SB[5],))
                    cp("dve", Sbf[:, 0, :], s0[:, 0:64], (s0B,), (SB[6],))
                    cp("act", Sbf[:, 1:33, :], DS.rearrange("p v n -> p n v"), (SB[5],), (SB[6],))
                    ts("dve", Sbf[:, 4:32:4, :], Sbf[:, 4:32:4, :], notP, None, ALU.mult, None, (SB[6], vecsB), (SB[6],))
                    fin = Sx[6][:, 1100:1612]
                    cp("dve", fin.rearrange("p (j v) -> p j v", v=64), DS[:, :, 3:32:4].rearrange("p v j -> p j v"), (SB[5],), (SB[6],))
                    S.dma("sp", out=hgo_d[l, gi], in_=fin, reads=(SB[6],))
                    chk("B5")
                    for nq in range(4):
                        qs = slice(nq * 512, (nq + 1) * 512)
                        pI = [bank(), bank()]
                        pN = [bank(), bank()]
                        for c8 in range(8):
                            n_ = nq * 8 + c8
                            m_ = sidx(n_)
                            par, j = c8 % 2, n_ // 2
                            tp = par * 64
                            cs = slice(n_ * 64, (n_ + 1) * 64)
                            for hh in range(2):
                                ps_ = slice(hh * 64, (hh + 1) * 64)
                                mm(pI[par][0][ps_, c8 * 64:(c8 + 1) * 64], Vh[tp:tp + 64, j, hh * 64:(hh + 1) * 64],
                                   AT2[tp:tp + 64, hh, j, :], True, True, (SB[7], SB[4]), (pI[par][1],))
                                mm(pN[hh][0][ps_, c8 * 64:(c8 + 1) * 64], Sbf[ps_, m_, :], Qd[ps_, cs], True, True,
                                   (SB[6], SB[4]), (pN[hh][1],))
                        oq = o_acc[:, qs]
                        for hh in range(2):
                            ps_ = slice(hh * 64, (hh + 1) * 64)
                            if dr == 0:
                                cp("dve" if hh else "act", oq[ps_, :], pN[hh][0][ps_, :], (pN[hh][1],), (SB[3],))
                            else:
                                tt("dve", oq[ps_, :], oq[ps_, :], pN[hh][0][ps_, :], ALU.add, (SB[3], pN[hh][1]), (SB[3],))
                        for par in range(2):
                            ov = oq.rearrange("p (a b t) -> p a b t", b=2, t=64)[:, :, par, :]
                            iv = pI[par][0][:].rearrange("p (a b t) -> p a b t", b=2, t=64)[:, :, par, :]
                            tt("dve", ov, ov, iv, ALU.add, (SB[3], pI[par][1]), (SB[3],))
                chk("B6")
                if cc == 0:
                    wGv, wGb = load_piece(l, "BG")
                for n in range(4):
                    qs = slice(n * 512, (n + 1) * 512)
                    pbg, pbgB = proj_fm(wGv, wGb, cc * 128, n)
                    sbg = Sx[6][:, 0:512]
                    act(sbg, pbg[:], AF.Silu, (pbgB,), (SB[6],))
                    sqb = S7b[:, T:T + 512]
                    act(sqb, o_acc[:, qs], AF.Square, (SB[3],), (SB[7],))
                    pm, pmB = bank()
                    mm(pm[:], blk64, sqb, True, True, (cstB, SB[7]), (pmB,))
                    rs = Sx[6][:, 512:1024]
                    act(rs, pm[:], AF.Sqrt, (pmB,), (SB[6],), bias=1e-6)
                    S.op("dve", lambda e, rs=rs: e.reciprocal(out=rs, in_=rs), (SB[6],), (SB[6],))
                    t_ = Sx[6][:, 1024:1536]
                    stt(t_, o_acc[:, qs], V("hg_norm", l * 2 + cc), rs, ALU.mult, ALU.mult, (SB[3], SB[6], vecsB), (SB[6],))
                    ob = S7b[:, T + 512 + (n % 2) * 512:T + 1024 + (n % 2) * 512]
                    tt("dve", ob, t_, sbg, ALU.mult, (SB[6],), (SB[7],))
                    S.dma("sp", out=mix_d[256 + cc * 128:384 + cc * 128, qs], in_=ob, reads=(SB[7],),
                          writes=(mixB[4 + cc * 2], mixB[5 + cc * 2]))

            if stage == 'B':
                break
            def qv(c):
                return Sx[c // 2][:].bitcast(BF16)[:, (c % 2) * T:(c % 2 + 1) * T]

            def kv(c):
                return Sx[2 + c // 2][:].bitcast(BF16)[:, (c % 2) * T:(c % 2 + 1) * T]

            def vaug(blk):
                t_ = 4 + blk // 7
                o = (blk % 7) * 528
                return Sx[t_][:].bitcast(BF16)[:, o:o + 528].rearrange("p (h d) -> p h d", d=66), SB[t_]

            S6b = Sx[6][:].bitcast(BF16)
            S7b = Sx[7][:].bitcast(BF16)
            ctxV = S6b[:, 1056:1056 + 2112].rearrange("p (b h d) -> p b h d", b=4, d=66)
            ctxK = S7b[:, 0:T].rearrange("p (c k) -> p c k", k=512)
            for t_ in (4, 5, 6):
                S.op("dve", lambda e, t_=t_: e.memset(Sx[t_][:].bitcast(BF16), 1.0), (), (SB[t_],))
            S.dma("pool", out=ctxK, in_=ctxk_d[l].rearrange("(c p) k -> p c k", p=128), writes=(SB[7],))
            for b in range(4):
                S.dma("pool", out=ctxV[:, b, :, 0:64], in_=ctxv_d[l, b * 128:(b + 1) * 128, :].rearrange("p (h d) -> p h d", d=64),
                      writes=(SB[6],))
            chk("C0a")
            wQ, wQb = load_piece(l, "CQ")
            prefetch(l, "CK")
            for c in range(4):
                for n in range(4):
                    pb, pbB = proj_fm(wQ, wQb, c * 128, n)
                    cp(evac_eng(), qv(c)[:, n * 512:(n + 1) * 512], pb[:], (pbB,), (SB[c // 2],))
            chk("C0b")
            wK, wKb = load_piece(l, "CK")
            prefetch(l, "CV")
            for c in range(4):
                for n in range(4):
                    pb, pbB = proj_fm(wK, wKb, c * 128, n)
                    st_, stB_ = staging()
                    cp("act", st_[:], pb[:], (pbB,), (stB_,))
                    cp("dve", kv(c)[:, n * 512:(n + 1) * 512], pb[:], (pbB,), (SB[2 + c // 2],))
                    S.dma("sp", out=kT_d[l, c * 128:(c + 1) * 128, n * 512:(n + 1) * 512], in_=st_[:], reads=(stB_,))
            chk("C0c")
            wV, wVb = load_piece(l, "CV")
            for blk in range(16):
                pb, pbB = bank()
                for kc in range(8):
                    mm(pb[:], h_sb[:, kc, blk * 128:(blk + 1) * 128], wV[:, kc, 0:512], kc == 0, kc == 7, (wVb, hB[blk // 4]), (pbB,))
                st_, stB_ = staging()
                cp("act", st_[:], pb[:], (pbB,), (stB_,))
                va, vaB = vaug(blk)
                cp("dve", va[:, :, 0:64], pb[:].rearrange("p (h d) -> p h d", d=64), (pbB,), (vaB,))
                S.dma("sp", out=vo_d[l, blk * 128:(blk + 1) * 128, :], in_=st_[:], reads=(stB_,))
            chk("C1")
            amB, ebB, bsB = Buf("amask"), [Buf("ebm0"), Buf("ebm1")], Buf("bstg")
            ptB = [Buf("pt%d" % i) for i in range(8)]
            S.op("pool", lambda e: e.memset(small[:, 200:201], 0.0), (), tuple(hB) + (amB, bsB, smallB) + tuple(ebB) + tuple(ptB))
            amask = h_sb[:, 0:3, :].rearrange("p a b -> p (a b)")[:, 0:4608]
            ebm = [h_sb[:, 3, 0:1536], h_sb[:, 4, 0:1536]]
            bstg = h_sb[:, 5:7, :].rearrange("p a b -> p (a b)").bitcast(F32)[:, 0:1536]
            PT = [h_sb[:, 7, i * 256:(i + 1) * 256] for i in range(8)]
            for c0 in range(0, 4608, 1536):
                S.dma("pool", out=amask[:, c0:c0 + 1536], in_=maskT_d[:, c0:c0 + 1536], writes=(amB,))
            modq = {"mm": 0, "ld": 0, "on": (l + 1 < nl_run)}

            def mod_step(final=False):
                if not modq["on"]:
                    return
                n_do = 12 if final else 2
                for _ in range(n_do):
                    if modq["mm"] < modq["ld"]:
                        mod_mm(l + 1, modq["mm"])
                        modq["mm"] += 1
                    if modq["ld"] < 12 and modq["ld"] - modq["mm"] < 2:
                        mod_load(l + 1, modq["ld"])
                        modq["ld"] += 1
                    if modq["mm"] >= 12:
                        break
                if final:
                    while modq["mm"] < 12:
                        if modq["ld"] <= modq["mm"]:
                            mod_load(l + 1, modq["ld"])
                            modq["ld"] += 1
                        mod_mm(l + 1, modq["mm"])
                        modq["mm"] += 1
                    mod_finish(l + 1)

            LA = 4
            work = []
            for hd in range(8):
                for qg in range(8):
                    items = [("l", kb, ri) for (kb, ri) in pairs[qg]] + [("c", cb, 0) for cb in range(4)]
                    for ii, (kind, kb, ri) in enumerate(items):
                        work.append((hd, qg, kind, kb, ri, ii == 0, ii == len(items) - 1))
            st8 = {"ebi": 0, "cls": -1, "hd": -1, "obank": 0}
            stage1 = []

            def emit_front(w, idx):
                hd, qg, kind, kb, ri, first, last = w
                c, hh = hd // 2, hd % 2
                ps_ = slice(hh * 64, (hh + 1) * 64)
                qcols = slice(qg * 256, (qg + 1) * 256)
                if hd != st8["hd"]:
                    mod_step()
                    S.dma("sp", out=bstg, in_=biasT_d[l, hd], writes=(bsB,))
                    act(bstg, bstg, AF.Exp, (bsB,), (bsB,))
                    st8["hd"] = hd
                    st8["cls"] = -1
                cls = _mask_class(qg)
                if cls != st8["cls"]:
                    st8["ebi"] ^= 1
                    tt("dve", ebm[st8["ebi"]], bstg, amask[:, cls * 1536:(cls + 1) * 1536], ALU.mult, (bsB, amB), (ebB[st8["ebi"]],))
                    st8["cls"] = cls
                if first:
                    st8["obank"] = (st8["obank"] + 1) % 3
                ob_, obB = banks[st8["obank"]], bankB[st8["obank"]]
                sp_, spB = banks[3 + idx % 4], bankB[3 + idx % 4]
                pt, ptb = PT[idx % 8], ptB[idx % 8]
                if kind == "l":
                    mm(sp_[:, 0:256], kv(c)[ps_, kb * 128:(kb + 1) * 128], qv(c)[ps_, qcols], True, True,
                       (SB[2 + c // 2], SB[c // 2]), (spB,))
                    act(pt, sp_[:, 0:256], AF.Exp, (spB,), (ptb,), scale=0.125)
                    e_ = st8["ebi"]
                    tt("dve", pt, pt, ebm[e_][:, ri * 256:(ri + 1) * 256], ALU.mult, (ptb, ebB[e_]), (ptb,))
                    va, vaB = vaug(kb)
                    lhs = va[:, hd, 0:65]
                else:
                    mm(sp_[:, 0:256], ctxK[ps_, c, kb * 128:(kb + 1) * 128], qv(c)[ps_, qcols], True, True,
                       (SB[7], SB[c // 2]), (spB,))
                    act(pt, sp_[:, 0:256], AF.Exp, (spB, vecsB), (ptb,), scale=0.125, bias=ctxneg)
                    vaB = SB[6]
                    lhs = ctxV[:, kb, hd, 0:65]
                stage1.append((pt, ptb, lhs, vaB, ob_, obB))

            fin_cnt = [0]

            def emit_back(w, idx):
                hd, qg, kind, kb, ri, first, last = w
                pt, ptb, lhs, vaB, ob_, obB = stage1[idx]
                qcols = slice(qg * 256, (qg + 1) * 256)
                mm(ob_[0:65, 0:256], lhs, pt, first, last, (vaB, ptb), (obB,))
                if last:
                    st_, stB_ = staging()
                    fi = fin_cnt[0] % 2
                    fin_cnt[0] += 1
                    oo = S7b[:, T + fi * 256:T + (fi + 1) * 256]
                    cp("dve", st_[64:65, 0:256], ob_[64:65, 0:256], (obB,), (stB_,))
                    cp("act", oo[0:64, :], ob_[0:64, 0:256], (obB,), (SB[7],))
                    S.dma("sp", out=den_d[hd:hd + 1, qcols], in_=st_[64:65, 0:256], reads=(stB_,), writes=(denB,))
                    S.dma("sp", out=mix_d[512 + hd * 64:576 + hd * 64, qcols], in_=oo[0:64, :], reads=(SB[7],), writes=(mixB[8 + hd],))

            G = 4
            nw = len(work)
            for base in range(0, nw + G, G):
                for idx in range(base, min(base + G, nw)):
                    emit_front(work[idx], idx)
                for idx in range(max(base - G, 0), min(base, nw)):
                    emit_back(work[idx], idx)
            mod_step(final=True)
            S.op("pool", lambda e: e.memset(small[:, 200:201], 0.0), (), (amB, bsB, smallB) + tuple(ebB) + tuple(ptB) + tuple(hB))

            if stage == 'C':
                break
            wO = []
            for o in range(2):
                wt, wb = wload(wout_d[l, :, o * 512:(o + 1) * 512].rearrange("(kc p) n -> p kc n", p=128),
                               lambda t: t[:].rearrange("p (kc n) -> p kc n", kc=8))
                wO.append((wt[:].rearrange("p (kc n) -> p kc n", kc=8), wb))
            def wout_prep(n):
                mt = Sx[n % 3][:].bitcast(BF16)[:, 0:4096].rearrange("p (kc t) -> p kc t", kc=8)
                S.dma("sp", out=mt, in_=mix_d[:, n * 512:(n + 1) * 512].rearrange("(kc p) t -> p kc t", p=128),
                      reads=tuple(mixB), writes=(SB[n % 3],))
                dn = Sx[3 + n % 3][:, 0:T].rearrange("p (c t) -> p c t", c=4)
                for hd in range(8):
                    S.dma("sp", out=dn[(hd % 2) * 64:(hd % 2 + 1) * 64, hd // 2, :],
                          in_=den_d[hd:hd + 1, n * 512:(n + 1) * 512].to_broadcast([64, 512]), reads=(denB,), writes=(SB[3 + n % 3],))
                S.op("dve", lambda e, dn=dn: e.reciprocal(out=dn, in_=dn), (SB[3 + n % 3],), (SB[3 + n % 3],))
                tt("dve", mt[:, 4:8, :], mt[:, 4:8, :], dn, ALU.mult, (SB[n % 3], SB[3 + n % 3]), (SB[n % 3],))
                return mt

            mts = {0: wout_prep(0), 1: wout_prep(1)}
            for n in range(4):
                if n + 2 < 4:
                    mts[n + 2] = wout_prep(n + 2)
                mt = mts[n]
                for oc in range(8):
                    wv, wb = wO[oc // 4]
                    pb, pbB = bank()
                    for kc in range(8):
                        mm(pb[:], wv[:, kc, (oc % 4) * 128:(oc % 4 + 1) * 128], mt[:, kc, :], kc == 0, kc == 7, (wb, SB[n % 3]), (pbB,))
                    stt(x_sb[:, oc, n * 512:(n + 1) * 512], pb[:], MOD(l, 2, oc), x_sb[:, oc, n * 512:(n + 1) * 512], ALU.mult, ALU.add,
                        (pbB, modB, xB[oc][n]), (xB[oc][n],))
                if stage != "wout":
                    norm_mod(D_GS2, lambda j: MOD(l, 3, j), l, tiles=(n,))

            if stage == 'wout':
                break
            def hid(c):
                return Sx[c // 2][:].bitcast(BF16)[:, (c % 2) * T:(c % 2 + 1) * T]

            for g in range(4):
                for half in range(2):
                    c0 = g * 1024 + half * 512
                    wt, wb = wload(w1_d[l, :, c0:c0 + 512].rearrange("(kc p) n -> p kc n", p=128),
                                   lambda t: t[:].rearrange("p (kc n) -> p kc n", kc=8))
                    wv = wt[:].rearrange("p (kc n) -> p kc n", kc=8)
                    for mc in range(4):
                        c = half * 4 + mc
                        for n in range(4):
                            pb, pbB = proj_fm(wv, wb, mc * 128, n)
                            st_, stB_ = staging()
                            act(st_[:], pb[:], AF.Relu, (pbB,), (stB_,))
                            act(hid(c)[:, n * 512:(n + 1) * 512], st_[:], AF.Square, (stB_,), (SB[c // 2],))
                w2v = []
                for half in range(2):
                    wt, wb = wload(w2_d[l, g * 1024:(g + 1) * 1024, half * 512:(half + 1) * 512].rearrange("(kc p) n -> p kc n", p=128),
                                   lambda t: t[:].rearrange("p (kc n) -> p kc n", kc=8))
                    w2v.append((wt[:].rearrange("p (kc n) -> p kc n", kc=8), wb))
                for n in range(4):
                    for oc in range(8):
                        wv, wb = w2v[oc // 4]
                        oc4 = oc % 4
                        pb, pbB = bank()
                        for kc in range(8):
                            mm(pb[:], wv[:, kc, oc4 * 128:(oc4 + 1) * 128], hid(kc)[:, n * 512:(n + 1) * 512], kc == 0, kc == 7,
                               (wb, SB[kc // 2]), (pbB,))
                        stt(x_sb[:, oc, n * 512:(n + 1) * 512], pb[:], MOD(l, 5, oc), x_sb[:, oc, n * 512:(n + 1) * 512],
                            ALU.mult, ALU.add, (pbB, modB, xB[oc][n]), (xB[oc][n],))
                    if g == 3 and l + 1 < nl_run and stage is None:
                        norm_mod(D_GS1, lambda j, l1=l + 1: MOD(l1, 0, j), l + 1, tiles=(n,))
            if stage is not None and l + 1 < nl_run:
                norm_mod(D_GS1, lambda j, l1=l + 1: MOD(l1, 0, j), l + 1)

    for l in range(nl_run):
        try:
            layer_body(l)
        except _Stop:
            break

    sq = Sx[7][:].bitcast(BF16)
    for n in range(4):
        tsl = slice(n * 512, (n + 1) * 512)
        pb, pbB = bank()
        for j in range(8):
            act(sq[:, j * 512:(j + 1) * 512], x_sb[:, j, tsl], AF.Square, (xB[j][n],), (SB[7],))
            mm(pb[:], onesD, sq[:, j * 512:(j + 1) * 512], j == 0, j == 7, (SB[7], cstB), (pbB,))
        rs = Sx[6][:, 0:512]
        act(rs, pb[:], AF.Sqrt, (pbB,), (SB[6],), bias=1e-6)
        S.op("dve", lambda e, rs=rs: e.reciprocal(out=rs, in_=rs), (SB[6],), (SB[6],))
        for j in range(8):
            st_, stB_ = staging()
            stt(st_[:], x_sb[:, j, tsl], V("norm_f", j), rs, ALU.mult, ALU.mult, (xB[j][n], vecsB, SB[6]), (stB_,))
            S.dma("sp", out=yT_d[j * 128:(j + 1) * 128, tsl], in_=st_[:], reads=(stB_,))
    S.dma("sp", out=rgo_d, in_=rgst[:], reads=(rgstB,))
    allb = [b for row in xB for b in row] + hB + SB + wrB + stgB + [vecsB, cstB, rgwB, derB, modB, smallB, rgstB, denB] + mixB
    S.finalize(allb)
    return nc


def _pvec(a):
    a = np.asarray(a, np.float32)
    lead = a.shape[:-1]
    n = a.shape[-1] // 128
    a = a.reshape(lead + (n, 128))
    return np.moveaxis(a, -1, 0)


def _consts():
    c = np.zeros((128, NCONST), np.float32)
    c[:, C_ID:C_ID + 128] = np.eye(128)
    c[:, C_ONES:C_ONES + 128] = 1.0 / 1024.0
    for hh in range(2):
        c[hh * 64:(hh + 1) * 64, C_BLK + hh * 64:C_BLK + (hh + 1) * 64] = 1.0 / 64.0
    s = np.arange(64)[:, None]
    t = np.arange(64)[None, :]
    mf = (t >= s).astype(np.float32)
    mb = (t <= s).astype(np.float32)
    c[:, C_MF:C_MF + 64] = np.concatenate([mf, mf], 0)
    c[:, C_MB:C_MB + 64] = np.concatenate([mb, mb], 0)
    cm = np.ones(T, np.float32)
    cm[::64] = 0.0
    c[:, C_CM:C_CM + T] = cm[None, :]
    c[:, C_HF:C_HF + 32] = 1.0
    c[:, C_HB + 32:C_HB + 64] = 1.0
    return c


def _mask_tables(is_prompt):
    m = np.zeros((128, 3, 6, 256), np.float32)
    kp = np.arange(128)
    qq = np.arange(256)
    if is_prompt:
        for cls in range(3):
            m[:, cls, 2, :] = 1.0
            m[:, cls, 3, :] = 1.0
        return m.reshape(128, 3 * 1536)
    kr_l, kc = kp // 64, kp % 64
    qr_l, qc = qq // 64, qq % 64
    c0 = np.clip(qc - 8, 0, 48)
    colok = (kc[:, None] >= c0[None, :]) & (kc[:, None] < c0[None, :] + 16)
    for cls, qg in ((0, 0), (1, 3), (2, 7)):
        for ri in range(6):
            kb = 2 * qg + ri - 2
            if kb < 0 or kb > 15:
                continue
            kr = 2 * kb + kr_l
            qr = 4 * qg + qr_l
            st = np.clip(qr - 4, 0, 24)
            rowok = (kr[:, None] >= st[None, :]) & (kr[:, None] < st[None, :] + 8)
            m[:, cls, ri, :] = (rowok & colok).astype(np.float32)
    return m.reshape(128, 3 * 1536)


def _bias_tables(rpb):
    kp = np.arange(128)
    qq = np.arange(256)
    kr_l, kc = kp // 64, kp % 64
    qr_l, qc = qq // 64, qq % 64
    out = np.zeros((NL, 8, 128, 6, 256), np.float32)
    dx = kc[:, None] - qc[None, :]
    okx = np.abs(dx) <= 15
    dxi = np.clip(dx, -15, 15) + 15
    for ri in range(6):
        dy = (2 * (ri - 2) + kr_l)[:, None] - qr_l[None, :]
        oky = np.abs(dy) <= 7
        dyi = np.clip(dy, -7, 7) + 7
        g = rpb[:, :, dyi, dxi]
        out[:, :, :, ri, :] = g * (okx & oky)[None, None].astype(np.float32) if False else np.where((okx & oky)[None, None], g, 0.0)
    return out.reshape(NL, 8, 128, 1536)


_PROG = {}


def make_in_maps(x_prompt, x_sample, cache_k, cache_v, state_rglru, state_hgrn, c, c_ctx,
           w_mod, b_mod, norm1, norm2, w_in, rg_conv_w, rg_conv_b, rg_w_a, rg_b_a, rg_w_x, rg_b_x,
           rg_lambda, hg_lb, hg_norm, na_rpb, w_out, w1, w2, norm_f):
    f32 = lambda a: np.ascontiguousarray(np.asarray(a, np.float32))
    x_prompt, x_sample = f32(x_prompt), f32(x_sample)
    perm = _win_perm()
    w_in_p = f32(np.asarray(w_in)[:, :, perm])
    w_mod, w_out, w1, w2 = f32(w_mod), f32(w_out), f32(w1), f32(w2)
    consts = _consts()
    rgw = np.zeros((NL, 2, 2, 2, 128, 128), np.float32)
    for ax, wsrc in enumerate((np.asarray(rg_w_a), np.asarray(rg_w_x))):
        for cc in range(2):
            for hh in range(2):
                rgw[:, :, cc, ax, hh * 64:(hh + 1) * 64, hh * 64:(hh + 1) * 64] = wsrc[:, :, cc * 2 + hh]
    rgw = rgw.reshape(NL, 8, 128, 128)
    biasT_s = _bias_tables(np.asarray(na_rpb, np.float32))
    biasT_p = np.zeros_like(biasT_s)
    mask_s, mask_p = _mask_tables(False), _mask_tables(True)

    def vec_common():
        v = np.zeros((128, NVEC), np.float32)

        def put(name, arr):
            arr = np.asarray(arr, np.float32).reshape(128, -1)
            v[:, VOFF[name]:VOFF[name] + arr.shape[1]] = arr
        put("b_mod", _pvec(np.asarray(b_mod).reshape(NL, 48 * 128)).reshape(128, NL * 48))
        put("norm1", _pvec(norm1))
        put("norm2", _pvec(norm2))
        put("norm_f", _pvec(norm_f))
        cw = _pvec(np.asarray(rg_conv_w))
        put("conv_w", np.transpose(cw, (0, 1, 3, 2)))
        put("conv_b", _pvec(rg_conv_b))
        put("rg_b_a", _pvec(rg_b_a))
        put("rg_b_x", _pvec(rg_b_x))
        put("rg_lam", _pvec(rg_lambda))
        hl = _pvec(hg_lb)
        put("hg_lb", np.transpose(hl, (0, 2, 3, 1)))
        put("hg_norm", _pvec(hg_norm))
        return v, put

    in_maps = []
    for core in range(8):
        v, put = vec_common()
        if core < 4:
            xs = x_prompt[core * 8:(core + 1) * 8].reshape(T, 1024)
            put("cond", _pvec(c_ctx))
            v[:, VOFF["flags"]:VOFF["flags"] + 4] = np.array([1.0, 0.0, NEG, 0.0], np.float32)[None, :]
            hgs0 = np.zeros((NL, 4, 128, 64), np.float32)
            ctxk = np.zeros((NL, 512, 512), np.float32)
            ctxv = np.zeros((NL, 512, 512), np.float32)
            bT, mT = biasT_p, mask_p
        else:
            b = core - 4
            xs = x_sample[b]
            put("cond", _pvec(np.asarray(c)[b]))
            v[:, VOFF["flags"]:VOFF["flags"] + 4] = np.array([0.0, 1.0, 0.0, 0.0], np.float32)[None, :]
            put("rg_h0", _pvec(np.asarray(state_rglru)[b]))
            sh = np.asarray(state_hgrn, np.float32)[b]
            hgs0 = sh.reshape(NL, 2, 2, 2, 64, 64).reshape(NL, 4, 128, 64)
            ctxk = np.ascontiguousarray(np.transpose(np.asarray(cache_k, np.float32)[b], (0, 2, 3, 1)).reshape(NL, 512, 512))
            ctxv = np.asarray(cache_v, np.float32)[b].reshape(NL, 512, 512)
            bT, mT = biasT_s, mask_s
        in_maps.append({
            "xT": np.ascontiguousarray(xs.T), "vecs": v, "consts": consts, "rgw": rgw, "hgs0": f32(hgs0),
            "ctxk": f32(ctxk), "ctxv": f32(ctxv), "biasT": bT, "maskT": mT,
            "w_mod": w_mod, "w_in": w_in_p, "w_out": w_out, "w1": w1, "w2": w2,
        })
    return in_maps


def kernel(**inputs):
    in_maps = make_in_maps(**inputs)
    if "nc" not in _PROG:
        _PROG["nc"] = build_program()
    res = run_bass_kernel_spmd(_PROG["nc"], in_maps, core_ids=list(range(8)))
    return assemble(res.results)


def assemble(R):
    y_prompt = np.concatenate([R[i]["yT"].T.reshape(8, 256, 1024) for i in range(4)], 0)
    y_sample = np.stack([R[4 + i]["yT"].T for i in range(4)], 0)
    nk = np.concatenate([np.transpose(R[i]["kT"].reshape(NL, 8, 64, 8, 256), (3, 0, 4, 1, 2)) for i in range(4)], 0)
    nv = np.concatenate([np.transpose(R[i]["vo"].reshape(NL, 8, 256, 8, 64), (1, 0, 2, 3, 4)) for i in range(4)], 0)
    rgs = []
    hgs = []
    for i in range(4):
        r = R[i]["rgo"].reshape(128, NL, 2, 2, 8)
        rgs.append(np.transpose(r, (4, 1, 2, 3, 0)).reshape(8, NL, 2, 256))
        hg = R[i]["hgo"].reshape(NL, 2, 2, 2, 64, 8, 64)
        hg = np.transpose(hg, (5, 0, 1, 2, 3, 4, 6)).reshape(8, NL, 2, 4, 64, 64).copy()
        hg[:, :, 1] = hg[::-1, :, 1]
        hgs.append(hg)
    return (y_prompt.astype(np.float32), y_sample.astype(np.float32), np.ascontiguousarray(nk, np.float32),
            np.ascontiguousarray(nv, np.float32), np.concatenate(rgs, 0).astype(np.float32),
            np.concatenate(hgs, 0).astype(np.float32))
```
